# Optimizing a Trainium2 kernel written in Bass

```python
import math
import jax, jax.numpy as jnp
from jax import lax
import numpy as np

D_MODEL = 1024
BATCH = 4
SEQ = 8192
DEPTH = 2

CHUNK = 64
Q_BLOCK = 128
N_MIXERS = 2
N_ATTN_LAYERS = (DEPTH + 1) // 2
N_RET_LAYERS = DEPTH // 2

DA_HEADS = 8
DA_HEAD_DIM = D_MODEL // (2 * DA_HEADS)
DA_V_DIM = 2 * DA_HEAD_DIM
DA_QK_WIDTH = 2 * DA_HEADS * DA_HEAD_DIM
DA_V_WIDTH = DA_HEADS * DA_V_DIM
DA_IN_WIDTH = 2 * DA_QK_WIDTH + 2 * DA_V_WIDTH

RET_HEADS = 4
RET_QK_DIM = D_MODEL // RET_HEADS
RET_V_DIM = 2 * RET_QK_DIM
RET_QK_WIDTH = RET_HEADS * RET_QK_DIM
RET_V_WIDTH = RET_HEADS * RET_V_DIM
RET_IN_WIDTH = 2 * RET_QK_WIDTH + 2 * RET_V_WIDTH

NORM_EPS = 1e-6
HEAD_NORM_EPS = 1e-5

kernel_name = "hybrid_diffattn_retention_streaming_block"


def rms_norm(x, gain, eps=NORM_EPS):
    xf = x.astype(jnp.float32)
    y = xf * lax.rsqrt(jnp.mean(xf * xf, axis=-1, keepdims=True) + eps)
    return (y * gain.astype(jnp.float32)).astype(x.dtype)


def head_rms_norm(x, eps=HEAD_NORM_EPS):
    xf = x.astype(jnp.float32)
    return xf * lax.rsqrt(jnp.mean(xf * xf, axis=-1, keepdims=True) + eps)


def alibi_slopes(n_heads):
    return jnp.asarray([2.0 ** (-8.0 * (h + 1) / n_heads) for h in range(n_heads)], dtype=jnp.float32)


def retention_log_gammas(n_heads):
    gammas = 1.0 - 2.0 ** (-5.0 - jnp.arange(n_heads, dtype=jnp.float32))
    return jnp.log(gammas)


def diff_attention(h, w_in, w_out, lq1, lk1, lq2, lk2, subln_gain, lam_init):
    B, S, _ = h.shape
    f32 = jnp.float32
    proj = jnp.einsum("bsd,de->bse", h, w_in)
    q, k, v, g = jnp.split(proj, [DA_QK_WIDTH, 2 * DA_QK_WIDTH, 2 * DA_QK_WIDTH + DA_V_WIDTH], axis=-1)
    q = q.reshape(B, S, DA_HEADS, 2, DA_HEAD_DIM).astype(f32) * (DA_HEAD_DIM ** -0.5)
    k = k.reshape(B, S, DA_HEADS, 2, DA_HEAD_DIM).astype(f32)
    v = v.reshape(B, S, DA_HEADS, DA_V_DIM).astype(f32)
    lam = (jnp.exp(jnp.sum(lq1.astype(f32) * lk1.astype(f32)))
           - jnp.exp(jnp.sum(lq2.astype(f32) * lk2.astype(f32))) + lam_init)
    slopes = alibi_slopes(DA_HEADS)
    k_pos = jnp.arange(S)
    n_blocks = S // Q_BLOCK
    q_blocks = jnp.moveaxis(q.reshape(B, n_blocks, Q_BLOCK, DA_HEADS, 2, DA_HEAD_DIM), 1, 0)

    def attend(args):
        q_blk, blk = args
        q_pos = blk * Q_BLOCK + jnp.arange(Q_BLOCK)
        scores = jnp.einsum("bqhcd,bkhcd->bhcqk", q_blk, k)
        dist = jnp.abs(q_pos[:, None] - k_pos[None, :]).astype(f32)
        bias = -slopes[:, None, None, None] * dist
        allowed = (k_pos[None, :] // CHUNK) <= (q_pos[:, None] // CHUNK)
        scores = jnp.where(allowed, scores + bias, -jnp.inf)
        probs = jax.nn.softmax(scores, axis=-1)
        diff = probs[:, :, 0] - lam * probs[:, :, 1]
        return jnp.einsum("bhqk,bkhe->bqhe", diff, v)

    out = lax.map(attend, (q_blocks, jnp.arange(n_blocks)))
    out = jnp.moveaxis(out, 0, 1).reshape(B, S, DA_HEADS, DA_V_DIM)
    out = head_rms_norm(out) * subln_gain.astype(f32) * (1.0 - lam_init)
    y = jax.nn.silu(g.astype(f32)) * out.reshape(B, S, DA_V_WIDTH)
    return jnp.einsum("bse,ed->bsd", y.astype(h.dtype), w_out)


def retention(h, w_in, w_out):
    B, S, _ = h.shape
    f32 = jnp.float32
    n_chunks = S // CHUNK
    proj = jnp.einsum("bsd,de->bse", h, w_in)
    q, k, v, g = jnp.split(proj, [RET_QK_WIDTH, 2 * RET_QK_WIDTH, 2 * RET_QK_WIDTH + RET_V_WIDTH], axis=-1)
    q = q.reshape(B, n_chunks, CHUNK, RET_HEADS, RET_QK_DIM).astype(f32)
    k = k.reshape(B, n_chunks, CHUNK, RET_HEADS, RET_QK_DIM).astype(f32) * (RET_QK_DIM ** -0.5)
    v = v.reshape(B, n_chunks, CHUNK, RET_HEADS, RET_V_DIM).astype(f32)

    log_gamma = retention_log_gammas(RET_HEADS)
    idx = jnp.arange(CHUNK, dtype=f32)
    intra_decay = jnp.exp(log_gamma[:, None, None] * jnp.abs(idx[:, None] - idx[None, :]))
    query_decay = jnp.exp(idx[:, None] * log_gamma[None, :])
    key_decay = jnp.exp((CHUNK - idx)[:, None] * log_gamma[None, :])
    chunk_decay = jnp.exp(CHUNK * log_gamma)

    inner = jnp.einsum("bnihd,bnjhd->bnhij", q, k) * intra_decay
    inner_out = jnp.einsum("bnhij,bnjhe->bnihe", inner, v)

    def step(state, qkv):
        q_c, k_c, v_c = qkv
        cross = jnp.einsum("bihd,bhde->bihe", q_c * query_decay[None, :, :, None], state)
        state = (chunk_decay[None, :, None, None] * state
                 + jnp.einsum("bjhd,bjhe->bhde", k_c * key_decay[None, :, :, None], v_c))
        return state, cross

    state0 = jnp.zeros((B, RET_HEADS, RET_QK_DIM, RET_V_DIM), f32)
    _, cross = lax.scan(step, state0, (jnp.moveaxis(q, 1, 0), jnp.moveaxis(k, 1, 0), jnp.moveaxis(v, 1, 0)))
    out = inner_out + jnp.moveaxis(cross, 0, 1)
    out = head_rms_norm(out).reshape(B, S, RET_V_WIDTH)
    y = jax.nn.silu(g.astype(f32)) * out
    return jnp.einsum("bse,ed->bsd", y.astype(h.dtype), w_out)


def setup_inputs(seed: int = 0) -> dict:
    key = jax.random.key(seed)
    ks = jax.random.split(key, 16)
    D = D_MODEL
    nrm = jax.random.normal
    return {
        "x": nrm(ks[0], (BATCH, SEQ, D), jnp.float32),
        "c": nrm(ks[1], (BATCH, D), jnp.float32),
        "ada_w": nrm(ks[2], (DEPTH, D, 3 * D), jnp.float32) * (0.5 * D ** -0.5),
        "ada_b": nrm(ks[3], (DEPTH, 3 * D), jnp.float32) * 0.01,
        "pre_gain": 1.0 + 0.05 * nrm(ks[4], (DEPTH, D), jnp.float32),
        "post_gain": 1.0 + 0.05 * nrm(ks[5], (DEPTH, D), jnp.float32),
        "da_w_in": nrm(ks[6], (N_ATTN_LAYERS, D, DA_IN_WIDTH), jnp.float32) * D ** -0.5,
        "da_w_out": nrm(ks[7], (N_ATTN_LAYERS, DA_V_WIDTH, D), jnp.float32) * DA_V_WIDTH ** -0.5,
        "da_lambda_q1": 0.1 * nrm(ks[8], (N_ATTN_LAYERS, DA_HEAD_DIM), jnp.float32),
        "da_lambda_k1": 0.1 * nrm(ks[9], (N_ATTN_LAYERS, DA_HEAD_DIM), jnp.float32),
        "da_lambda_q2": 0.1 * nrm(ks[10], (N_ATTN_LAYERS, DA_HEAD_DIM), jnp.float32),
        "da_lambda_k2": 0.1 * nrm(ks[11], (N_ATTN_LAYERS, DA_HEAD_DIM), jnp.float32),
        "da_subln_gain": 1.0 + 0.05 * nrm(ks[12], (N_ATTN_LAYERS, DA_V_DIM), jnp.float32),
        "ret_w_in": nrm(ks[13], (N_RET_LAYERS, D, RET_IN_WIDTH), jnp.float32) * D ** -0.5,
        "ret_w_out": nrm(ks[14], (N_RET_LAYERS, RET_V_WIDTH, D), jnp.float32) * RET_V_WIDTH ** -0.5,
    }


def reference(x, c, ada_w, ada_b, pre_gain, post_gain, da_w_in, da_w_out, da_lambda_q1, da_lambda_k1,
              da_lambda_q2, da_lambda_k2, da_subln_gain, ret_w_in, ret_w_out):
    cond = jax.nn.silu(c)
    for layer in range(DEPTH):
        mod = jnp.einsum("bd,de->be", cond, ada_w[layer]) + ada_b[layer]
        shift, scale, gate = jnp.split(mod, 3, axis=-1)
        h = rms_norm(x, pre_gain[layer]) * (1.0 + scale[:, None, :]) + shift[:, None, :]
        j = layer // N_MIXERS
        if layer % N_MIXERS == 0:
            lam_init = 0.8 - 0.6 * math.exp(-0.3 * layer)
            y = diff_attention(h, da_w_in[j], da_w_out[j], da_lambda_q1[j], da_lambda_k1[j],
                               da_lambda_q2[j], da_lambda_k2[j], da_subln_gain[j], lam_init)
        else:
            y = retention(h, ret_w_in[j], ret_w_out[j])
        x = x + gate[:, None, :] * rms_norm(y, post_gain[layer])
    return x
```

```python
import math
from contextlib import ExitStack

import numpy as np
import ml_dtypes

import concourse.bass as bass
import concourse.mybir as mybir
from concourse.bass_utils import run_bass_kernel_spmd

F32 = mybir.dt.float32
BF16 = mybir.dt.bfloat16
ALU = mybir.AluOpType
AF = mybir.ActivationFunctionType
ENGS = ("pe", "act", "dve", "pool", "sp")

S_LEN = 8192
D = 1024
NB = S_LEN // 128
NSB = S_LEN // 512
HL = ((7, 5, 3, 1), (6, 4, 2, 0))
T_SKIP = 40.0
SLOT_SLOPE_MIN = [min(2.0 ** -(HL[0][s] + 1), 2.0 ** -(HL[1][s] + 1)) for s in range(4)]
SLOT_WIDE = [max(2.0 ** -(HL[0][s] + 1), 2.0 ** -(HL[1][s] + 1)) <= 0.125 for s in range(4)]
SLOT_BACK = [min(NB, int((T_SKIP / SLOT_SLOPE_MIN[s] + 127) // 128)) for s in range(4)]
SLOT_RING = [min(NB, ((SLOT_BACK[s] + 4 + 3) // 4) * 4) for s in range(4)]
NEG_BIG = -30000.0
LAM_INIT0 = 0.8 - 0.6 * math.exp(-0.3 * 0)


class Buf:
    __slots__ = ("name", "w", "r", "excl")

    def __init__(self, name="", excl=False):
        self.name = name
        self.w = None
        self.r = []
        self.excl = excl


class Sched:
    def __init__(self, nc):
        self.nc = nc
        self.ops = {e: [] for e in ENGS}
        self.waited = {e: {} for e in ENGS}
        self.dma_sems = []
        self.barrier_skip = set()

    def _deps(self, eng, reads, writes):
        deps = []
        for b in list(reads) + list(writes):
            if b.w is not None:
                deps.append((b.w, "raw"))
        for b in writes:
            for t in b.r:
                deps.append((t, "war"))
        for b in reads:
            if b.excl:
                for t in b.r:
                    deps.append((t, "war"))
        wd = self.waited[eng]
        best = {}
        for t, kind in deps:
            if t[0] == "e" and t[1] == eng and eng == "pe":
                continue
            key = (t[0], t[1])
            if wd.get(key, -1) >= t[2]:
                continue
            if key not in best or best[key][2] < t[2]:
                best[key] = t
        waits = list(best.values())
        for t in waits:
            wd[(t[0], t[1])] = t[2]
            if t[0] == "e":
                self.ops[t[1]][t[2]]["marked"] = True
        return waits

    def op(self, eng, fn, reads=(), writes=()):
        waits = self._deps(eng, reads, writes)
        idx = len(self.ops[eng])
        self.ops[eng].append({"waits": waits, "fn": fn, "marked": False, "dma": None})
        tok = ("e", eng, idx)
        for b in reads:
            b.r.append(tok)
        for b in writes:
            b.w = tok
            b.r = []
        return tok

    def new_dma_sem(self):
        self.dma_sems.append(0)
        return len(self.dma_sems) - 1

    def dma(self, queue, fn, sem, reads=(), writes=(), inc=16):
        waits = self._deps(queue, reads, writes)
        self.dma_sems[sem] += inc
        tok = ("d", sem, self.dma_sems[sem])
        self.ops[queue].append({"waits": waits, "fn": fn, "marked": False, "dma": sem, "inc": inc})
        for b in reads:
            b.r.append(tok)
        for b in writes:
            b.w = tok
            b.r = []
        return tok

    def wait_all(self, eng, toks):
        for t in toks:
            if t[0] == "e":
                self.ops[t[1]][t[2]]["marked"] = True
        self.ops[eng].append({"waits": list(toks), "fn": None, "marked": False, "dma": None})

    def barrier(self):
        last = {}
        for e in ENGS:
            for i in range(len(self.ops[e]) - 1, -1, -1):
                o = self.ops[e][i]
                if o["dma"] is None and o["fn"] is not None:
                    last[e] = ("e", e, i)
                    break
        for e in ENGS:
            toks = [t for k, t in last.items() if k != e]
            toks += [("d", i, v) for i, v in enumerate(self.dma_sems) if v > 0 and i not in self.barrier_skip]
            wd = self.waited[e]
            toks = [t for t in toks if wd.get((t[0], t[1]), -1) < t[2]]
            for t in toks:
                wd[(t[0], t[1])] = t[2]
            self.wait_all(e, toks)

    def emit(self, stack):
        nc = self.nc
        esem = {e: stack.enter_context(nc.semaphore("s_" + e)) for e in ENGS}
        dsem = [stack.enter_context(nc.semaphore("d%d" % i)) for i in range(len(self.dma_sems))]
        pref = {}
        for e in ENGS:
            c = 0
            arr = []
            for o in self.ops[e]:
                if o["marked"]:
                    c += 1
                arr.append(c)
            pref[e] = arr
        block = stack.enter_context(nc.Block())

        def run(e_name):
            def body(e):
                for o in self.ops[e_name]:
                    for t in o["waits"]:
                        if t[0] == "e":
                            e.wait_ge(esem[t[1]], pref[t[1]][t[2]])
                        else:
                            e.wait_ge(dsem[t[1]], t[2])
                    if o["fn"] is None:
                        continue
                    ins = o["fn"](e)
                    if o["dma"] is not None:
                        ins.then_inc(dsem[o["dma"]], o["inc"])
                    elif o["marked"]:
                        ins.then_inc(esem[e_name], 1)
            return body

        block.tensor(run("pe"))
        block.scalar(run("act"))
        block.vector(run("dve"))
        block.gpsimd(run("pool"))
        block.sync(run("sp"))
        return {e: len(self.ops[e]) for e in ENGS}


class Ctx:
    ARENA_F32 = 47 * 1024

    def __init__(self, nc, st):
        self.nc = nc
        self.st = st
        self.S = Sched(nc)
        self.bufs = {}
        self.banks = []
        self.bank_b = []
        self.arena = st.enter_context(nc.sbuf_tensor("arena", [128, self.ARENA_F32], F32))
        self.off = 0
        self.phase = 0

    def new_phase(self):
        self.S.barrier()
        self.off = 0
        self.bufs = {}
        self.phase += 1

    def sb(self, name, shape, dt):
        n = 1
        for d in shape[1:]:
            n *= d
        nbytes = n * (2 if dt == BF16 else 4)
        nf = (nbytes + 31) // 32 * 8
        assert self.off + nf <= self.ARENA_F32, ("SBUF arena overflow", name, self.off, nf)
        ap = self.arena[:, self.off:self.off + nbytes // 4]
        self.off += nf
        if dt == BF16:
            ap = ap.bitcast(BF16)
        if len(shape) == 3:
            ap = ap.rearrange("p (a b) -> p a b", a=shape[1])
        elif len(shape) == 4:
            ap = ap.rearrange("p (a b c) -> p a b c", a=shape[1], b=shape[2])
        return ap

    def alloc_banks(self):
        for i in range(8):
            self.banks.append(self.st.enter_context(self.nc.psum_tensor("bank%d" % i, [128, 512], F32)))
            self.bank_b.append(Buf("bank%d" % i, excl=True))

    def B(self, name):
        if name not in self.bufs:
            self.bufs[name] = Buf(name)
        return self.bufs[name]

    def din(self, name, shape, dt=F32):
        return self.nc.dram_tensor(name, list(shape), dt, kind="ExternalInput").ap()

    def dscratch(self, name, shape, dt=F32):
        return self.nc.dram_tensor(name, list(shape), dt)

    def dout(self, name, shape, dt=F32):
        return self.nc.dram_tensor(name, list(shape), dt, kind="ExternalOutput").ap()

    def load_const(self, tile_ap, dram_ap, buf, queue="sp"):
        sem = self.S.new_dma_sem()
        return self.S.dma(queue, lambda e: e.dma_start(out=tile_ap, in_=dram_ap), sem, writes=[buf])


def emit_adaln(cx, ccol_d, adaw_d, adab_t, adab_b, wchunk, wchunk_b, ncols, consume):
    S = cx.S
    ccol = cx.sb("ccol", [128, 8], F32)
    cth = cx.sb("cth", [128, 8], F32)
    cond = cx.sb("cond", [128, 8], F32)
    crep = cx.sb("crep", [128, 8, 128], F32)
    b_c, b_crep = cx.B("ccol"), cx.B("crep")
    cx.load_const(ccol[:], ccol_d, b_c)
    S.op("act", lambda e: e.activation(out=cth[:], in_=ccol[:], func=AF.Tanh, scale=0.5), reads=[b_c], writes=[cx.B("cth")])
    S.op("dve", lambda e: e.scalar_tensor_tensor(out=cond[:], in0=cth[:], scalar=1.0, in1=ccol[:], op0=ALU.add, op1=ALU.mult),
         reads=[cx.B("cth"), b_c], writes=[cx.B("cond")])
    S.op("dve", lambda e: e.tensor_scalar(out=cond[:], in0=cond[:], scalar1=0.5, scalar2=None, op0=ALU.mult),
         reads=[cx.B("cond")], writes=[cx.B("cond")])
    for j in range(8):
        S.op("dve", lambda e, j=j: e.tensor_copy(out=crep[:, j, :], in_=cond[:, j:j + 1].to_broadcast([128, 128])),
             reads=[cx.B("cond")], writes=[b_crep])
    wsem = S.new_dma_sem()
    adaw_v = adaw_d.rearrange("(dc p) n -> p dc n", p=128)
    for ci in range(ncols // 512):
        S.dma("sp", lambda e, ci=ci: e.dma_start(out=wchunk, in_=adaw_v[:, :, ci * 512:(ci + 1) * 512]), wsem, writes=[wchunk_b])
        bk = ci % 2 + 6
        for j in range(8):
            S.op("pe", lambda e, j=j, bk=bk: e.matmul(cx.banks[bk][:], lhsT=crep[:, j, :], rhs=wchunk[:, j, :],
                                                        start=(j == 0), stop=(j == 7)),
                 reads=[b_crep, wchunk_b], writes=[cx.bank_b[bk]])
        consume(ci, cx.banks[bk], cx.bank_b[bk])


def emit_adaln_all(cx, io, A_t, B_t, b_A, b_B):
    S = cx.S
    banks, bank_b = cx.banks, cx.bank_b
    ccol = cx.sb("ccol", [128, 8], F32)
    cth = cx.sb("cth", [128, 8], F32)
    cond = cx.sb("cond", [128, 8], F32)
    crep = cx.sb("crep", [128, 8, 128], F32)
    adab = cx.sb("adab", [128, 3072], F32)
    g_pre = cx.sb("g_pre", [128, D], F32)
    g_post = cx.sb("g_post", [128, D], F32)
    tga = cx.sb("tga", [128, 512], F32)
    stg = [cx.sb("stg%d" % i, [128, D], F32) for i in range(3)]
    wch = [cx.sb("wch%d" % i, [128, 8, 512], F32) for i in range(2)]
    b_c, b_cth, b_cond, b_crep, b_adab, b_gpre, b_gpost, b_tga = [Buf() for _ in range(8)]
    b_stg = [Buf(), Buf(), Buf()]
    b_wch = [Buf(), Buf()]
    cx.load_const(ccol, io["ccol"], b_c)
    S.op("act", lambda e: e.activation(out=cth, in_=ccol, func=AF.Tanh, scale=0.5), reads=[b_c], writes=[b_cth])
    S.op("dve", lambda e: e.scalar_tensor_tensor(out=cond, in0=cth, scalar=1.0, in1=ccol, op0=ALU.add, op1=ALU.mult),
         reads=[b_cth, b_c], writes=[b_cond])
    S.op("dve", lambda e: e.tensor_scalar(out=cond, in0=cond, scalar1=0.5, scalar2=None, op0=ALU.mult), reads=[b_cond], writes=[b_cond])
    for j in range(8):
        S.op("dve", lambda e, j=j: e.tensor_copy(out=crep[:, j, :], in_=cond[:, j:j + 1].to_broadcast([128, 128])),
             reads=[b_cond], writes=[b_crep])
    wsem = [S.new_dma_sem(), S.new_dma_sem()]
    csem = [S.new_dma_sem() for _ in range(3)]
    ssem = [S.new_dma_sem() for _ in range(3)]
    n = 0
    for layer in range(2):
        S.dma("sp", lambda e, layer=layer: e.dma_start(out=adab, in_=io["adab%d" % layer]), csem[0], writes=[b_adab])
        S.dma("sp", lambda e, layer=layer: e.dma_start(out=g_pre, in_=io["pre%d" % layer]), csem[1], writes=[b_gpre])
        S.dma("sp", lambda e, layer=layer: e.dma_start(out=g_post, in_=io["post%d" % layer]), csem[2], writes=[b_gpost])
        adaw_v = io["adaw%d" % layer].rearrange("(dc p) n -> p dc n", p=128)
        for ci in range(6):
            wi = n % 2
            bk = 6 + n % 2
            n += 1
            S.dma("sp", lambda e, ci=ci, wi=wi, adaw_v=adaw_v: e.dma_start(out=wch[wi], in_=adaw_v[:, :, ci * 512:(ci + 1) * 512]), wsem[wi], writes=[b_wch[wi]])
            for j in range(8):
                S.op("pe", lambda e, j=j, bk=bk, wi=wi: e.matmul(banks[bk][:], lhsT=crep[:, j, :], rhs=wch[wi][:, j, :], start=(j == 0), stop=(j == 7)),
                     reads=[b_crep, b_wch[wi]], writes=[bank_b[bk]])
            kind, half = ci // 2, ci % 2
            cols = slice(half * 512, half * 512 + 512)
            acols = slice(ci * 512, ci * 512 + 512)
            if kind == 0:
                dst, dbuf = (B_t, b_B) if layer == 0 else (stg[1], b_stg[1])
                S.op("dve", lambda e, bk=bk, dst=dst, cols=cols, acols=acols: e.tensor_tensor(out=dst[:, cols], in0=banks[bk][:], in1=adab[:, acols], op=ALU.add),
                     reads=[bank_b[bk], b_adab], writes=[dbuf])
            else:
                S.op("dve", lambda e, bk=bk, acols=acols: e.tensor_tensor(out=tga, in0=banks[bk][:], in1=adab[:, acols], op=ALU.add),
                     reads=[bank_b[bk], b_adab], writes=[b_tga])
                if kind == 1:
                    dst, dbuf = (A_t, b_A) if layer == 0 else (stg[0], b_stg[0])
                    S.op("dve", lambda e, dst=dst, cols=cols: e.scalar_tensor_tensor(out=dst[:, cols], in0=tga, scalar=1.0, in1=g_pre[:, cols], op0=ALU.add, op1=ALU.mult),
                         reads=[b_tga, b_gpre], writes=[dbuf])
                else:
                    S.op("dve", lambda e, cols=cols: e.tensor_tensor(out=stg[2][:, cols], in0=tga, in1=g_post[:, cols], op=ALU.mult),
                         reads=[b_tga, b_gpost], writes=[b_stg[2]])
        if layer == 1:
            S.dma("sp", lambda e: e.dma_start(out=io["adaA1"][:, :], in_=stg[0]), ssem[0], reads=[b_stg[0]])
            S.dma("sp", lambda e: e.dma_start(out=io["adaB1"][:, :], in_=stg[1]), ssem[1], reads=[b_stg[1]])
        S.dma("sp", lambda e, layer=layer: e.dma_start(out=io["adaG%d" % layer][:, :], in_=stg[2]), ssem[2], reads=[b_stg[2]])


PAIRS = [[0, 1], [2, 3], [4, 5], [6, 7]]


def phase_l0(cx, io, nsb=NSB):
    S = cx.S
    x_d, ccol_d = io["x"], io["ccol"]
    adaw_d, adab_d, pg_d = io["adaw0"][:, 0:2048], io["adab0"][:, 0:2048], io["pre0"]
    w_d = [io[n] for n in ("wq0", "wk0", "wv0", "wg0")]
    banks, bank_b = cx.banks, cx.bank_b
    ring = [min(NB, SLOT_RING[s] + 4) for s in range(4)]
    ring_off = [0]
    for s in range(4):
        ring_off.append(ring_off[-1] + ring[s])
    RT = ring_off[-1]
    kT = cx.sb("kT", [128, RT * 128], BF16)
    Va = cx.sb("Va", [128, RT, 130], BF16)
    wbf = [cx.sb("wbf%d" % i, [128, 8, 512], BF16) for i in range(4)]
    A_t = cx.sb("A_t", [128, D], F32)
    B_t = cx.sb("B_t", [128, D], F32)
    kbt = cx.sb("kbt", [128, 4, 68], F32)
    dg0 = cx.sb("dg0", [128, 4, 128], F32)
    cbt = cx.sb("cbt", [128, 4, 4], F32)
    ident = cx.sb("ident", [128, 128], BF16)
    lsc = cx.sb("lsc", [128, 8], F32)
    subg = cx.sb("subg", [128, 512], F32)
    mhalf = cx.sb("mhalf", [128, 4], F32)
    b_w = [Buf() for _ in range(4)]
    b_A, b_B, b_lam, b_ones = Buf(), Buf(), Buf(), Buf()
    cx.load_const(kbt.rearrange("p a b -> p (a b)"), io["kbtab"], cx.B("kbt"))
    cx.load_const(dg0.rearrange("p a b -> p (a b)"), io["dg0"], cx.B("dg0"))
    cx.load_const(cbt.rearrange("p a b -> p (a b)"), io["cbt"], cx.B("cbt"))
    cx.load_const(ident, io["ident"], cx.B("ident"))
    cx.load_const(subg, io["subg"], cx.B("subg"))
    for i in range(4):
        sem = S.new_dma_sem()
        wv_ = w_d[i].rearrange("(dc p) n -> p dc n", p=128)
        S.dma("pool", lambda e, i=i, wv_=wv_: e.dma_start(out=wbf[i], in_=wv_), sem, writes=[b_w[i]])
    S.op("pool", lambda e: e.memset(mhalf, -0.5), writes=[cx.B("mhalf")])
    S.op("pool", lambda e: e.memset(Va[:, :, 128:130], 1.0), writes=[b_ones])
    S.op("dve", lambda e: e.tensor_scalar(out=subg, in0=subg, scalar1=0.5 * (1.0 - LAM_INIT0), scalar2=None, op0=ALU.mult),
         reads=[cx.B("subg")], writes=[cx.B("subg")])

    off0 = cx.off
    lamv = cx.sb("lamv", [128, 256], F32)
    ljunk = cx.sb("ljunk", [128, 64], F32)
    cx.load_const(lamv, io["lamv"], cx.B("lamv"))
    b_lj = Buf()
    S.op("dve", lambda e: e.scalar_tensor_tensor(out=ljunk, in0=lamv[:, 0:64], scalar=1.0, in1=lamv[:, 64:128],
                                                  op0=ALU.mult, op1=ALU.mult, accum_out=lsc[:, 0:1]),
         reads=[cx.B("lamv")], writes=[b_lj, b_lam])
    S.op("dve", lambda e: e.scalar_tensor_tensor(out=ljunk, in0=lamv[:, 128:192], scalar=1.0, in1=lamv[:, 192:256],
                                                  op0=ALU.mult, op1=ALU.mult, accum_out=lsc[:, 1:2]),
         reads=[cx.B("lamv")], writes=[b_lj, b_lam])
    S.op("act", lambda e: e.activation(out=lsc[:, 2:4], in_=lsc[:, 0:2], func=AF.Exp), reads=[b_lam], writes=[b_lam])
    S.op("dve", lambda e: e.tensor_tensor(out=lsc[:, 4:5], in0=lsc[:, 2:3], in1=lsc[:, 3:4], op=ALU.subtract),
         reads=[b_lam], writes=[b_lam])
    S.op("dve", lambda e: e.tensor_scalar(out=lsc[:, 5:6], in0=lsc[:, 4:5], scalar1=LAM_INIT0, scalar2=-1.0,
                                           op0=ALU.add, op1=ALU.mult), reads=[b_lam], writes=[b_lam])
    neglam = lsc[:, 5:6]

    emit_adaln_all(cx, io, A_t, B_t, b_A, b_B)
    S.barrier()
    cx.off = off0

    NXB = 4
    xbuf = cx.sb("xbuf", [128, NXB, 1024], F32)
    hT = cx.sb("hT", [128, 8, 512], BF16)
    hb = [cx.sb("hb%d" % i, [128, 1024], BF16) for i in range(NXB)]
    tmp = cx.sb("tmp", [128, 1024], F32)
    qT = [[cx.sb("qT%d%d" % (p, c), [128, 4, 512], BF16) for c in range(2)] for p in range(2)]
    gs = [cx.sb("gs%d" % p, [128, 4, 512], BF16) for p in range(2)]
    tg = cx.sb("tg", [128, 512], F32)
    tg2 = cx.sb("tg2", [128, 512], F32)
    PT = [cx.sb("PT%d" % i, [128, 512], BF16) for i in range(3)]
    dtmp = [cx.sb("dtmp%d" % i, [128, 128], F32) for i in range(2)]
    stat = [cx.sb("stat%d" % i, [128, 4], F32) for i in range(NXB)]
    o1_t = cx.sb("o1_t", [128, 4, 128], F32)
    o_t = cx.sb("o_t", [128, 4, 128], F32)
    sqj = cx.sb("sqj", [128, 128], F32)
    ep = cx.sb("ep", [128, 24], F32)
    yb = [cx.sb("yb%d" % i, [128, 4, 128], BF16) for i in range(2)]
    ystage = [cx.sb("ystage%d" % i, [128, 512], BF16) for i in range(2)]
    b_x = [Buf() for _ in range(NXB)]
    b_hT, b_tmp, b_tg, b_tg2 = Buf(), Buf(), Buf(), Buf()
    b_hb = [Buf() for _ in range(NXB)]
    b_qT = [Buf(), Buf()]
    b_gs = [Buf(), Buf()]
    b_PT = [Buf(), Buf(), Buf()]
    b_dtmp = [Buf(), Buf()]
    b_stat = [Buf() for _ in range(NXB)]
    b_kring = [[Buf() for _ in range(ring[s] // 4)] for s in range(4)]
    b_vring = [[Buf() for _ in range(ring[s] // 4)] for s in range(4)]
    b_o1, b_o, b_sqj, b_ep0, b_ep1 = Buf(), Buf(), Buf(), Buf(), Buf()
    b_yb = [Buf(), Buf()]
    deferred = []
    b_ys = [Buf(), Buf()]
    b_ywr = [Buf(), Buf()]
    xsem = [S.new_dma_sem() for _ in range(NXB)]
    ysem = [S.new_dma_sem(), S.new_dma_sem()]
    for p in range(2):
        S.op("pool", lambda e, p=p: e.memset(qT[p][0][64:128, :, :], 0.0), writes=[b_qT[p]])
        S.op("pool", lambda e, p=p: e.memset(qT[p][1][0:64, :, :], 0.0), writes=[b_qT[p]])

    def ring_col(s, j):
        return (ring_off[s] + (j % ring[s])) * 128

    def ring_blk(s, j):
        return ring_off[s] + (j % ring[s])

    def ring_grp(s, j):
        return (j // 4) % (ring[s] // 4)

    pb_i = [0]

    def next_pbank():
        return 7

    ST_BANKS = (0, 1, 6)
    ntb = 4 * nsb

    def load_x(tb):
        xi = tb % NXB
        S.dma("sp", lambda e: e.dma_start(out=xbuf[:, xi, :], in_=x_d[tb * 128:(tb + 1) * 128, :]), xsem[xi], writes=[b_x[xi]])

    def chunk_norm_a(tb):
        def f():
            xi = tb % NXB
            xt = xbuf[:, xi, :]
            stt = stat[xi]
            S.op("dve", lambda e: e.scalar_tensor_tensor(out=tmp, in0=xt, scalar=1.0, in1=xt, op0=ALU.mult, op1=ALU.mult, accum_out=stt[:, 0:1]),
                 reads=[b_x[xi]], writes=[b_tmp, b_stat[xi]])
            S.op("dve", lambda e: e.tensor_scalar(out=stt[:, 1:2], in0=stt[:, 0:1], scalar1=1.0 / D, scalar2=1e-6, op0=ALU.mult, op1=ALU.add),
                 reads=[b_stat[xi]], writes=[b_stat[xi]])
            S.op("pool", lambda e: e.tensor_tensor(out=stt[:, 2:3], in0=stt[:, 1:2], in1=mhalf[:, 0:1], op=ALU.pow),
                 reads=[b_stat[xi], cx.B("mhalf")], writes=[b_stat[xi]])
        return f

    def chunk_norm(tb):
        def f():
            xi = tb % NXB
            xt = xbuf[:, xi, :]
            stt = stat[xi]
            S.op("dve", lambda e: e.scalar_tensor_tensor(out=tmp, in0=xt, scalar=stt[:, 2:3], in1=A_t, op0=ALU.mult, op1=ALU.mult),
                 reads=[b_x[xi], b_stat[xi], b_A], writes=[b_tmp])
            S.op("pool", lambda e: e.tensor_tensor(out=hb[xi], in0=tmp, in1=B_t, op=ALU.add), reads=[b_tmp, b_B], writes=[b_hb[xi]])
            if tb + NXB < ntb:
                load_x(tb + NXB)
        return f

    def evac(sb, out_ap, in_ap, reads, writes):
        if sb < 8:
            S.op("act", lambda e: e.copy(out=out_ap, in_=in_ap), reads=reads, writes=writes)
        else:
            S.op("dve", lambda e: e.tensor_copy(out=out_ap, in_=in_ap), reads=reads, writes=writes)

    def chunk_tr(tb):
        def f():
            xi, t = tb % NXB, tb % 4
            pb = next_pbank()
            pbv = banks[pb][:].bitcast(BF16)
            for dc in range(8):
                S.op("pe", lambda e, dc=dc: e.transpose(out=pbv[:, dc * 128:(dc + 1) * 128], in_=hb[xi][:, dc * 128:(dc + 1) * 128], identity=ident),
                     reads=[b_hb[xi], cx.B("ident")], writes=[bank_b[pb]])
            evac(tb // 4, hT[:, :, t * 128:(t + 1) * 128], pbv.rearrange("p (dc n) -> p dc n", n=128), [bank_b[pb]], [b_hT])
        return f

    def chunk_q(sb, s):
        def f():
            p = sb % 2
            pb = next_pbank()
            for dc in range(8):
                S.op("pe", lambda e, dc=dc: e.matmul(banks[pb][:], lhsT=wbf[0][:, dc, s * 128:(s + 1) * 128], rhs=hT[:, dc, :], start=(dc == 0), stop=(dc == 7)),
                     reads=[b_w[0], b_hT], writes=[bank_b[pb]])
            evac(sb, qT[p][0][0:64, s, :], banks[pb][0:64, :], [bank_b[pb]], [b_qT[p]])
            evac(sb, qT[p][1][64:128, s, :], banks[pb][64:128, :], [bank_b[pb]], [b_qT[p]])
        return f

    def chunk_k(sb, s):
        def f():
            pb = next_pbank()
            for dc in range(8):
                S.op("pe", lambda e, dc=dc: e.matmul(banks[pb][:], lhsT=wbf[1][:, dc, s * 128:(s + 1) * 128], rhs=hT[:, dc, :], start=(dc == 0), stop=(dc == 7)),
                     reads=[b_w[1], b_hT], writes=[bank_b[pb]])
            c0 = ring_col(s, 4 * sb)
            evac(sb, kT[:, c0:c0 + 512], banks[pb][:], [bank_b[pb]], [b_kring[s][ring_grp(s, 4 * sb)]])
        return f

    def chunk_v(sb, t):
        def f():
            pb = next_pbank()
            for dc in range(8):
                S.op("pe", lambda e, dc=dc: e.matmul(banks[pb][:], lhsT=hT[:, dc, t * 128:(t + 1) * 128], rhs=wbf[2][:, dc, :], start=(dc == 0), stop=(dc == 7)),
                     reads=[b_w[2], b_hT], writes=[bank_b[pb]])
            for s in range(4):
                rb = ring_blk(s, 4 * sb + t)
                evac(sb, Va[:, rb, 0:128], banks[pb][:, s * 128:(s + 1) * 128], [bank_b[pb]], [b_vring[s][ring_grp(s, 4 * sb)]])
        return f

    def chunk_g(sb, t):
        def f():
            p = sb % 2
            pb = next_pbank()
            for dc in range(8):
                S.op("pe", lambda e, dc=dc: e.matmul(banks[pb][:], lhsT=hT[:, dc, t * 128:(t + 1) * 128], rhs=wbf[3][:, dc, :], start=(dc == 0), stop=(dc == 7)),
                     reads=[b_w[3], b_hT], writes=[bank_b[pb]])
            S.op("act", lambda e: e.activation(out=tg, in_=banks[pb][:], func=AF.Tanh, scale=0.5), reads=[bank_b[pb]], writes=[b_tg])
            S.op("dve", lambda e: e.scalar_tensor_tensor(out=tg2, in0=tg, scalar=1.0, in1=banks[pb][:], op0=ALU.add, op1=ALU.mult),
                 reads=[b_tg, bank_b[pb]], writes=[b_tg2])
            S.op("pool", lambda e: e.tensor_tensor(out=gs[p][:, t, :], in0=tg2, in1=subg, op=ALU.mult), reads=[b_tg2, cx.B("subg")], writes=[b_gs[p]])
        return f

    def proj_chunks(sb):
        tb0 = 4 * sb
        ch = [chunk_norm_a(tb0), chunk_norm_a(tb0 + 1), chunk_norm(tb0), chunk_norm_a(tb0 + 2), chunk_norm(tb0 + 1), chunk_norm_a(tb0 + 3),
              chunk_norm(tb0 + 2), chunk_norm(tb0 + 3)] + [chunk_tr(tb0 + t) for t in range(4)]
        ch += [chunk_k(sb, s) for s in range(4)] + [chunk_v(sb, t) for t in range(4)]
        ch += [chunk_q(sb, s) for s in range(4)] + [chunk_g(sb, t) for t in range(4)]
        return ch

    st_i, pt_i, dt_i = [0], [0], [0]
    accs = [(2, 3), (4, 5)]

    def acc_ap(c, r, lo, hi):
        return banks[accs[c][r // 2]][:, (r % 2) * 130 + lo:(r % 2) * 130 + hi]

    def emit_qk(tile):
        sb, s, c, idx, j = tile
        p = sb % 2
        r0 = max(0, j - 4 * sb)
        sti = ST_BANKS[st_i[0] % 3]
        st_i[0] += 1
        tile.append(sti)
        kc = ring_col(s, j)
        S.op("pe", lambda e: e.matmul(banks[sti][:, r0 * 128:512], lhsT=kT[:, kc:kc + 128], rhs=qT[p][c][:, s, r0 * 128:512], start=True, stop=True),
             reads=[b_kring[s][ring_grp(s, j)], b_qT[p]], writes=[bank_b[sti]])

    def emit_exp_pv(tile):
        sb, s, c, idx, j, sti = tile
        wide = SLOT_WIDE[s]
        r0 = max(0, j - 4 * sb)
        ST = banks[sti]
        pti = pt_i[0] % 3
        pt_i[0] += 1
        P = PT[pti]
        rstart = r0
        if j >= 4 * sb:
            di = dt_i[0] % 2
            dt_i[0] += 1
            S.op("dve", lambda e: e.scalar_tensor_tensor(out=dtmp[di], in0=ST[:, r0 * 128:(r0 + 1) * 128], scalar=0.125, in1=dg0[:, s, :], op0=ALU.mult, op1=ALU.add),
                 reads=[bank_b[sti], cx.B("dg0")], writes=[b_dtmp[di]])
            S.op("act", lambda e: e.activation(out=P[:, r0 * 128:(r0 + 1) * 128], in_=dtmp[di], func=AF.Exp, bias=cbt[:, s, r0:r0 + 1], scale=1.0),
                 reads=[b_dtmp[di], cx.B("cbt")], writes=[b_PT[pti]])
            rstart = r0 + 1
        if rstart < 4:
            if wide:
                ti = 4 * sb + 3 - j
                S.op("act", lambda e: e.activation(out=P[:, rstart * 128:512], in_=ST[:, rstart * 128:512], func=AF.Exp, bias=kbt[:, s, ti:ti + 1], scale=0.125),
                     reads=[bank_b[sti], cx.B("kbt")], writes=[b_PT[pti]])
            else:
                for r in range(rstart, 4):
                    ti = 4 * sb + r - j
                    S.op("act", lambda e, r=r, ti=ti: e.activation(out=P[:, r * 128:(r + 1) * 128], in_=ST[:, r * 128:(r + 1) * 128], func=AF.Exp,
                                                                     bias=kbt[:, s, ti:ti + 1], scale=0.125),
                         reads=[bank_b[sti], cx.B("kbt")], writes=[b_PT[pti]])
        rb = ring_blk(s, j)
        for r in range(r0, 4):
            S.op("pe", lambda e, r=r: e.matmul(acc_ap(c, r, 0, 130), lhsT=P[:, r * 128:(r + 1) * 128], rhs=Va[:, rb, :],
                                                start=(idx == 0 and r % 2 == 0), stop=True, skip_group_check=True),
                 reads=[b_PT[pti], b_vring[s][ring_grp(s, j)], b_ones], writes=[bank_b[accs[c][r // 2]]])

    def epi_c0(sb, s):
        for hb2 in range(2):
            bk = accs[0][hb2]
            S.op("dve", lambda e, bk=bk, hb2=hb2: e.reciprocal(out=ep[:, 2 * hb2:2 * hb2 + 2], in_=banks[bk][:, 128:259:130]),
                 reads=[bank_b[bk]], writes=[b_ep0])
        for r in range(4):
            S.op("dve", lambda e, r=r: e.tensor_scalar(out=o1_t[:, r, :], in0=acc_ap(0, r, 0, 128), scalar1=ep[:, r:r + 1], scalar2=None, op0=ALU.mult),
                 reads=[bank_b[accs[0][r // 2]], b_ep0], writes=[b_o1])

    def epi_c1(sb, s):
        p = sb % 2
        for hb2 in range(2):
            bk = accs[1][hb2]
            S.op("dve", lambda e, bk=bk, hb2=hb2: e.reciprocal(out=ep[:, 4 + 2 * hb2:4 + 2 * hb2 + 2], in_=banks[bk][:, 128:259:130]),
                 reads=[bank_b[bk]], writes=[b_ep1])
        S.op("dve", lambda e: e.tensor_scalar(out=ep[:, 8:12], in0=ep[:, 4:8], scalar1=neglam, scalar2=None, op0=ALU.mult),
             reads=[b_ep1, b_lam], writes=[b_ep1])
        for r in range(4):
            S.op("dve", lambda e, r=r: e.scalar_tensor_tensor(out=o_t[:, r, :], in0=acc_ap(1, r, 0, 128), scalar=ep[:, 8 + r:9 + r], in1=o1_t[:, r, :],
                                                               op0=ALU.mult, op1=ALU.add),
                 reads=[bank_b[accs[1][r // 2]], b_ep1, b_o1], writes=[b_o])
        for r in range(4):
            S.op("dve", lambda e, r=r: e.scalar_tensor_tensor(out=sqj, in0=o_t[:, r, :], scalar=1.0, in1=o_t[:, r, :], op0=ALU.mult, op1=ALU.mult,
                                                               accum_out=ep[:, 12 + r:13 + r]),
                 reads=[b_o], writes=[b_sqj, b_ep1])
        S.op("dve", lambda e: e.tensor_scalar(out=ep[:, 12:16], in0=ep[:, 12:16], scalar1=1.0 / 128, scalar2=1e-5, op0=ALU.mult, op1=ALU.add),
             reads=[b_ep1], writes=[b_ep1])
        S.op("pool", lambda e: e.tensor_tensor(out=ep[:, 16:20], in0=ep[:, 12:16], in1=mhalf[:, 0:4], op=ALU.pow),
             reads=[b_ep1, cx.B("mhalf")], writes=[b_ep1])
        yi = (sb * 4 + s) % 2
        ybt = yb[yi]

        def part_b():
            for r in range(4):
                S.op("dve", lambda e, r=r: e.scalar_tensor_tensor(out=ybt[:, r, :], in0=o_t[:, r, :], scalar=ep[:, 16 + r:17 + r],
                                                                   in1=gs[p][:, r, s * 128:(s + 1) * 128], op0=ALU.mult, op1=ALU.mult),
                     reads=[b_o, b_ep1, b_gs[p]], writes=[b_yb[yi]])
        deferred.append([4, part_b])

        def finish():
            pb = next_pbank()
            pbv = banks[pb][:].bitcast(BF16)
            for r in range(4):
                S.op("pe", lambda e, r=r: e.transpose(out=pbv[:, r * 128:(r + 1) * 128], in_=ybt[:, r, :], identity=ident),
                     reads=[b_yb[yi], cx.B("ident")], writes=[bank_b[pb]])
            S.op("dve", lambda e: e.tensor_copy(out=ystage[yi], in_=pbv[:, 0:512]), reads=[bank_b[pb]], writes=[b_ys[yi]])
            S.dma("sp", lambda e: e.dma_start(out=io["yg0_src"][sb // 4][s * 128:(s + 1) * 128, (sb % 4) * 512:(sb % 4 + 1) * 512], in_=ystage[yi]),
                  ysem[yi], reads=[b_ys[yi]], writes=[b_ywr[yi]])
            if sb % 4 == 3 and s == 3:
                g = sb // 4
                S.dma("pool", lambda e: e.collective_compute("AllGather", ALU.bypass, replica_groups=PAIRS,
                                                             ins=[io["yg0_src"][g].ap().opt()], outs=[io["yg0_all"][g].ap().opt()]),
                      io["ccsem"], reads=[b_ywr[0], b_ywr[1]], writes=[io["b_cc0"][g]], inc=1)
        deferred.append([12, finish])

    def tiles_of(sb):
        tl = []
        for s in range(4):
            jmin = max(0, 4 * sb - SLOT_BACK[s])
            for c in range(2):
                for idx, j in enumerate(range(jmin, 4 * sb + 4)):
                    tl.append([sb, s, c, idx, j])
        return tl

    for tb in range(min(NXB, ntb)):
        load_x(tb)
    for f in proj_chunks(0):
        f()
    precast = ["wout0", "wq1", "wk1", "wv1", "wg1", "wout1"]
    for sb in range(nsb):
        if (sb >= 5 or sb == nsb - 1) and precast:
            for nme in ([precast.pop(0)] if sb < nsb - 1 else list(precast)):
                sem = S.new_dma_sem()
                S.dma("pool", lambda e, nme=nme: e.dma_start(out=io[nme + "_bf"][:, :], in_=io[nme]), sem)
            if sb == nsb - 1:
                precast = []
        tiles = tiles_of(sb)
        chunks = proj_chunks(sb + 1) if sb + 1 < nsb else []
        nt = len(tiles)
        emit_qk(tiles[0])
        if nt > 1:
            emit_qk(tiles[1])
        done_chunks = 0
        for i, tile in enumerate(tiles):
            if i + 2 < nt:
                emit_qk(tiles[i + 2])
            emit_exp_pv(tile)
            last_of_group = (i + 1 == nt) or (tiles[i + 1][1] != tile[1]) or (tiles[i + 1][2] != tile[2])
            if last_of_group:
                if tile[2] == 0:
                    epi_c0(sb, tile[1])
                else:
                    epi_c1(sb, tile[1])
            for dfr in deferred:
                dfr[0] -= 1
            deferred.sort(key=lambda d: d[0])
            while deferred and deferred[0][0] <= 0:
                deferred.pop(0)[1]()
            want = (len(chunks) * (i + 1)) // nt
            while done_chunks < want:
                chunks[done_chunks]()
                done_chunks += 1
    deferred.sort(key=lambda d: d[0])
    while deferred:
        deferred.pop(0)[1]()


def l0_consts(hh):
    kb = np.zeros((128, 4, 68), np.float32)
    dg = np.zeros((128, 4, 128), np.float32)
    cb = np.zeros((128, 4, 4), np.float32)
    ki = np.arange(128, dtype=np.float64)[:, None]
    qi = np.arange(128, dtype=np.float64)[None, :]
    allowed = (ki // 64) <= (qi // 64)
    for s in range(4):
        slope = 2.0 ** -(HL[hh][s] + 1)
        t = np.arange(68, dtype=np.float64)[None, :]
        if SLOT_WIDE[s]:
            kb[:, s, :] = slope * (128.0 * (3 - t) + ki - 256.0)
            for r in range(4):
                cb[:, s, r] = slope * (128.0 * r - 192.0)
        else:
            kb[:, s, :] = slope * (-128.0 * t + ki - 64.0)
        d = -slope * np.abs(qi - ki) + slope * (qi - 64.0)
        dg[:, s, :] = np.where(allowed, d, NEG_BIG)
    return kb.reshape(128, 4 * 68), dg.reshape(128, 4 * 128), cb.reshape(128, 16)


def rep128(v):
    return np.ascontiguousarray(np.broadcast_to(np.asarray(v, np.float32).reshape(1, -1), (128, v.size)))


def head_cols(hh, width):
    return np.concatenate([np.arange(h * width, (h + 1) * width) for h in HL[hh]])


def phase_outproj(cx, io, layer, E, ntok=S_LEN):
    EC = E // 128
    if True:
        S = cx.S
        ccol_d = io["ccol"]
        adaw_d, adab_d, pg_d = io["adaw%d" % layer][:, 2048:3072], io["adab%d" % layer][:, 2048:3072], io["post%d" % layer]
        w_d = io["wout%d_bf" % layer][:, :]
        x_d, xo_d = io["op_x%d" % layer], io["op_xo%d" % layer]
        ysrc = io["op_ysrc%d" % layer]
        banks, bank_b = cx.banks, cx.bank_b
        wbf = cx.sb("wbf", [128, EC, D], BF16)
        GP = cx.sb("GP", [128, D], F32)
        NX = 4
        yT = [cx.sb("yT%d" % i, [128, EC, 512], BF16) for i in range(2)]
        xt = [cx.sb("xt%d" % i, [128, D], F32) for i in range(NX)]
        tmp = [cx.sb("tmp%d" % i, [128, D], F32) for i in range(2)]
        sq = cx.sb("sq", [128, D], F32)
        b_sq = Buf()
        xo = [cx.sb("xo%d" % i, [128, D], F32) for i in range(NX)]
        stat = [cx.sb("stat%d" % i, [128, 4], F32) for i in range(2)]
        mhalf = cx.sb("mhalf", [128, 1], F32)
        b_w, b_GP, b_tg = Buf(), Buf(), Buf()
        b_tmp = [Buf(), Buf()]
        b_yT, b_stat = [Buf(), Buf()], [Buf(), Buf()]
        b_xt, b_xo = [Buf() for _ in range(NX)], [Buf() for _ in range(NX)]
        cx.load_const(GP, io["adaG%d" % layer][:, :], b_GP)
        wsem = S.new_dma_sem()
        wv = w_d.rearrange("(ec p) n -> p ec n", p=128)
        for ec0 in range(0, EC, 4):
            S.dma("sp", lambda e, ec0=ec0: e.dma_start(out=wbf[:, ec0:ec0 + 4, :], in_=wv[:, ec0:ec0 + 4, :]), wsem, writes=[b_w])
        S.op("pool", lambda e: e.memset(mhalf[:], -0.5), writes=[cx.B("mhalf")])

        ysem = [S.new_dma_sem(), S.new_dma_sem()]
        xsem = [S.new_dma_sem() for _ in range(NX)]
        osem = [S.new_dma_sem() for _ in range(NX)]
        out_toks = {}
        nblk = ntok // 128

        def emit_loads(tb):
            if tb >= nblk:
                return
            if tb % 4 == 0:
                g4 = tb // 4
                yi = g4 % 2
                y_ap, y_buf = ysrc(g4)
                S.dma("sp", lambda e: e.dma_start(out=yT[yi][:], in_=y_ap), ysem[yi], reads=[y_buf], writes=[b_yT[yi]])
            xi = tb % NX
            S.dma("sp", lambda e: e.dma_start(out=xt[xi][:], in_=x_d[tb * 128:(tb + 1) * 128, :]), xsem[xi], writes=[b_xt[xi]])

        for tb in range(3):
            emit_loads(tb)
        for tb in range(nblk):
            emit_loads(tb + 3)
            g4, t = tb // 4, tb % 4
            yi = g4 % 2
            xi = tb % NX
            pi = tb % 2
            bk = (4 * pi, 4 * pi + 1)
            for half in range(2):
                for ec in range(EC):
                    S.op("pe", lambda e, half=half, ec=ec, yi=yi, t=t, bk=bk: e.matmul(
                        banks[bk[half]][:], lhsT=yT[yi][:, ec, t * 128:(t + 1) * 128], rhs=wbf[:, ec, half * 512:(half + 1) * 512],
                        start=(ec == 0), stop=(ec == EC - 1)), reads=[b_yT[yi], b_w], writes=[bank_b[bk[half]]])
            stt = stat[pi]
            for half in range(2):
                S.op("act", lambda e, half=half, bk=bk, stt=stt: e.activation(
                    out=sq[:, half * 512:(half + 1) * 512], in_=banks[bk[half]][:], func=AF.Square,
                    accum_out=stt[:, half:half + 1]), reads=[bank_b[bk[half]]], writes=[b_sq, b_stat[pi]])
            S.op("dve", lambda e, stt=stt: e.tensor_tensor(out=stt[:, 2:3], in0=stt[:, 0:1], in1=stt[:, 1:2], op=ALU.add),
                 reads=[b_stat[pi]], writes=[b_stat[pi]])
            S.op("dve", lambda e, stt=stt: e.tensor_scalar(out=stt[:, 2:3], in0=stt[:, 2:3], scalar1=1.0 / D, scalar2=1e-6, op0=ALU.mult, op1=ALU.add),
                 reads=[b_stat[pi]], writes=[b_stat[pi]])
            S.op("pool", lambda e, stt=stt: e.tensor_tensor(out=stt[:, 3:4], in0=stt[:, 2:3], in1=mhalf[:], op=ALU.pow),
                 reads=[b_stat[pi], cx.B("mhalf")], writes=[b_stat[pi]])
            for half in range(2):
                S.op("dve", lambda e, half=half, bk=bk, stt=stt, pi=pi: e.scalar_tensor_tensor(
                    out=tmp[pi][:, half * 512:(half + 1) * 512], in0=banks[bk[half]][:], scalar=stt[:, 3:4], in1=GP[:, half * 512:(half + 1) * 512],
                    op0=ALU.mult, op1=ALU.mult), reads=[bank_b[bk[half]], b_stat[pi], b_GP], writes=[b_tmp[pi]])
            S.op("pool", lambda e, xi=xi, pi=pi: e.tensor_tensor(out=xo[xi][:], in0=tmp[pi][:], in1=xt[xi][:], op=ALU.add),
                 reads=[b_tmp[pi], b_xt[xi]], writes=[b_xo[xi]])
            out_toks[xi] = S.dma("sp", lambda e, tb=tb, xi=xi: e.dma_start(out=xo_d[tb * 128:(tb + 1) * 128, :], in_=xo[xi][:]), osem[xi],
                                 reads=[b_xo[xi]])
        return list(out_toks.values())


RET_HEADS = 4


def ret_gamma(h):
    return float(np.float32(1.0) - np.float32(2.0) ** np.float32(-5.0 - h))


def phase_l1(cx, io, nsb=NSB):
    S = cx.S
    x_d, ccol_d = io["x1"], io["ccol"]
    adaw_d, adab_d, pg_d = io["adaw1"][:, 0:2048], io["adab1"][:, 0:2048], io["pre1"]
    banks, bank_b = cx.banks, cx.bank_b
    wq = cx.sb("wq", [128, 8, 512], BF16)
    wk = cx.sb("wk", [128, 8, 512], BF16)
    wv = cx.sb("wv", [128, 8, 1024], BF16)
    wg = cx.sb("wg", [128, 8, 1024], BF16)
    A_t = cx.sb("A_t", [128, D], F32)
    B_t = cx.sb("B_t", [128, D], F32)
    dmt = cx.sb("dmt", [128, 2, 128], F32)
    qd = cx.sb("qd", [128, 2, 512], F32)
    kdg = cx.sb("kdg", [128, 4], F32)
    ident = cx.sb("ident", [128, 128], BF16)
    mhalf = cx.sb("mhalf", [128, 4], F32)
    b_wq, b_wk, b_wv, b_wg, b_A, b_B = Buf(), Buf(), Buf(), Buf(), Buf(), Buf()
    cx.load_const(dmt.rearrange("p a b -> p (a b)"), io["dmt"], cx.B("dmt"))
    cx.load_const(qd.rearrange("p a b -> p (a b)"), io["qd"], cx.B("qd"))
    cx.load_const(kdg, io["kdg"], cx.B("kdg"))
    cx.load_const(ident, io["ident"], cx.B("ident"))
    for (wt, wd, bw) in ((wq, io["wq1_bf"], b_wq), (wk, io["wk1_bf"], b_wk), (wv, io["wv1_bf"], b_wv), (wg, io["wg1_bf"], b_wg)):
        sem = S.new_dma_sem()
        wvv = wd[:, :].rearrange("(dc p) n -> p dc n", p=128)
        for dc0 in range(0, 8, 4):
            S.dma("sp", lambda e, wt=wt, wvv=wvv, dc0=dc0: e.dma_start(out=wt[:, dc0:dc0 + 4, :], in_=wvv[:, dc0:dc0 + 4, :]), sem, writes=[bw])
    S.op("pool", lambda e: e.memset(mhalf, -0.5), writes=[cx.B("mhalf")])

    cx.load_const(A_t, io["adaA1"][:, :], b_A)
    cx.load_const(B_t, io["adaB1"][:, :], b_B)

    NXB = 4
    xbuf = cx.sb("xbuf", [128, NXB, 1024], F32)
    hT = [cx.sb("hT%d" % p, [128, 8, 512], BF16) for p in range(2)]
    hb = [cx.sb("hb%d" % i, [128, 1024], BF16) for i in range(NXB)]
    tmp = cx.sb("tmp", [128, 1024], F32)
    tg = cx.sb("tg", [128, 512], F32)
    qT = [cx.sb("qT%d" % p, [128, 4, 512], BF16) for p in range(2)]
    qdT = [cx.sb("qdT%d" % p, [128, 4, 512], BF16) for p in range(2)]
    kT = [cx.sb("kT%d" % p, [128, 4, 512], BF16) for p in range(2)]
    kd = [cx.sb("kd%d" % i, [128, 512], BF16) for i in range(2)]
    vt = [cx.sb("vt%d" % i, [128, 1024], BF16) for i in range(2)]
    gs = [cx.sb("gs%d" % i, [128, 1024], BF16) for i in range(3)]
    S32 = cx.sb("S32", [128, 2, 2, 512], F32)
    Sbf = cx.sb("Sbf", [128, 2, 2, 512], BF16)
    MT = [cx.sb("MT%d" % i, [128, 128], BF16) for i in range(2)]
    stat = [cx.sb("stat%d" % i, [128, 4], F32) for i in range(NXB)]
    ep = [[cx.sb("ep%d%d" % (i, hl), [128, 4], F32) for hl in range(2)] for i in range(2)]
    sq = cx.sb("sq", [128, 512], F32)
    yb = [[cx.sb("yb%d%d" % (i, hl), [128, 512], BF16) for hl in range(2)] for i in range(2)]
    ystage = [cx.sb("ystage%d" % i, [128, 8, 512], BF16) for i in range(2)]
    b_hT, b_qT, b_qdT, b_kT = [Buf(), Buf()], [Buf(), Buf()], [Buf(), Buf()], [Buf(), Buf()]
    b_tmp, b_tg, b_sq = Buf(), Buf(), Buf()
    b_x, b_hb, b_stat = [Buf() for _ in range(NXB)], [Buf() for _ in range(NXB)], [Buf() for _ in range(NXB)]
    b_kd, b_vt = [Buf(), Buf()], [Buf(), Buf()]
    b_gs = [Buf(), Buf(), Buf()]
    b_S32 = [[Buf(), Buf()], [Buf(), Buf()]]
    b_Sbf = [[Buf(), Buf()], [Buf(), Buf()]]
    b_MT = [Buf(), Buf()]
    b_ep = [[Buf(), Buf()], [Buf(), Buf()]]
    b_yb = [[Buf(), Buf()], [Buf(), Buf()]]
    b_ys, b_ywr = [Buf(), Buf()], [Buf(), Buf()]
    xsem = [S.new_dma_sem() for _ in range(NXB)]
    ysem = [S.new_dma_sem(), S.new_dma_sem()]
    for hl in range(2):
        for dk in range(2):
            S.op("pool", lambda e, hl=hl, dk=dk: e.memset(S32[:, hl, dk, :], 0.0), writes=[b_S32[hl][dk]])
            S.op("pool", lambda e, hl=hl, dk=dk: e.memset(Sbf[:, hl, dk, :], 0.0), writes=[b_Sbf[hl][dk]])

    pb_i = [0]

    PROT = (0, 1, 2, 6, 7)

    def next_pbank():
        b = PROT[pb_i[0] % 5]
        pb_i[0] += 1
        return b

    ob_i = [0]
    ntb = 4 * nsb

    def load_x(tb):
        xi = tb % NXB
        S.dma("sp", lambda e: e.dma_start(out=xbuf[:, xi, :], in_=x_d[tb * 128:(tb + 1) * 128, :]), xsem[xi], writes=[b_x[xi]])

    def chunk_norm(tb):
        def f():
            xi = tb % NXB
            xt = xbuf[:, xi, :]
            stt = stat[xi]
            S.op("dve", lambda e: e.scalar_tensor_tensor(out=tmp, in0=xt, scalar=1.0, in1=xt, op0=ALU.mult, op1=ALU.mult, accum_out=stt[:, 0:1]),
                 reads=[b_x[xi]], writes=[b_tmp, b_stat[xi]])
            S.op("dve", lambda e: e.tensor_scalar(out=stt[:, 1:2], in0=stt[:, 0:1], scalar1=1.0 / D, scalar2=1e-6, op0=ALU.mult, op1=ALU.add),
                 reads=[b_stat[xi]], writes=[b_stat[xi]])
            S.op("pool", lambda e: e.tensor_tensor(out=stt[:, 2:3], in0=stt[:, 1:2], in1=mhalf[:, 0:1], op=ALU.pow),
                 reads=[b_stat[xi], cx.B("mhalf")], writes=[b_stat[xi]])
            S.op("dve", lambda e: e.scalar_tensor_tensor(out=tmp, in0=xt, scalar=stt[:, 2:3], in1=A_t, op0=ALU.mult, op1=ALU.mult),
                 reads=[b_x[xi], b_stat[xi], b_A], writes=[b_tmp])
            S.op("pool", lambda e: e.tensor_tensor(out=hb[xi], in0=tmp, in1=B_t, op=ALU.add), reads=[b_tmp, b_B], writes=[b_hb[xi]])
            if tb + NXB < ntb:
                load_x(tb + NXB)
        return f

    def chunk_tr(tb):
        def f():
            xi, t, p = tb % NXB, tb % 4, (tb // 4) % 2
            pb = next_pbank()
            pbv = banks[pb][:].bitcast(BF16)
            for dc in range(8):
                S.op("pe", lambda e, dc=dc: e.transpose(out=pbv[:, dc * 128:(dc + 1) * 128], in_=hb[xi][:, dc * 128:(dc + 1) * 128], identity=ident),
                     reads=[b_hb[xi], cx.B("ident")], writes=[bank_b[pb]])
            S.op("act", lambda e: e.copy(out=hT[p][:, :, t * 128:(t + 1) * 128], in_=pbv.rearrange("p (dc n) -> p dc n", n=128)),
                 reads=[bank_b[pb]], writes=[b_hT[p]])
        return f

    def chunk_q(sb, ch):
        def f():
            p = sb % 2
            pb = next_pbank()
            for dc in range(8):
                S.op("pe", lambda e, dc=dc: e.matmul(banks[pb][:], lhsT=wq[:, dc, ch * 128:(ch + 1) * 128], rhs=hT[p][:, dc, :], start=(dc == 0), stop=(dc == 7)),
                     reads=[b_wq, b_hT[p]], writes=[bank_b[pb]])
            S.op("act", lambda e: e.copy(out=qT[p][:, ch, :], in_=banks[pb][:]), reads=[bank_b[pb]], writes=[b_qT[p]])
            S.op("dve", lambda e: e.tensor_tensor(out=qdT[p][:, ch, :], in0=banks[pb][:], in1=qd[:, ch // 2, :], op=ALU.mult),
                 reads=[bank_b[pb], cx.B("qd")], writes=[b_qdT[p]])
        return f

    def chunk_k(sb, ch):
        def f():
            p = sb % 2
            pb = next_pbank()
            for dc in range(8):
                S.op("pe", lambda e, dc=dc: e.matmul(banks[pb][:], lhsT=wk[:, dc, ch * 128:(ch + 1) * 128], rhs=hT[p][:, dc, :], start=(dc == 0), stop=(dc == 7)),
                     reads=[b_wk, b_hT[p]], writes=[bank_b[pb]])
            S.op("act", lambda e: e.copy(out=kT[p][:, ch, :], in_=banks[pb][:]), reads=[bank_b[pb]], writes=[b_kT[p]])
        return f

    def sb_chunks(sb):
        tb0 = 4 * sb
        ch = [chunk_norm(tb0 + t) for t in range(4)] + [chunk_tr(tb0 + t) for t in range(4)]
        ch += [chunk_q(sb, c) for c in range(4)] + [chunk_k(sb, c) for c in range(4)]
        return ch

    def proj_tok(tb):
        p, t, bi = (tb // 4) % 2, tb % 4, tb % 2
        gi = tb % 3
        tsl = slice(t * 128, (t + 1) * 128)
        pb = next_pbank()
        pbk = banks[pb][:].bitcast(BF16)
        for ch in range(4):
            S.op("pe", lambda e, ch=ch, pbk=pbk: e.transpose(out=pbk[:, ch * 128:(ch + 1) * 128], in_=kT[p][:, ch, tsl], identity=ident),
                 reads=[b_kT[p], cx.B("ident")], writes=[bank_b[pb]])
        for hl in range(2):
            S.op("dve", lambda e, hl=hl, pbk=pbk: e.tensor_scalar(out=kd[bi][:, hl * 256:(hl + 1) * 256], in0=pbk[:, hl * 256:(hl + 1) * 256],
                                                                  scalar1=kdg[:, hl:hl + 1], scalar2=None, op0=ALU.mult),
                 reads=[bank_b[pb], cx.B("kdg")], writes=[b_kd[bi]])
        for half in range(2):
            pb = next_pbank()
            for dc in range(8):
                S.op("pe", lambda e, dc=dc, pb=pb, half=half: e.matmul(banks[pb][:], lhsT=hT[p][:, dc, tsl], rhs=wv[:, dc, half * 512:(half + 1) * 512],
                                                                         start=(dc == 0), stop=(dc == 7)),
                     reads=[b_wv, b_hT[p]], writes=[bank_b[pb]])
            S.op("act", lambda e, pb=pb, half=half: e.copy(out=vt[bi][:, half * 512:(half + 1) * 512], in_=banks[pb][:]),
                 reads=[bank_b[pb]], writes=[b_vt[bi]])
        for half in range(2):
            pb = next_pbank()
            for dc in range(8):
                S.op("pe", lambda e, dc=dc, pb=pb, half=half: e.matmul(banks[pb][:], lhsT=hT[p][:, dc, tsl], rhs=wg[:, dc, half * 512:(half + 1) * 512],
                                                                         start=(dc == 0), stop=(dc == 7)),
                     reads=[b_wg, b_hT[p]], writes=[bank_b[pb]])
            S.op("act", lambda e, pb=pb: e.activation(out=tg, in_=banks[pb][:], func=AF.Tanh, scale=0.5), reads=[bank_b[pb]], writes=[b_tg])
            S.op("dve", lambda e, pb=pb, half=half: e.scalar_tensor_tensor(out=gs[gi][:, half * 512:(half + 1) * 512], in0=tg, scalar=1.0, in1=banks[pb][:],
                                                                            op0=ALU.add, op1=ALU.mult),
                 reads=[b_tg, bank_b[pb]], writes=[b_gs[gi]])

    def core(tb):
        p, t, bi = (tb // 4) % 2, tb % 4, tb % 2
        tsl = slice(t * 128, (t + 1) * 128)
        obs = []
        for hl in range(2):
            ob = 4 + ob_i[0] % 2
            ob_i[0] += 1
            obs.append(ob)
        vsl = [slice(hl * 512, (hl + 1) * 512) for hl in range(2)]

        def sc(hl):
            sreg = banks[3][:, hl * 128:(hl + 1) * 128]
            for dk in range(2):
                ch = 2 * hl + dk
                S.op("pe", lambda e, ch=ch, dk=dk: e.matmul(sreg, lhsT=kT[p][:, ch, tsl], rhs=qT[p][:, ch, tsl], start=(dk == 0), stop=(dk == 1)),
                     reads=[b_kT[p], b_qT[p]], writes=[bank_b[3]])
            S.op("dve", lambda e: e.tensor_tensor(out=MT[hl], in0=sreg, in1=dmt[:, hl, :], op=ALU.mult), reads=[bank_b[3], cx.B("dmt")], writes=[b_MT[hl]])

        dsb = {}

        def dS(hl):
            for dk in range(2):
                bk = next_pbank()
                dsb[(hl, dk)] = bk
                S.op("pe", lambda e, dk=dk, bk=bk: e.matmul(banks[bk][:], lhsT=kd[bi][:, hl * 256 + dk * 128:hl * 256 + (dk + 1) * 128], rhs=vt[bi][:, vsl[hl]],
                                                             start=True, stop=True),
                     reads=[b_kd[bi], b_vt[bi]], writes=[bank_b[bk]])

        def upd(hl):
            for dk in range(2):
                bk = dsb[(hl, dk)]
                S.op("dve", lambda e, dk=dk, bk=bk: e.scalar_tensor_tensor(out=S32[:, hl, dk, :], in0=S32[:, hl, dk, :], scalar=kdg[:, 2 + hl:3 + hl],
                                                                            in1=banks[bk][:], op0=ALU.mult, op1=ALU.add),
                     reads=[b_S32[hl][dk], cx.B("kdg"), bank_b[bk]], writes=[b_S32[hl][dk]])
                S.op("act", lambda e, dk=dk: e.copy(out=Sbf[:, hl, dk, :], in_=S32[:, hl, dk, :]), reads=[b_S32[hl][dk]], writes=[b_Sbf[hl][dk]])

        def out(hl):
            ob = obs[hl]
            S.op("pe", lambda e: e.matmul(banks[ob][:], lhsT=MT[hl], rhs=vt[bi][:, vsl[hl]], start=True, stop=False),
                 reads=[b_MT[hl], b_vt[bi]], writes=[bank_b[ob]])
            for dk in range(2):
                ch = 2 * hl + dk
                S.op("pe", lambda e, dk=dk, ch=ch: e.matmul(banks[ob][:], lhsT=qdT[p][:, ch, tsl], rhs=Sbf[:, hl, dk, :], start=False, stop=(dk == 1)),
                     reads=[b_qdT[p], b_Sbf[hl][dk]], writes=[bank_b[ob]])

        def epi(hl):
            ob = obs[hl]
            epi_t = ep[bi][hl]
            S.op("act", lambda e: e.activation(out=sq, in_=banks[ob][:], func=AF.Square, accum_out=epi_t[:, 0:1]),
                 reads=[bank_b[ob]], writes=[b_sq, b_ep[bi][hl]])
            S.op("dve", lambda e: e.tensor_scalar(out=epi_t[:, 1:2], in0=epi_t[:, 0:1], scalar1=1.0 / 128, scalar2=4e-5, op0=ALU.mult, op1=ALU.add),
                 reads=[b_ep[bi][hl]], writes=[b_ep[bi][hl]])
            S.op("pool", lambda e: e.tensor_tensor(out=epi_t[:, 2:3], in0=epi_t[:, 1:2], in1=mhalf[:, 0:1], op=ALU.pow),
                 reads=[b_ep[bi][hl], cx.B("mhalf")], writes=[b_ep[bi][hl]])
            S.op("dve", lambda e: e.scalar_tensor_tensor(out=yb[bi][hl], in0=banks[ob][:], scalar=epi_t[:, 2:3], in1=gs[tb % 3][:, vsl[hl]],
                                                          op0=ALU.mult, op1=ALU.mult),
                 reads=[bank_b[ob], b_ep[bi][hl], b_gs[tb % 3]], writes=[b_yb[bi][hl]])

        sc(0)
        dS(0)
        sc(1)
        out(0)
        upd(0)
        epi(0)
        dS(1)
        out(1)
        upd(1)
        epi(1)

    def transposes(tb):
        bi, t, yi = tb % 2, tb % 4, (tb // 4) % 2
        tsl = slice(t * 128, (t + 1) * 128)
        for hl in range(2):
            pb = next_pbank()
            pbv = banks[pb][:].bitcast(BF16)
            for ec in range(4):
                S.op("pe", lambda e, ec=ec, hl=hl, pbv=pbv: e.transpose(out=pbv[:, ec * 128:(ec + 1) * 128], in_=yb[bi][hl][:, ec * 128:(ec + 1) * 128],
                                                                         identity=ident),
                     reads=[b_yb[bi][hl], cx.B("ident")], writes=[bank_b[pb]])
            S.op("act", lambda e, hl=hl, pbv=pbv: e.copy(out=ystage[yi][:, hl * 4:(hl + 1) * 4, tsl], in_=pbv[:, 0:512].rearrange("p (c n) -> p c n", n=128)),
                 reads=[bank_b[pb]], writes=[b_ys[yi]])
        if t == 3:
            sb = tb // 4
            S.dma("sp", lambda e: e.dma_start(
                out=io["yg1_src"][sb // 2][:, :].rearrange("(c p) t -> p c t", p=128)[:, :, (sb % 2) * 512:(sb % 2 + 1) * 512],
                in_=ystage[yi]), ysem[yi], reads=[b_ys[yi]], writes=[b_ywr[yi]])
            if sb % 2 == 1:
                g = sb // 2
                S.dma("pool", lambda e: e.collective_compute("AllGather", ALU.bypass, replica_groups=PAIRS,
                                                             ins=[io["yg1_src"][g].ap().opt()], outs=[io["yg1_all"][g].ap().opt()]),
                      io["ccsem"], reads=[b_ywr[0], b_ywr[1]], writes=[io["b_cc1"][g]], inc=1)

    for tb in range(min(NXB, ntb)):
        load_x(tb)
    for f in sb_chunks(0):
        f()
    if nsb > 1:
        for t in range(4):
            chunk_norm(4 + t)()
    proj_tok(0)

    def sched(sb, t):
        out = []
        n1, n2 = sb + 1, sb + 2
        if n1 < nsb:
            if t == 0:
                out += [chunk_tr(4 * n1), chunk_tr(4 * n1 + 1)]
            elif t == 1:
                out += [chunk_tr(4 * n1 + 2), chunk_tr(4 * n1 + 3), chunk_q(n1, 0), chunk_q(n1, 1)]
            elif t == 2:
                out += [chunk_q(n1, 2), chunk_q(n1, 3), chunk_k(n1, 0)]
            else:
                out += [chunk_k(n1, 1), chunk_k(n1, 2), chunk_k(n1, 3)]
        if n2 < nsb:
            if t == 2:
                out += [chunk_norm(4 * n2), chunk_norm(4 * n2 + 1)]
            elif t == 3:
                out += [chunk_norm(4 * n2 + 2), chunk_norm(4 * n2 + 3)]
        return out

    for tb in range(ntb):
        sb, t = tb // 4, tb % 4
        for f in sched(sb, t):
            f()
        if tb + 1 < ntb:
            proj_tok(tb + 1)
        if tb >= 1:
            transposes(tb - 1)
        core(tb)
    transposes(ntb - 1)


def l1_consts(hh):
    dmt = np.zeros((128, 2, 128), np.float32)
    qd = np.zeros((128, 2, 512), np.float32)
    kdg = np.zeros((128, 4), np.float32)
    j = np.arange(128, dtype=np.float64)[:, None]
    i = np.arange(128, dtype=np.float64)[None, :]
    for hl in range(2):
        lg = math.log(ret_gamma(2 * hh + hl))
        same = (j // 64) == (i // 64)
        later = (i // 64) > (j // 64)
        m = np.where(same, np.exp(lg * np.abs(i - j)), np.where(later, np.exp(lg * (i - j)), 0.0))
        dmt[:, hl, :] = m / 16.0
        qd[:, hl, :] = np.tile(np.exp(lg * np.arange(128, dtype=np.float64)), 4)[None, :]
        kdg[:, hl] = np.exp(lg * (128.0 - np.arange(128, dtype=np.float64))) / 16.0
        kdg[:, 2 + hl] = math.exp(lg * 128.0)
    return dmt.reshape(128, 256), qd.reshape(128, 1024), kdg


def build_fused(nsb=NSB):
    nc = bass.Bass("TRN2", target_bir_lowering=False)
    with ExitStack() as st:
        cx = Ctx(nc, st)
        S = cx.S
        io = {}
        shapes = {
            "x": [S_LEN, D], "ccol": [128, 8],
            "adaw0": [D, 3072], "adab0": [128, 3072], "adaw1": [D, 3072], "adab1": [128, 3072],
            "pre0": [128, D], "post0": [128, D], "pre1": [128, D], "post1": [128, D],
            "wq0": [D, 512], "wk0": [D, 512], "wv0": [D, 512], "wg0": [D, 512],
            "lamv": [128, 256], "subg": [128, 512], "kbtab": [128, 4 * 68], "dg0": [128, 4 * 128], "cbt": [128, 16],
            "wout0": [1024, D],
            "wq1": [D, 512], "wk1": [D, 512], "wv1": [D, 1024], "wg1": [D, 1024],
            "dmt": [128, 256], "qd": [128, 1024], "kdg": [128, 4], "wout1": [2048, D],
        }
        for n, shp in shapes.items():
            io[n] = cx.din(n, shp)
        io["ident"] = cx.din("ident", [128, 128], BF16)
        xo_d = cx.dout("xo", [S_LEN, D])
        io["yg0_src"] = [cx.dscratch("yg0_src%d" % g, [512, 2048], BF16) for g in range(4)]
        io["yg0_all"] = [cx.dscratch("yg0_all%d" % g, [1024, 2048], BF16) for g in range(4)]
        io["yg1_src"] = [cx.dscratch("yg1_src%d" % g, [1024, 1024], BF16) for g in range(8)]
        io["yg1_all"] = [cx.dscratch("yg1_all%d" % g, [2048, 1024], BF16) for g in range(8)]
        io["x1"] = cx.dscratch("x1s", [S_LEN, D], F32)
        for nme in ("wout0", "wq1", "wk1", "wv1", "wg1", "wout1"):
            io[nme + "_bf"] = cx.dscratch(nme + "_bf", shapes[nme], BF16)
        for nme in ("adaA1", "adaB1", "adaG0", "adaG1"):
            io[nme] = cx.dscratch(nme, [128, D], F32)
        io["b_cc0"] = [Buf() for _ in range(4)]
        io["b_cc1"] = [Buf() for _ in range(8)]
        io["ccsem"] = S.new_dma_sem()
        S.barrier_skip.add(io["ccsem"])
        cx.alloc_banks()

        phase_l0(cx, io, nsb)
        cx.new_phase()
        io["op_x0"], io["op_xo0"] = io["x"], io["x1"]
        io["op_ysrc0"] = lambda g4: (io["yg0_all"][g4 // 4][:, :].rearrange("(ec p) t -> p ec t", p=128)[:, :, (g4 % 4) * 512:(g4 % 4 + 1) * 512],
                                     io["b_cc0"][g4 // 4])
        phase_outproj(cx, io, 0, 1024, ntok=512 * nsb)
        cx.new_phase()
        phase_l1(cx, io, nsb)
        cx.new_phase()
        io["op_x1"], io["op_xo1"] = io["x1"], xo_d
        io["op_ysrc1"] = lambda g4: (io["yg1_all"][g4 // 2][:, :].rearrange("(ec p) t -> p ec t", p=128)[:, :, (g4 % 2) * 512:(g4 % 2 + 1) * 512],
                                     io["b_cc1"][g4 // 2])
        toks = phase_outproj(cx, io, 1, 2048, ntok=512 * nsb)
        S.wait_all("sp", toks)
        counts = S.emit(st)
    return nc, counts


_CACHE = {}


def make_in_maps(inp):
    ident = np.eye(128, dtype=np.float32).astype(ml_dtypes.bfloat16)
    w_in0, w_in1 = inp["da_w_in"][0], inp["ret_w_in"][0]
    lamv = np.concatenate([rep128(inp["da_lambda_q1"][0]), rep128(inp["da_lambda_k1"][0]),
                           rep128(inp["da_lambda_q2"][0]), rep128(inp["da_lambda_k2"][0])], axis=1)
    subg = np.ascontiguousarray(np.tile(rep128(inp["da_subln_gain"][0]), (1, 4)))
    wout0_rows = np.concatenate([np.arange(h * 128, (h + 1) * 128) for h in HL[0] + HL[1]])
    shared = {
        "adaw0": np.ascontiguousarray(inp["ada_w"][0]), "adab0": rep128(inp["ada_b"][0]),
        "adaw1": np.ascontiguousarray(inp["ada_w"][1]), "adab1": rep128(inp["ada_b"][1]),
        "pre0": rep128(inp["pre_gain"][0]), "post0": rep128(inp["post_gain"][0]),
        "pre1": rep128(inp["pre_gain"][1]), "post1": rep128(inp["post_gain"][1]),
        "lamv": lamv, "subg": subg, "ident": ident,
        "wout0": np.ascontiguousarray(inp["da_w_out"][0][wout0_rows]),
        "wout1": np.ascontiguousarray(inp["ret_w_out"][0]),
    }
    in_maps = []
    for core in range(8):
        b, hh = core // 2, core % 2
        hc = head_cols(hh, 128)
        kb, dg, cb = l0_consts(hh)
        dmt, qd, kdg = l1_consts(hh)
        qc = np.arange(hh * 512, (hh + 1) * 512)
        vc = np.arange(hh * 1024, (hh + 1) * 1024)
        m = dict(shared)
        m.update({
            "x": np.ascontiguousarray(inp["x"][b], dtype=np.float32),
            "ccol": np.ascontiguousarray(inp["c"][b].reshape(8, 128).T),
            "wq0": np.ascontiguousarray(w_in0[:, 0 + hc]), "wk0": np.ascontiguousarray(w_in0[:, 1024 + hc]),
            "wv0": np.ascontiguousarray(w_in0[:, 2048 + hc]), "wg0": np.ascontiguousarray(w_in0[:, 3072 + hc]),
            "kbtab": kb, "dg0": dg, "cbt": cb,
            "wq1": np.ascontiguousarray(w_in1[:, qc]), "wk1": np.ascontiguousarray(w_in1[:, 1024 + qc]),
            "wv1": np.ascontiguousarray(w_in1[:, 2048 + vc]), "wg1": np.ascontiguousarray(w_in1[:, 4096 + vc]),
            "dmt": dmt, "qd": qd, "kdg": kdg,
        })
        in_maps.append(m)
    return in_maps


def kernel(**inp):
    inp = {k: np.asarray(v) for k, v in inp.items()}
    if "fused" not in _CACHE:
        _CACHE["fused"] = build_fused()
    nc, _ = _CACHE["fused"]
    res = run_bass_kernel_spmd(nc, make_in_maps(inp), core_ids=list(range(8)))
    out = np.empty((4, S_LEN, D), np.float32)
    for b in range(4):
        out[b, :4096] = res.results[2 * b]["xo"][:4096]
        out[b, 4096:] = res.results[2 * b + 1]["xo"][4096:]
    return out
```

```python
import math
from contextlib import ExitStack

import numpy as np
import ml_dtypes

import concourse.bass as bass
import concourse.mybir as mybir
from concourse.bass_utils import run_bass_kernel_spmd

F32 = mybir.dt.float32
BF16 = mybir.dt.bfloat16
ALU = mybir.AluOpType
AF = mybir.ActivationFunctionType
ENGS = ("pe", "act", "dve", "pool", "sp")

S_LEN = 8192
D = 1024
NB = S_LEN // 128
NSB = S_LEN // 512
HL = ((7, 5, 3, 1), (6, 4, 2, 0))
T_SKIP = 40.0
SLOT_SLOPE_MIN = [min(2.0 ** -(HL[0][s] + 1), 2.0 ** -(HL[1][s] + 1)) for s in range(4)]
SLOT_WIDE = [max(2.0 ** -(HL[0][s] + 1), 2.0 ** -(HL[1][s] + 1)) <= 0.125 for s in range(4)]
SLOT_BACK = [min(NB, int((T_SKIP / SLOT_SLOPE_MIN[s] + 127) // 128)) for s in range(4)]
SLOT_RING = [min(NB, ((SLOT_BACK[s] + 4 + 3) // 4) * 4) for s in range(4)]
NEG_BIG = -30000.0
LAM_INIT0 = 0.8 - 0.6 * math.exp(-0.3 * 0)


class Buf:
    __slots__ = ("name", "w", "r", "excl")

    def __init__(self, name="", excl=False):
        self.name = name
        self.w = None
        self.r = []
        self.excl = excl


class Sched:
    def __init__(self, nc):
        self.nc = nc
        self.ops = {e: [] for e in ENGS}
        self.waited = {e: {} for e in ENGS}
        self.dma_sems = []
        self.barrier_skip = set()

    def _deps(self, eng, reads, writes):
        deps = []
        for b in list(reads) + list(writes):
            if b.w is not None:
                deps.append((b.w, "raw"))
        for b in writes:
            for t in b.r:
                deps.append((t, "war"))
        for b in reads:
            if b.excl:
                for t in b.r:
                    deps.append((t, "war"))
        wd = self.waited[eng]
        best = {}
        for t, kind in deps:
            if t[0] == "e" and t[1] == eng and eng == "pe":
                continue
            key = (t[0], t[1])
            if wd.get(key, -1) >= t[2]:
                continue
            if key not in best or best[key][2] < t[2]:
                best[key] = t
        waits = list(best.values())
        for t in waits:
            wd[(t[0], t[1])] = t[2]
            if t[0] == "e":
                self.ops[t[1]][t[2]]["marked"] = True
        return waits

    def op(self, eng, fn, reads=(), writes=()):
        waits = self._deps(eng, reads, writes)
        idx = len(self.ops[eng])
        self.ops[eng].append({"waits": waits, "fn": fn, "marked": False, "dma": None})
        tok = ("e", eng, idx)
        for b in reads:
            b.r.append(tok)
        for b in writes:
            b.w = tok
            b.r = []
        return tok

    def new_dma_sem(self):
        self.dma_sems.append(0)
        return len(self.dma_sems) - 1

    def dma(self, queue, fn, sem, reads=(), writes=(), inc=16):
        waits = self._deps(queue, reads, writes)
        self.dma_sems[sem] += inc
        tok = ("d", sem, self.dma_sems[sem])
        self.ops[queue].append({"waits": waits, "fn": fn, "marked": False, "dma": sem, "inc": inc})
        for b in reads:
            b.r.append(tok)
        for b in writes:
            b.w = tok
            b.r = []
        return tok

    def wait_all(self, eng, toks):
        for t in toks:
            if t[0] == "e":
                self.ops[t[1]][t[2]]["marked"] = True
        self.ops[eng].append({"waits": list(toks), "fn": None, "marked": False, "dma": None})

    def barrier(self):
        last = {}
        for e in ENGS:
            for i in range(len(self.ops[e]) - 1, -1, -1):
                o = self.ops[e][i]
                if o["dma"] is None and o["fn"] is not None:
                    last[e] = ("e", e, i)
                    break
        for e in ENGS:
            toks = [t for k, t in last.items() if k != e]
            toks += [("d", i, v) for i, v in enumerate(self.dma_sems) if v > 0 and i not in self.barrier_skip]
            wd = self.waited[e]
            toks = [t for t in toks if wd.get((t[0], t[1]), -1) < t[2]]
            for t in toks:
                wd[(t[0], t[1])] = t[2]
            self.wait_all(e, toks)

    def emit(self, stack):
        nc = self.nc
        esem = {e: stack.enter_context(nc.semaphore("s_" + e)) for e in ENGS}
        dsem = [stack.enter_context(nc.semaphore("d%d" % i)) for i in range(len(self.dma_sems))]
        pref = {}
        for e in ENGS:
            c = 0
            arr = []
            for o in self.ops[e]:
                if o["marked"]:
                    c += 1
                arr.append(c)
            pref[e] = arr
        block = stack.enter_context(nc.Block())

        def run(e_name):
            def body(e):
                for o in self.ops[e_name]:
                    for t in o["waits"]:
                        if t[0] == "e":
                            e.wait_ge(esem[t[1]], pref[t[1]][t[2]])
                        else:
                            e.wait_ge(dsem[t[1]], t[2])
                    if o["fn"] is None:
                        continue
                    ins = o["fn"](e)
                    if o["dma"] is not None:
                        ins.then_inc(dsem[o["dma"]], o["inc"])
                    elif o["marked"]:
                        ins.then_inc(esem[e_name], 1)
            return body

        block.tensor(run("pe"))
        block.scalar(run("act"))
        block.vector(run("dve"))
        block.gpsimd(run("pool"))
        block.sync(run("sp"))
        return {e: len(self.ops[e]) for e in ENGS}


class Ctx:
    ARENA_F32 = 47 * 1024

    def __init__(self, nc, st):
        self.nc = nc
        self.st = st
        self.S = Sched(nc)
        self.bufs = {}
        self.banks = []
        self.bank_b = []
        self.arena = st.enter_context(nc.sbuf_tensor("arena", [128, self.ARENA_F32], F32))
        self.off = 0
        self.phase = 0

    def new_phase(self):
        self.S.barrier()
        self.off = 0
        self.bufs = {}
        self.phase += 1

    def sb(self, name, shape, dt):
        n = 1
        for d in shape[1:]:
            n *= d
        nbytes = n * (2 if dt == BF16 else 4)
        nf = (nbytes + 31) // 32 * 8
        assert self.off + nf <= self.ARENA_F32, ("SBUF arena overflow", name, self.off, nf)
        ap = self.arena[:, self.off:self.off + nbytes // 4]
        self.off += nf
        if dt == BF16:
            ap = ap.bitcast(BF16)
        if len(shape) == 3:
            ap = ap.rearrange("p (a b) -> p a b", a=shape[1])
        elif len(shape) == 4:
            ap = ap.rearrange("p (a b c) -> p a b c", a=shape[1], b=shape[2])
        return ap

    def alloc_banks(self):
        for i in range(8):
            self.banks.append(self.st.enter_context(self.nc.psum_tensor("bank%d" % i, [128, 512], F32)))
            self.bank_b.append(Buf("bank%d" % i, excl=True))

    def B(self, name):
        if name not in self.bufs:
            self.bufs[name] = Buf(name)
        return self.bufs[name]

    def din(self, name, shape, dt=F32):
        return self.nc.dram_tensor(name, list(shape), dt, kind="ExternalInput").ap()

    def dscratch(self, name, shape, dt=F32):
        return self.nc.dram_tensor(name, list(shape), dt)

    def dout(self, name, shape, dt=F32):
        return self.nc.dram_tensor(name, list(shape), dt, kind="ExternalOutput").ap()

    def load_const(self, tile_ap, dram_ap, buf, queue="sp"):
        sem = self.S.new_dma_sem()
        return self.S.dma(queue, lambda e: e.dma_start(out=tile_ap, in_=dram_ap), sem, writes=[buf])


def emit_adaln(cx, ccol_d, adaw_d, adab_t, adab_b, wchunk, wchunk_b, ncols, consume):
    S = cx.S
    ccol = cx.sb("ccol", [128, 8], F32)
    cth = cx.sb("cth", [128, 8], F32)
    cond = cx.sb("cond", [128, 8], F32)
    crep = cx.sb("crep", [128, 8, 128], F32)
    b_c, b_crep = cx.B("ccol"), cx.B("crep")
    cx.load_const(ccol[:], ccol_d, b_c)
    S.op("act", lambda e: e.activation(out=cth[:], in_=ccol[:], func=AF.Tanh, scale=0.5), reads=[b_c], writes=[cx.B("cth")])
    S.op("dve", lambda e: e.scalar_tensor_tensor(out=cond[:], in0=cth[:], scalar=1.0, in1=ccol[:], op0=ALU.add, op1=ALU.mult),
         reads=[cx.B("cth"), b_c], writes=[cx.B("cond")])
    S.op("dve", lambda e: e.tensor_scalar(out=cond[:], in0=cond[:], scalar1=0.5, scalar2=None, op0=ALU.mult),
         reads=[cx.B("cond")], writes=[cx.B("cond")])
    for j in range(8):
        S.op("dve", lambda e, j=j: e.tensor_copy(out=crep[:, j, :], in_=cond[:, j:j + 1].to_broadcast([128, 128])),
             reads=[cx.B("cond")], writes=[b_crep])
    wsem = S.new_dma_sem()
    adaw_v = adaw_d.rearrange("(dc p) n -> p dc n", p=128)
    for ci in range(ncols // 512):
        S.dma("sp", lambda e, ci=ci: e.dma_start(out=wchunk, in_=adaw_v[:, :, ci * 512:(ci + 1) * 512]), wsem, writes=[wchunk_b])
        bk = ci % 2 + 6
        for j in range(8):
            S.op("pe", lambda e, j=j, bk=bk: e.matmul(cx.banks[bk][:], lhsT=crep[:, j, :], rhs=wchunk[:, j, :],
                                                        start=(j == 0), stop=(j == 7)),
                 reads=[b_crep, wchunk_b], writes=[cx.bank_b[bk]])
        consume(ci, cx.banks[bk], cx.bank_b[bk])


def emit_adaln_all(cx, io, A_t, B_t, b_A, b_B):
    S = cx.S
    banks, bank_b = cx.banks, cx.bank_b
    ccol = cx.sb("ccol", [128, 8], F32)
    cth = cx.sb("cth", [128, 8], F32)
    cond = cx.sb("cond", [128, 8], F32)
    crep = cx.sb("crep", [128, 8, 128], F32)
    adab = cx.sb("adab", [128, 3072], F32)
    g_pre = cx.sb("g_pre", [128, D], F32)
    g_post = cx.sb("g_post", [128, D], F32)
    tga = cx.sb("tga", [128, 512], F32)
    stg = [cx.sb("stg%d" % i, [128, D], F32) for i in range(3)]
    wch = [cx.sb("wch%d" % i, [128, 8, 512], F32) for i in range(2)]
    b_c, b_cth, b_cond, b_crep, b_adab, b_gpre, b_gpost, b_tga = [Buf() for _ in range(8)]
    b_stg = [Buf(), Buf(), Buf()]
    b_wch = [Buf(), Buf()]
    cx.load_const(ccol, io["ccol"], b_c)
    S.op("act", lambda e: e.activation(out=cth, in_=ccol, func=AF.Tanh, scale=0.5), reads=[b_c], writes=[b_cth])
    S.op("dve", lambda e: e.scalar_tensor_tensor(out=cond, in0=cth, scalar=1.0, in1=ccol, op0=ALU.add, op1=ALU.mult),
         reads=[b_cth, b_c], writes=[b_cond])
    S.op("dve", lambda e: e.tensor_scalar(out=cond, in0=cond, scalar1=0.5, scalar2=None, op0=ALU.mult), reads=[b_cond], writes=[b_cond])
    for j in range(8):
        S.op("dve", lambda e, j=j: e.tensor_copy(out=crep[:, j, :], in_=cond[:, j:j + 1].to_broadcast([128, 128])),
             reads=[b_cond], writes=[b_crep])
    wsem = [S.new_dma_sem(), S.new_dma_sem()]
    csem = [S.new_dma_sem() for _ in range(3)]
    ssem = [S.new_dma_sem() for _ in range(3)]
    n = 0
    for layer in range(2):
        S.dma("sp", lambda e, layer=layer: e.dma_start(out=adab, in_=io["adab%d" % layer]), csem[0], writes=[b_adab])
        S.dma("sp", lambda e, layer=layer: e.dma_start(out=g_pre, in_=io["pre%d" % layer]), csem[1], writes=[b_gpre])
        S.dma("sp", lambda e, layer=layer: e.dma_start(out=g_post, in_=io["post%d" % layer]), csem[2], writes=[b_gpost])
        adaw_v = io["adaw%d" % layer].rearrange("(dc p) n -> p dc n", p=128)
        for ci in range(6):
            wi = n % 2
            bk = 6 + n % 2
            n += 1
            S.dma("sp", lambda e, ci=ci, wi=wi, adaw_v=adaw_v: e.dma_start(out=wch[wi], in_=adaw_v[:, :, ci * 512:(ci + 1) * 512]), wsem[wi], writes=[b_wch[wi]])
            for j in range(8):
                S.op("pe", lambda e, j=j, bk=bk, wi=wi: e.matmul(banks[bk][:], lhsT=crep[:, j, :], rhs=wch[wi][:, j, :], start=(j == 0), stop=(j == 7)),
                     reads=[b_crep, b_wch[wi]], writes=[bank_b[bk]])
            kind, half = ci // 2, ci % 2
            cols = slice(half * 512, half * 512 + 512)
            acols = slice(ci * 512, ci * 512 + 512)
            if kind == 0:
                dst, dbuf = (B_t, b_B) if layer == 0 else (stg[1], b_stg[1])
                S.op("dve", lambda e, bk=bk, dst=dst, cols=cols, acols=acols: e.tensor_tensor(out=dst[:, cols], in0=banks[bk][:], in1=adab[:, acols], op=ALU.add),
                     reads=[bank_b[bk], b_adab], writes=[dbuf])
            else:
                S.op("dve", lambda e, bk=bk, acols=acols: e.tensor_tensor(out=tga, in0=banks[bk][:], in1=adab[:, acols], op=ALU.add),
                     reads=[bank_b[bk], b_adab], writes=[b_tga])
                if kind == 1:
                    dst, dbuf = (A_t, b_A) if layer == 0 else (stg[0], b_stg[0])
                    S.op("dve", lambda e, dst=dst, cols=cols: e.scalar_tensor_tensor(out=dst[:, cols], in0=tga, scalar=1.0, in1=g_pre[:, cols], op0=ALU.add, op1=ALU.mult),
                         reads=[b_tga, b_gpre], writes=[dbuf])
                else:
                    S.op("dve", lambda e, cols=cols: e.tensor_tensor(out=stg[2][:, cols], in0=tga, in1=g_post[:, cols], op=ALU.mult),
                         reads=[b_tga, b_gpost], writes=[b_stg[2]])
        if layer == 1:
            S.dma("sp", lambda e: e.dma_start(out=io["adaA1"][:, :], in_=stg[0]), ssem[0], reads=[b_stg[0]])
            S.dma("sp", lambda e: e.dma_start(out=io["adaB1"][:, :], in_=stg[1]), ssem[1], reads=[b_stg[1]])
        S.dma("sp", lambda e, layer=layer: e.dma_start(out=io["adaG%d" % layer][:, :], in_=stg[2]), ssem[2], reads=[b_stg[2]])


PAIRS = [[0, 1], [2, 3], [4, 5], [6, 7]]


def phase_l0(cx, io, nsb=NSB):
    S = cx.S
    x_d, ccol_d = io["x"], io["ccol"]
    adaw_d, adab_d, pg_d = io["adaw0"][:, 0:2048], io["adab0"][:, 0:2048], io["pre0"]
    w_d = [io[n] for n in ("wq0", "wk0", "wv0", "wg0")]
    banks, bank_b = cx.banks, cx.bank_b
    ring = [min(NB, SLOT_RING[s] + 4) for s in range(4)]
    ring_off = [0]
    for s in range(4):
        ring_off.append(ring_off[-1] + ring[s])
    RT = ring_off[-1]
    kT = cx.sb("kT", [128, RT * 128], BF16)
    Va = cx.sb("Va", [128, RT, 130], BF16)
    wbf = [cx.sb("wbf%d" % i, [128, 8, 512], BF16) for i in range(4)]
    A_t = cx.sb("A_t", [128, D], F32)
    B_t = cx.sb("B_t", [128, D], F32)
    kbt = cx.sb("kbt", [128, 4, 68], F32)
    dg0 = cx.sb("dg0", [128, 4, 128], F32)
    cbt = cx.sb("cbt", [128, 4, 4], F32)
    ident = cx.sb("ident", [128, 128], BF16)
    lsc = cx.sb("lsc", [128, 8], F32)
    subg = cx.sb("subg", [128, 512], F32)
    mhalf = cx.sb("mhalf", [128, 4], F32)
    b_w = [Buf() for _ in range(4)]
    b_A, b_B, b_lam, b_ones = Buf(), Buf(), Buf(), Buf()
    cx.load_const(kbt.rearrange("p a b -> p (a b)"), io["kbtab"], cx.B("kbt"))
    cx.load_const(dg0.rearrange("p a b -> p (a b)"), io["dg0"], cx.B("dg0"))
    cx.load_const(cbt.rearrange("p a b -> p (a b)"), io["cbt"], cx.B("cbt"))
    cx.load_const(ident, io["ident"], cx.B("ident"))
    cx.load_const(subg, io["subg"], cx.B("subg"))
    for i in range(4):
        sem = S.new_dma_sem()
        wv_ = w_d[i].rearrange("(dc p) n -> p dc n", p=128)
        S.dma("pool", lambda e, i=i, wv_=wv_: e.dma_start(out=wbf[i], in_=wv_), sem, writes=[b_w[i]])
    S.op("pool", lambda e: e.memset(mhalf, -0.5), writes=[cx.B("mhalf")])
    S.op("pool", lambda e: e.memset(Va[:, :, 128:130], 1.0), writes=[b_ones])
    S.op("dve", lambda e: e.tensor_scalar(out=subg, in0=subg, scalar1=0.5 * (1.0 - LAM_INIT0), scalar2=None, op0=ALU.mult),
         reads=[cx.B("subg")], writes=[cx.B("subg")])

    off0 = cx.off
    lamv = cx.sb("lamv", [128, 256], F32)
    ljunk = cx.sb("ljunk", [128, 64], F32)
    cx.load_const(lamv, io["lamv"], cx.B("lamv"))
    b_lj = Buf()
    S.op("dve", lambda e: e.scalar_tensor_tensor(out=ljunk, in0=lamv[:, 0:64], scalar=1.0, in1=lamv[:, 64:128],
                                                  op0=ALU.mult, op1=ALU.mult, accum_out=lsc[:, 0:1]),
         reads=[cx.B("lamv")], writes=[b_lj, b_lam])
    S.op("dve", lambda e: e.scalar_tensor_tensor(out=ljunk, in0=lamv[:, 128:192], scalar=1.0, in1=lamv[:, 192:256],
                                                  op0=ALU.mult, op1=ALU.mult, accum_out=lsc[:, 1:2]),
         reads=[cx.B("lamv")], writes=[b_lj, b_lam])
    S.op("act", lambda e: e.activation(out=lsc[:, 2:4], in_=lsc[:, 0:2], func=AF.Exp), reads=[b_lam], writes=[b_lam])
    S.op("dve", lambda e: e.tensor_tensor(out=lsc[:, 4:5], in0=lsc[:, 2:3], in1=lsc[:, 3:4], op=ALU.subtract),
         reads=[b_lam], writes=[b_lam])
    S.op("dve", lambda e: e.tensor_scalar(out=lsc[:, 5:6], in0=lsc[:, 4:5], scalar1=LAM_INIT0, scalar2=-1.0,
                                           op0=ALU.add, op1=ALU.mult), reads=[b_lam], writes=[b_lam])
    neglam = lsc[:, 5:6]

    emit_adaln_all(cx, io, A_t, B_t, b_A, b_B)
    S.barrier()
    cx.off = off0

    NXB = 4
    xbuf = cx.sb("xbuf", [128, NXB, 1024], F32)
    hT = cx.sb("hT", [128, 8, 512], BF16)
    hb = [cx.sb("hb%d" % i, [128, 1024], BF16) for i in range(NXB)]
    tmp = cx.sb("tmp", [128, 1024], F32)
    qT = [[cx.sb("qT%d%d" % (p, c), [128, 4, 512], BF16) for c in range(2)] for p in range(2)]
    gs = [cx.sb("gs%d" % p, [128, 4, 512], BF16) for p in range(2)]
    tg = cx.sb("tg", [128, 512], F32)
    tg2 = cx.sb("tg2", [128, 512], F32)
    PT = [cx.sb("PT%d" % i, [128, 512], BF16) for i in range(3)]
    dtmp = [cx.sb("dtmp%d" % i, [128, 128], F32) for i in range(2)]
    stat = [cx.sb("stat%d" % i, [128, 4], F32) for i in range(NXB)]
    o1_t = cx.sb("o1_t", [128, 4, 128], F32)
    o_t = cx.sb("o_t", [128, 4, 128], F32)
    sqj = cx.sb("sqj", [128, 128], F32)
    ep = cx.sb("ep", [128, 24], F32)
    yb = [cx.sb("yb%d" % i, [128, 4, 128], BF16) for i in range(2)]
    ystage = [cx.sb("ystage%d" % i, [128, 512], BF16) for i in range(2)]
    b_x = [Buf() for _ in range(NXB)]
    b_hT, b_tmp, b_tg, b_tg2 = Buf(), Buf(), Buf(), Buf()
    b_hb = [Buf() for _ in range(NXB)]
    b_qT = [Buf(), Buf()]
    b_gs = [Buf(), Buf()]
    b_PT = [Buf(), Buf(), Buf()]
    b_dtmp = [Buf(), Buf()]
    b_stat = [Buf() for _ in range(NXB)]
    b_kring = [[Buf() for _ in range(ring[s] // 4)] for s in range(4)]
    b_vring = [[Buf() for _ in range(ring[s] // 4)] for s in range(4)]
    b_o1, b_o, b_sqj, b_ep0, b_ep1 = Buf(), Buf(), Buf(), Buf(), Buf()
    b_yb = [Buf(), Buf()]
    deferred = []
    b_ys = [Buf(), Buf()]
    b_ywr = [Buf(), Buf()]
    xsem = [S.new_dma_sem() for _ in range(NXB)]
    ysem = [S.new_dma_sem(), S.new_dma_sem()]
    for p in range(2):
        S.op("pool", lambda e, p=p: e.memset(qT[p][0][64:128, :, :], 0.0), writes=[b_qT[p]])
        S.op("pool", lambda e, p=p: e.memset(qT[p][1][0:64, :, :], 0.0), writes=[b_qT[p]])

    def ring_col(s, j):
        return (ring_off[s] + (j % ring[s])) * 128

    def ring_blk(s, j):
        return ring_off[s] + (j % ring[s])

    def ring_grp(s, j):
        return (j // 4) % (ring[s] // 4)

    pb_i = [0]

    def next_pbank():
        return 7

    ST_BANKS = (0, 1, 6)
    ntb = 4 * nsb

    def load_x(tb):
        xi = tb % NXB
        S.dma("sp", lambda e: e.dma_start(out=xbuf[:, xi, :], in_=x_d[tb * 128:(tb + 1) * 128, :]), xsem[xi], writes=[b_x[xi]])

    def chunk_norm_a(tb):
        def f():
            xi = tb % NXB
            xt = xbuf[:, xi, :]
            stt = stat[xi]
            S.op("dve", lambda e: e.scalar_tensor_tensor(out=tmp, in0=xt, scalar=1.0, in1=xt, op0=ALU.mult, op1=ALU.mult, accum_out=stt[:, 0:1]),
                 reads=[b_x[xi]], writes=[b_tmp, b_stat[xi]])
            S.op("dve", lambda e: e.tensor_scalar(out=stt[:, 1:2], in0=stt[:, 0:1], scalar1=1.0 / D, scalar2=1e-6, op0=ALU.mult, op1=ALU.add),
                 reads=[b_stat[xi]], writes=[b_stat[xi]])
            S.op("pool", lambda e: e.tensor_tensor(out=stt[:, 2:3], in0=stt[:, 1:2], in1=mhalf[:, 0:1], op=ALU.pow),
                 reads=[b_stat[xi], cx.B("mhalf")], writes=[b_stat[xi]])
        return f

    def chunk_norm(tb):
        def f():
            xi = tb % NXB
            xt = xbuf[:, xi, :]
            stt = stat[xi]
            S.op("dve", lambda e: e.scalar_tensor_tensor(out=tmp, in0=xt, scalar=stt[:, 2:3], in1=A_t, op0=ALU.mult, op1=ALU.mult),
                 reads=[b_x[xi], b_stat[xi], b_A], writes=[b_tmp])
            S.op("pool", lambda e: e.tensor_tensor(out=hb[xi], in0=tmp, in1=B_t, op=ALU.add), reads=[b_tmp, b_B], writes=[b_hb[xi]])
            if tb + NXB < ntb:
                load_x(tb + NXB)
        return f

    def evac(sb, out_ap, in_ap, reads, writes):
        if sb < 8:
            S.op("act", lambda e: e.copy(out=out_ap, in_=in_ap), reads=reads, writes=writes)
        else:
            S.op("dve", lambda e: e.tensor_copy(out=out_ap, in_=in_ap), reads=reads, writes=writes)

    def chunk_tr(tb):
        def f():
            xi, t = tb % NXB, tb % 4
            pb = next_pbank()
            pbv = banks[pb][:].bitcast(BF16)
            for dc in range(8):
                S.op("pe", lambda e, dc=dc: e.transpose(out=pbv[:, dc * 128:(dc + 1) * 128], in_=hb[xi][:, dc * 128:(dc + 1) * 128], identity=ident),
                     reads=[b_hb[xi], cx.B("ident")], writes=[bank_b[pb]])
            evac(tb // 4, hT[:, :, t * 128:(t + 1) * 128], pbv.rearrange("p (dc n) -> p dc n", n=128), [bank_b[pb]], [b_hT])
        return f

    def chunk_q(sb, s):
        def f():
            p = sb % 2
            pb = next_pbank()
            for dc in range(8):
                S.op("pe", lambda e, dc=dc: e.matmul(banks[pb][:], lhsT=wbf[0][:, dc, s * 128:(s + 1) * 128], rhs=hT[:, dc, :], start=(dc == 0), stop=(dc == 7)),
                     reads=[b_w[0], b_hT], writes=[bank_b[pb]])
            evac(sb, qT[p][0][0:64, s, :], banks[pb][0:64, :], [bank_b[pb]], [b_qT[p]])
            evac(sb, qT[p][1][64:128, s, :], banks[pb][64:128, :], [bank_b[pb]], [b_qT[p]])
        return f

    def chunk_k(sb, s):
        def f():
            pb = next_pbank()
            for dc in range(8):
                S.op("pe", lambda e, dc=dc: e.matmul(banks[pb][:], lhsT=wbf[1][:, dc, s * 128:(s + 1) * 128], rhs=hT[:, dc, :], start=(dc == 0), stop=(dc == 7)),
                     reads=[b_w[1], b_hT], writes=[bank_b[pb]])
            c0 = ring_col(s, 4 * sb)
            evac(sb, kT[:, c0:c0 + 512], banks[pb][:], [bank_b[pb]], [b_kring[s][ring_grp(s, 4 * sb)]])
        return f

    def chunk_v(sb, t):
        def f():
            pb = next_pbank()
            for dc in range(8):
                S.op("pe", lambda e, dc=dc: e.matmul(banks[pb][:], lhsT=hT[:, dc, t * 128:(t + 1) * 128], rhs=wbf[2][:, dc, :], start=(dc == 0), stop=(dc == 7)),
                     reads=[b_w[2], b_hT], writes=[bank_b[pb]])
            for s in range(4):
                rb = ring_blk(s, 4 * sb + t)
                evac(sb, Va[:, rb, 0:128], banks[pb][:, s * 128:(s + 1) * 128], [bank_b[pb]], [b_vring[s][ring_grp(s, 4 * sb)]])
        return f

    def chunk_g(sb, t):
        def f():
            p = sb % 2
            pb = next_pbank()
            for dc in range(8):
                S.op("pe", lambda e, dc=dc: e.matmul(banks[pb][:], lhsT=hT[:, dc, t * 128:(t + 1) * 128], rhs=wbf[3][:, dc, :], start=(dc == 0), stop=(dc == 7)),
                     reads=[b_w[3], b_hT], writes=[bank_b[pb]])
            S.op("act", lambda e: e.activation(out=tg, in_=banks[pb][:], func=AF.Tanh, scale=0.5), reads=[bank_b[pb]], writes=[b_tg])
            S.op("dve", lambda e: e.scalar_tensor_tensor(out=tg2, in0=tg, scalar=1.0, in1=banks[pb][:], op0=ALU.add, op1=ALU.mult),
                 reads=[b_tg, bank_b[pb]], writes=[b_tg2])
            S.op("pool", lambda e: e.tensor_tensor(out=gs[p][:, t, :], in0=tg2, in1=subg, op=ALU.mult), reads=[b_tg2, cx.B("subg")], writes=[b_gs[p]])
        return f

    def proj_chunks(sb):
        tb0 = 4 * sb
        ch = [chunk_norm_a(tb0), chunk_norm_a(tb0 + 1), chunk_norm(tb0), chunk_norm_a(tb0 + 2), chunk_norm(tb0 + 1), chunk_norm_a(tb0 + 3),
              chunk_norm(tb0 + 2), chunk_norm(tb0 + 3)] + [chunk_tr(tb0 + t) for t in range(4)]
        ch += [chunk_k(sb, s) for s in range(4)] + [chunk_v(sb, t) for t in range(4)]
        ch += [chunk_q(sb, s) for s in range(4)] + [chunk_g(sb, t) for t in range(4)]
        return ch

    st_i, pt_i, dt_i = [0], [0], [0]
    accs = [(2, 3), (4, 5)]

    def acc_ap(c, r, lo, hi):
        return banks[accs[c][r // 2]][:, (r % 2) * 130 + lo:(r % 2) * 130 + hi]

    def emit_qk(tile):
        sb, s, c, idx, j = tile
        p = sb % 2
        r0 = max(0, j - 4 * sb)
        sti = ST_BANKS[st_i[0] % 3]
        st_i[0] += 1
        tile.append(sti)
        kc = ring_col(s, j)
        S.op("pe", lambda e: e.matmul(banks[sti][:, r0 * 128:512], lhsT=kT[:, kc:kc + 128], rhs=qT[p][c][:, s, r0 * 128:512], start=True, stop=True),
             reads=[b_kring[s][ring_grp(s, j)], b_qT[p]], writes=[bank_b[sti]])

    def emit_exp_pv(tile):
        sb, s, c, idx, j, sti = tile
        wide = SLOT_WIDE[s]
        r0 = max(0, j - 4 * sb)
        ST = banks[sti]
        pti = pt_i[0] % 3
        pt_i[0] += 1
        P = PT[pti]
        rstart = r0
        if j >= 4 * sb:
            di = dt_i[0] % 2
            dt_i[0] += 1
            S.op("dve", lambda e: e.scalar_tensor_tensor(out=dtmp[di], in0=ST[:, r0 * 128:(r0 + 1) * 128], scalar=0.125, in1=dg0[:, s, :], op0=ALU.mult, op1=ALU.add),
                 reads=[bank_b[sti], cx.B("dg0")], writes=[b_dtmp[di]])
            S.op("act", lambda e: e.activation(out=P[:, r0 * 128:(r0 + 1) * 128], in_=dtmp[di], func=AF.Exp, bias=cbt[:, s, r0:r0 + 1], scale=1.0),
                 reads=[b_dtmp[di], cx.B("cbt")], writes=[b_PT[pti]])
            rstart = r0 + 1
        if rstart < 4:
            if wide:
                ti = 4 * sb + 3 - j
                S.op("act", lambda e: e.activation(out=P[:, rstart * 128:512], in_=ST[:, rstart * 128:512], func=AF.Exp, bias=kbt[:, s, ti:ti + 1], scale=0.125),
                     reads=[bank_b[sti], cx.B("kbt")], writes=[b_PT[pti]])
            else:
                for r in range(rstart, 4):
                    ti = 4 * sb + r - j
                    S.op("act", lambda e, r=r, ti=ti: e.activation(out=P[:, r * 128:(r + 1) * 128], in_=ST[:, r * 128:(r + 1) * 128], func=AF.Exp,
                                                                     bias=kbt[:, s, ti:ti + 1], scale=0.125),
                         reads=[bank_b[sti], cx.B("kbt")], writes=[b_PT[pti]])
        rb = ring_blk(s, j)
        for r in range(r0, 4):
            S.op("pe", lambda e, r=r: e.matmul(acc_ap(c, r, 0, 130), lhsT=P[:, r * 128:(r + 1) * 128], rhs=Va[:, rb, :],
                                                start=(idx == 0 and r % 2 == 0), stop=True, skip_group_check=True),
                 reads=[b_PT[pti], b_vring[s][ring_grp(s, j)], b_ones], writes=[bank_b[accs[c][r // 2]]])

    def epi_c0(sb, s):
        for hb2 in range(2):
            bk = accs[0][hb2]
            S.op("dve", lambda e, bk=bk, hb2=hb2: e.reciprocal(out=ep[:, 2 * hb2:2 * hb2 + 2], in_=banks[bk][:, 128:259:130]),
                 reads=[bank_b[bk]], writes=[b_ep0])
        for r in range(4):
            S.op("dve", lambda e, r=r: e.tensor_scalar(out=o1_t[:, r, :], in0=acc_ap(0, r, 0, 128), scalar1=ep[:, r:r + 1], scalar2=None, op0=ALU.mult),
                 reads=[bank_b[accs[0][r // 2]], b_ep0], writes=[b_o1])

    def epi_c1(sb, s):
        p = sb % 2
        for hb2 in range(2):
            bk = accs[1][hb2]
            S.op("dve", lambda e, bk=bk, hb2=hb2: e.reciprocal(out=ep[:, 4 + 2 * hb2:4 + 2 * hb2 + 2], in_=banks[bk][:, 128:259:130]),
                 reads=[bank_b[bk]], writes=[b_ep1])
        S.op("dve", lambda e: e.tensor_scalar(out=ep[:, 8:12], in0=ep[:, 4:8], scalar1=neglam, scalar2=None, op0=ALU.mult),
             reads=[b_ep1, b_lam], writes=[b_ep1])
        for r in range(4):
            S.op("dve", lambda e, r=r: e.scalar_tensor_tensor(out=o_t[:, r, :], in0=acc_ap(1, r, 0, 128), scalar=ep[:, 8 + r:9 + r], in1=o1_t[:, r, :],
                                                               op0=ALU.mult, op1=ALU.add),
                 reads=[bank_b[accs[1][r // 2]], b_ep1, b_o1], writes=[b_o])
        for r in range(4):
            S.op("dve", lambda e, r=r: e.scalar_tensor_tensor(out=sqj, in0=o_t[:, r, :], scalar=1.0, in1=o_t[:, r, :], op0=ALU.mult, op1=ALU.mult,
                                                               accum_out=ep[:, 12 + r:13 + r]),
                 reads=[b_o], writes=[b_sqj, b_ep1])
        S.op("dve", lambda e: e.tensor_scalar(out=ep[:, 12:16], in0=ep[:, 12:16], scalar1=1.0 / 128, scalar2=1e-5, op0=ALU.mult, op1=ALU.add),
             reads=[b_ep1], writes=[b_ep1])
        S.op("pool", lambda e: e.tensor_tensor(out=ep[:, 16:20], in0=ep[:, 12:16], in1=mhalf[:, 0:4], op=ALU.pow),
             reads=[b_ep1, cx.B("mhalf")], writes=[b_ep1])
        yi = (sb * 4 + s) % 2
        ybt = yb[yi]

        def part_b():
            for r in range(4):
                S.op("dve", lambda e, r=r: e.scalar_tensor_tensor(out=ybt[:, r, :], in0=o_t[:, r, :], scalar=ep[:, 16 + r:17 + r],
                                                                   in1=gs[p][:, r, s * 128:(s + 1) * 128], op0=ALU.mult, op1=ALU.mult),
                     reads=[b_o, b_ep1, b_gs[p]], writes=[b_yb[yi]])
        deferred.append([4, part_b])

        def finish():
            pb = next_pbank()
            pbv = banks[pb][:].bitcast(BF16)
            for r in range(4):
                S.op("pe", lambda e, r=r: e.transpose(out=pbv[:, r * 128:(r + 1) * 128], in_=ybt[:, r, :], identity=ident),
                     reads=[b_yb[yi], cx.B("ident")], writes=[bank_b[pb]])
            S.op("dve", lambda e: e.tensor_copy(out=ystage[yi], in_=pbv[:, 0:512]), reads=[bank_b[pb]], writes=[b_ys[yi]])
            S.dma("sp", lambda e: e.dma_start(out=io["yg0_src"][sb // 4][s * 128:(s + 1) * 128, (sb % 4) * 512:(sb % 4 + 1) * 512], in_=ystage[yi]),
                  ysem[yi], reads=[b_ys[yi]], writes=[b_ywr[yi]])
            if sb % 4 == 3 and s == 3:
                g = sb // 4
                S.dma("pool", lambda e: e.collective_compute("AllGather", ALU.bypass, replica_groups=PAIRS,
                                                             ins=[io["yg0_src"][g].ap().opt()], outs=[io["yg0_all"][g].ap().opt()]),
                      io["ccsem"], reads=[b_ywr[0], b_ywr[1]], writes=[io["b_cc0"][g]], inc=1)
        deferred.append([12, finish])

    def tiles_of(sb):
        tl = []
        for s in range(4):
            jmin = max(0, 4 * sb - SLOT_BACK[s])
            for c in range(2):
                for idx, j in enumerate(range(jmin, 4 * sb + 4)):
                    tl.append([sb, s, c, idx, j])
        return tl

    for tb in range(min(NXB, ntb)):
        load_x(tb)
    for f in proj_chunks(0):
        f()
    precast = ["wout0", "wq1", "wk1", "wv1", "wg1", "wout1"]
    for sb in range(nsb):
        if (sb >= 5 or sb == nsb - 1) and precast:
            for nme in ([precast.pop(0)] if sb < nsb - 1 else list(precast)):
                sem = S.new_dma_sem()
                S.dma("pool", lambda e, nme=nme: e.dma_start(out=io[nme + "_bf"][:, :], in_=io[nme]), sem)
            if sb == nsb - 1:
                precast = []
        tiles = tiles_of(sb)
        chunks = proj_chunks(sb + 1) if sb + 1 < nsb else []
        nt = len(tiles)
        emit_qk(tiles[0])
        if nt > 1:
            emit_qk(tiles[1])
        done_chunks = 0
        for i, tile in enumerate(tiles):
            if i + 2 < nt:
                emit_qk(tiles[i + 2])
            emit_exp_pv(tile)
            last_of_group = (i + 1 == nt) or (tiles[i + 1][1] != tile[1]) or (tiles[i + 1][2] != tile[2])
            if last_of_group:
                if tile[2] == 0:
                    epi_c0(sb, tile[1])
                else:
                    epi_c1(sb, tile[1])
            for dfr in deferred:
                dfr[0] -= 1
            deferred.sort(key=lambda d: d[0])
            while deferred and deferred[0][0] <= 0:
                deferred.pop(0)[1]()
            want = (len(chunks) * (i + 1)) // nt
            while done_chunks < want:
                chunks[done_chunks]()
                done_chunks += 1
    deferred.sort(key=lambda d: d[0])
    while deferred:
        deferred.pop(0)[1]()


def l0_consts(hh):
    kb = np.zeros((128, 4, 68), np.float32)
    dg = np.zeros((128, 4, 128), np.float32)
    cb = np.zeros((128, 4, 4), np.float32)
    ki = np.arange(128, dtype=np.float64)[:, None]
    qi = np.arange(128, dtype=np.float64)[None, :]
    allowed = (ki // 64) <= (qi // 64)
    for s in range(4):
        slope = 2.0 ** -(HL[hh][s] + 1)
        t = np.arange(68, dtype=np.float64)[None, :]
        if SLOT_WIDE[s]:
            kb[:, s, :] = slope * (128.0 * (3 - t) + ki - 256.0)
            for r in range(4):
                cb[:, s, r] = slope * (128.0 * r - 192.0)
        else:
            kb[:, s, :] = slope * (-128.0 * t + ki - 64.0)
        d = -slope * np.abs(qi - ki) + slope * (qi - 64.0)
        dg[:, s, :] = np.where(allowed, d, NEG_BIG)
    return kb.reshape(128, 4 * 68), dg.reshape(128, 4 * 128), cb.reshape(128, 16)


def rep128(v):
    return np.ascontiguousarray(np.broadcast_to(np.asarray(v, np.float32).reshape(1, -1), (128, v.size)))


def head_cols(hh, width):
    return np.concatenate([np.arange(h * width, (h + 1) * width) for h in HL[hh]])


def phase_outproj(cx, io, layer, E, ntok=S_LEN):
    EC = E // 128
    if True:
        S = cx.S
        ccol_d = io["ccol"]
        adaw_d, adab_d, pg_d = io["adaw%d" % layer][:, 2048:3072], io["adab%d" % layer][:, 2048:3072], io["post%d" % layer]
        w_d = io["wout%d_bf" % layer][:, :]
        x_d, xo_d = io["op_x%d" % layer], io["op_xo%d" % layer]
        ysrc = io["op_ysrc%d" % layer]
        banks, bank_b = cx.banks, cx.bank_b
        wbf = cx.sb("wbf", [128, EC, D], BF16)
        GP = cx.sb("GP", [128, D], F32)
        NX = 4
        yT = [cx.sb("yT%d" % i, [128, EC, 512], BF16) for i in range(2)]
        xt = [cx.sb("xt%d" % i, [128, D], F32) for i in range(NX)]
        tmp = [cx.sb("tmp%d" % i, [128, D], F32) for i in range(2)]
        sq = cx.sb("sq", [128, D], F32)
        b_sq = Buf()
        xo = [cx.sb("xo%d" % i, [128, D], F32) for i in range(NX)]
        stat = [cx.sb("stat%d" % i, [128, 4], F32) for i in range(2)]
        mhalf = cx.sb("mhalf", [128, 1], F32)
        b_w, b_GP, b_tg = Buf(), Buf(), Buf()
        b_tmp = [Buf(), Buf()]
        b_yT, b_stat = [Buf(), Buf()], [Buf(), Buf()]
        b_xt, b_xo = [Buf() for _ in range(NX)], [Buf() for _ in range(NX)]
        cx.load_const(GP, io["adaG%d" % layer][:, :], b_GP)
        wsem = S.new_dma_sem()
        wv = w_d.rearrange("(ec p) n -> p ec n", p=128)
        for ec0 in range(0, EC, 4):
            S.dma("sp", lambda e, ec0=ec0: e.dma_start(out=wbf[:, ec0:ec0 + 4, :], in_=wv[:, ec0:ec0 + 4, :]), wsem, writes=[b_w])
        S.op("pool", lambda e: e.memset(mhalf[:], -0.5), writes=[cx.B("mhalf")])

        ysem = [S.new_dma_sem(), S.new_dma_sem()]
        xsem = [S.new_dma_sem() for _ in range(NX)]
        osem = [S.new_dma_sem() for _ in range(NX)]
        out_toks = {}
        nblk = ntok // 128

        def emit_loads(tb):
            if tb >= nblk:
                return
            if tb % 4 == 0:
                g4 = tb // 4
                yi = g4 % 2
                y_ap, y_buf = ysrc(g4)
                S.dma("sp", lambda e: e.dma_start(out=yT[yi][:], in_=y_ap), ysem[yi], reads=[y_buf], writes=[b_yT[yi]])
            xi = tb % NX
            S.dma("sp", lambda e: e.dma_start(out=xt[xi][:], in_=x_d[tb * 128:(tb + 1) * 128, :]), xsem[xi], writes=[b_xt[xi]])

        for tb in range(3):
            emit_loads(tb)
        for tb in range(nblk):
            emit_loads(tb + 3)
            g4, t = tb // 4, tb % 4
            yi = g4 % 2
            xi = tb % NX
            pi = tb % 2
            bk = (4 * pi, 4 * pi + 1)
            for half in range(2):
                for ec in range(EC):
                    S.op("pe", lambda e, half=half, ec=ec, yi=yi, t=t, bk=bk: e.matmul(
                        banks[bk[half]][:], lhsT=yT[yi][:, ec, t * 128:(t + 1) * 128], rhs=wbf[:, ec, half * 512:(half + 1) * 512],
                        start=(ec == 0), stop=(ec == EC - 1)), reads=[b_yT[yi], b_w], writes=[bank_b[bk[half]]])
            stt = stat[pi]
            for half in range(2):
                S.op("act", lambda e, half=half, bk=bk, stt=stt: e.activation(
                    out=sq[:, half * 512:(half + 1) * 512], in_=banks[bk[half]][:], func=AF.Square,
                    accum_out=stt[:, half:half + 1]), reads=[bank_b[bk[half]]], writes=[b_sq, b_stat[pi]])
            S.op("dve", lambda e, stt=stt: e.tensor_tensor(out=stt[:, 2:3], in0=stt[:, 0:1], in1=stt[:, 1:2], op=ALU.add),
                 reads=[b_stat[pi]], writes=[b_stat[pi]])
            S.op("dve", lambda e, stt=stt: e.tensor_scalar(out=stt[:, 2:3], in0=stt[:, 2:3], scalar1=1.0 / D, scalar2=1e-6, op0=ALU.mult, op1=ALU.add),
                 reads=[b_stat[pi]], writes=[b_stat[pi]])
            S.op("pool", lambda e, stt=stt: e.tensor_tensor(out=stt[:, 3:4], in0=stt[:, 2:3], in1=mhalf[:], op=ALU.pow),
                 reads=[b_stat[pi], cx.B("mhalf")], writes=[b_stat[pi]])
            for half in range(2):
                S.op("dve", lambda e, half=half, bk=bk, stt=stt, pi=pi: e.scalar_tensor_tensor(
                    out=tmp[pi][:, half * 512:(half + 1) * 512], in0=banks[bk[half]][:], scalar=stt[:, 3:4], in1=GP[:, half * 512:(half + 1) * 512],
                    op0=ALU.mult, op1=ALU.mult), reads=[bank_b[bk[half]], b_stat[pi], b_GP], writes=[b_tmp[pi]])
            S.op("pool", lambda e, xi=xi, pi=pi: e.tensor_tensor(out=xo[xi][:], in0=tmp[pi][:], in1=xt[xi][:], op=ALU.add),
                 reads=[b_tmp[pi], b_xt[xi]], writes=[b_xo[xi]])
            out_toks[xi] = S.dma("sp", lambda e, tb=tb, xi=xi: e.dma_start(out=xo_d[tb * 128:(tb + 1) * 128, :], in_=xo[xi][:]), osem[xi],
                                 reads=[b_xo[xi]])
        return list(out_toks.values())


RET_HEADS = 4


def ret_gamma(h):
    return float(np.float32(1.0) - np.float32(2.0) ** np.float32(-5.0 - h))


def phase_l1(cx, io, nsb=NSB):
    S = cx.S
    x_d, ccol_d = io["x1"], io["ccol"]
    adaw_d, adab_d, pg_d = io["adaw1"][:, 0:2048], io["adab1"][:, 0:2048], io["pre1"]
    banks, bank_b = cx.banks, cx.bank_b
    wq = cx.sb("wq", [128, 8, 512], BF16)
    wk = cx.sb("wk", [128, 8, 512], BF16)
    wv = cx.sb("wv", [128, 8, 1024], BF16)
    wg = cx.sb("wg", [128, 8, 1024], BF16)
    A_t = cx.sb("A_t", [128, D], F32)
    B_t = cx.sb("B_t", [128, D], F32)
    dmt = cx.sb("dmt", [128, 2, 128], F32)
    qd = cx.sb("qd", [128, 2, 512], F32)
    kdg = cx.sb("kdg", [128, 4], F32)
    ident = cx.sb("ident", [128, 128], BF16)
    mhalf = cx.sb("mhalf", [128, 4], F32)
    b_wq, b_wk, b_wv, b_wg, b_A, b_B = Buf(), Buf(), Buf(), Buf(), Buf(), Buf()
    cx.load_const(dmt.rearrange("p a b -> p (a b)"), io["dmt"], cx.B("dmt"))
    cx.load_const(qd.rearrange("p a b -> p (a b)"), io["qd"], cx.B("qd"))
    cx.load_const(kdg, io["kdg"], cx.B("kdg"))
    cx.load_const(ident, io["ident"], cx.B("ident"))
    for (wt, wd, bw) in ((wq, io["wq1_bf"], b_wq), (wk, io["wk1_bf"], b_wk), (wv, io["wv1_bf"], b_wv), (wg, io["wg1_bf"], b_wg)):
        sem = S.new_dma_sem()
        wvv = wd[:, :].rearrange("(dc p) n -> p dc n", p=128)
        for dc0 in range(0, 8, 4):
            S.dma("sp", lambda e, wt=wt, wvv=wvv, dc0=dc0: e.dma_start(out=wt[:, dc0:dc0 + 4, :], in_=wvv[:, dc0:dc0 + 4, :]), sem, writes=[bw])
    S.op("pool", lambda e: e.memset(mhalf, -0.5), writes=[cx.B("mhalf")])

    cx.load_const(A_t, io["adaA1"][:, :], b_A)
    cx.load_const(B_t, io["adaB1"][:, :], b_B)

    NXB = 4
    xbuf = cx.sb("xbuf", [128, NXB, 1024], F32)
    hT = [cx.sb("hT%d" % p, [128, 8, 512], BF16) for p in range(2)]
    hb = [cx.sb("hb%d" % i, [128, 1024], BF16) for i in range(NXB)]
    tmp = cx.sb("tmp", [128, 1024], F32)
    tg = cx.sb("tg", [128, 512], F32)
    qT = [cx.sb("qT%d" % p, [128, 4, 512], BF16) for p in range(2)]
    qdT = [cx.sb("qdT%d" % p, [128, 4, 512], BF16) for p in range(2)]
    kT = [cx.sb("kT%d" % p, [128, 4, 512], BF16) for p in range(2)]
    kd = [cx.sb("kd%d" % i, [128, 512], BF16) for i in range(2)]
    vt = [cx.sb("vt%d" % i, [128, 1024], BF16) for i in range(2)]
    gs = [cx.sb("gs%d" % i, [128, 1024], BF16) for i in range(3)]
    S32 = cx.sb("S32", [128, 2, 2, 512], F32)
    Sbf = cx.sb("Sbf", [128, 2, 2, 512], BF16)
    MT = [cx.sb("MT%d" % i, [128, 128], BF16) for i in range(2)]
    stat = [cx.sb("stat%d" % i, [128, 4], F32) for i in range(NXB)]
    ep = [[cx.sb("ep%d%d" % (i, hl), [128, 4], F32) for hl in range(2)] for i in range(2)]
    sq = cx.sb("sq", [128, 512], F32)
    yb = [[cx.sb("yb%d%d" % (i, hl), [128, 512], BF16) for hl in range(2)] for i in range(2)]
    ystage = [cx.sb("ystage%d" % i, [128, 8, 512], BF16) for i in range(2)]
    b_hT, b_qT, b_qdT, b_kT = [Buf(), Buf()], [Buf(), Buf()], [Buf(), Buf()], [Buf(), Buf()]
    b_tmp, b_tg, b_sq = Buf(), Buf(), Buf()
    b_x, b_hb, b_stat = [Buf() for _ in range(NXB)], [Buf() for _ in range(NXB)], [Buf() for _ in range(NXB)]
    b_kd, b_vt = [Buf(), Buf()], [Buf(), Buf()]
    b_gs = [Buf(), Buf(), Buf()]
    b_S32 = [[Buf(), Buf()], [Buf(), Buf()]]
    b_Sbf = [[Buf(), Buf()], [Buf(), Buf()]]
    b_MT = [Buf(), Buf()]
    b_ep = [[Buf(), Buf()], [Buf(), Buf()]]
    b_yb = [[Buf(), Buf()], [Buf(), Buf()]]
    b_ys, b_ywr = [Buf(), Buf()], [Buf(), Buf()]
    xsem = [S.new_dma_sem() for _ in range(NXB)]
    ysem = [S.new_dma_sem(), S.new_dma_sem()]
    for hl in range(2):
        for dk in range(2):
            S.op("pool", lambda e, hl=hl, dk=dk: e.memset(S32[:, hl, dk, :], 0.0), writes=[b_S32[hl][dk]])
            S.op("pool", lambda e, hl=hl, dk=dk: e.memset(Sbf[:, hl, dk, :], 0.0), writes=[b_Sbf[hl][dk]])

    pb_i = [0]

    PROT = (0, 1, 2, 6, 7)

    def next_pbank():
        b = PROT[pb_i[0] % 5]
        pb_i[0] += 1
        return b

    ob_i = [0]
    ntb = 4 * nsb

    def load_x(tb):
        xi = tb % NXB
        S.dma("sp", lambda e: e.dma_start(out=xbuf[:, xi, :], in_=x_d[tb * 128:(tb + 1) * 128, :]), xsem[xi], writes=[b_x[xi]])

    def chunk_norm(tb):
        def f():
            xi = tb % NXB
            xt = xbuf[:, xi, :]
            stt = stat[xi]
            S.op("dve", lambda e: e.scalar_tensor_tensor(out=tmp, in0=xt, scalar=1.0, in1=xt, op0=ALU.mult, op1=ALU.mult, accum_out=stt[:, 0:1]),
                 reads=[b_x[xi]], writes=[b_tmp, b_stat[xi]])
            S.op("dve", lambda e: e.tensor_scalar(out=stt[:, 1:2], in0=stt[:, 0:1], scalar1=1.0 / D, scalar2=1e-6, op0=ALU.mult, op1=ALU.add),
                 reads=[b_stat[xi]], writes=[b_stat[xi]])
            S.op("pool", lambda e: e.tensor_tensor(out=stt[:, 2:3], in0=stt[:, 1:2], in1=mhalf[:, 0:1], op=ALU.pow),
                 reads=[b_stat[xi], cx.B("mhalf")], writes=[b_stat[xi]])
            S.op("dve", lambda e: e.scalar_tensor_tensor(out=tmp, in0=xt, scalar=stt[:, 2:3], in1=A_t, op0=ALU.mult, op1=ALU.mult),
                 reads=[b_x[xi], b_stat[xi], b_A], writes=[b_tmp])
            S.op("pool", lambda e: e.tensor_tensor(out=hb[xi], in0=tmp, in1=B_t, op=ALU.add), reads=[b_tmp, b_B], writes=[b_hb[xi]])
            if tb + NXB < ntb:
                load_x(tb + NXB)
        return f

    def chunk_tr(tb):
        def f():
            xi, t, p = tb % NXB, tb % 4, (tb // 4) % 2
            pb = next_pbank()
            pbv = banks[pb][:].bitcast(BF16)
            for dc in range(8):
                S.op("pe", lambda e, dc=dc: e.transpose(out=pbv[:, dc * 128:(dc + 1) * 128], in_=hb[xi][:, dc * 128:(dc + 1) * 128], identity=ident),
                     reads=[b_hb[xi], cx.B("ident")], writes=[bank_b[pb]])
            S.op("act", lambda e: e.copy(out=hT[p][:, :, t * 128:(t + 1) * 128], in_=pbv.rearrange("p (dc n) -> p dc n", n=128)),
                 reads=[bank_b[pb]], writes=[b_hT[p]])
        return f

    def chunk_q(sb, ch):
        def f():
            p = sb % 2
            pb = next_pbank()
            for dc in range(8):
                S.op("pe", lambda e, dc=dc: e.matmul(banks[pb][:], lhsT=wq[:, dc, ch * 128:(ch + 1) * 128], rhs=hT[p][:, dc, :], start=(dc == 0), stop=(dc == 7)),
                     reads=[b_wq, b_hT[p]], writes=[bank_b[pb]])
            S.op("act", lambda e: e.copy(out=qT[p][:, ch, :], in_=banks[pb][:]), reads=[bank_b[pb]], writes=[b_qT[p]])
            S.op("dve", lambda e: e.tensor_tensor(out=qdT[p][:, ch, :], in0=banks[pb][:], in1=qd[:, ch // 2, :], op=ALU.mult),
                 reads=[bank_b[pb], cx.B("qd")], writes=[b_qdT[p]])
        return f

    def chunk_k(sb, ch):
        def f():
            p = sb % 2
            pb = next_pbank()
            for dc in range(8):
                S.op("pe", lambda e, dc=dc: e.matmul(banks[pb][:], lhsT=wk[:, dc, ch * 128:(ch + 1) * 128], rhs=hT[p][:, dc, :], start=(dc == 0), stop=(dc == 7)),
                     reads=[b_wk, b_hT[p]], writes=[bank_b[pb]])
            S.op("act", lambda e: e.copy(out=kT[p][:, ch, :], in_=banks[pb][:]), reads=[bank_b[pb]], writes=[b_kT[p]])
        return f

    def sb_chunks(sb):
        tb0 = 4 * sb
        ch = [chunk_norm(tb0 + t) for t in range(4)] + [chunk_tr(tb0 + t) for t in range(4)]
        ch += [chunk_q(sb, c) for c in range(4)] + [chunk_k(sb, c) for c in range(4)]
        return ch

    def proj_tok(tb):
        p, t, bi = (tb // 4) % 2, tb % 4, tb % 2
        gi = tb % 3
        tsl = slice(t * 128, (t + 1) * 128)
        pb = next_pbank()
        pbk = banks[pb][:].bitcast(BF16)
        for ch in range(4):
            S.op("pe", lambda e, ch=ch, pbk=pbk: e.transpose(out=pbk[:, ch * 128:(ch + 1) * 128], in_=kT[p][:, ch, tsl], identity=ident),
                 reads=[b_kT[p], cx.B("ident")], writes=[bank_b[pb]])
        for hl in range(2):
            S.op("dve", lambda e, hl=hl, pbk=pbk: e.tensor_scalar(out=kd[bi][:, hl * 256:(hl + 1) * 256], in0=pbk[:, hl * 256:(hl + 1) * 256],
                                                                  scalar1=kdg[:, hl:hl + 1], scalar2=None, op0=ALU.mult),
                 reads=[bank_b[pb], cx.B("kdg")], writes=[b_kd[bi]])
        for half in range(2):
            pb = next_pbank()
            for dc in range(8):
                S.op("pe", lambda e, dc=dc, pb=pb, half=half: e.matmul(banks[pb][:], lhsT=hT[p][:, dc, tsl], rhs=wv[:, dc, half * 512:(half + 1) * 512],
                                                                         start=(dc == 0), stop=(dc == 7)),
                     reads=[b_wv, b_hT[p]], writes=[bank_b[pb]])
            S.op("act", lambda e, pb=pb, half=half: e.copy(out=vt[bi][:, half * 512:(half + 1) * 512], in_=banks[pb][:]),
                 reads=[bank_b[pb]], writes=[b_vt[bi]])
        for half in range(2):
            pb = next_pbank()
            for dc in range(8):
                S.op("pe", lambda e, dc=dc, pb=pb, half=half: e.matmul(banks[pb][:], lhsT=hT[p][:, dc, tsl], rhs=wg[:, dc, half * 512:(half + 1) * 512],
                                                                         start=(dc == 0), stop=(dc == 7)),
                     reads=[b_wg, b_hT[p]], writes=[bank_b[pb]])
            S.op("act", lambda e, pb=pb: e.activation(out=tg, in_=banks[pb][:], func=AF.Tanh, scale=0.5), reads=[bank_b[pb]], writes=[b_tg])
            S.op("dve", lambda e, pb=pb, half=half: e.scalar_tensor_tensor(out=gs[gi][:, half * 512:(half + 1) * 512], in0=tg, scalar=1.0, in1=banks[pb][:],
                                                                            op0=ALU.add, op1=ALU.mult),
                 reads=[b_tg, bank_b[pb]], writes=[b_gs[gi]])

    def core(tb):
        p, t, bi = (tb // 4) % 2, tb % 4, tb % 2
        tsl = slice(t * 128, (t + 1) * 128)
        obs = []
        for hl in range(2):
            ob = 4 + ob_i[0] % 2
            ob_i[0] += 1
            obs.append(ob)
        vsl = [slice(hl * 512, (hl + 1) * 512) for hl in range(2)]

        def sc(hl):
            sreg = banks[3][:, hl * 128:(hl + 1) * 128]
            for dk in range(2):
                ch = 2 * hl + dk
                S.op("pe", lambda e, ch=ch, dk=dk: e.matmul(sreg, lhsT=kT[p][:, ch, tsl], rhs=qT[p][:, ch, tsl], start=(dk == 0), stop=(dk == 1)),
                     reads=[b_kT[p], b_qT[p]], writes=[bank_b[3]])
            S.op("dve", lambda e: e.tensor_tensor(out=MT[hl], in0=sreg, in1=dmt[:, hl, :], op=ALU.mult), reads=[bank_b[3], cx.B("dmt")], writes=[b_MT[hl]])

        dsb = {}

        def dS(hl):
            for dk in range(2):
                bk = next_pbank()
                dsb[(hl, dk)] = bk
                S.op("pe", lambda e, dk=dk, bk=bk: e.matmul(banks[bk][:], lhsT=kd[bi][:, hl * 256 + dk * 128:hl * 256 + (dk + 1) * 128], rhs=vt[bi][:, vsl[hl]],
                                                             start=True, stop=True),
                     reads=[b_kd[bi], b_vt[bi]], writes=[bank_b[bk]])

        def upd(hl):
            for dk in range(2):
                bk = dsb[(hl, dk)]
                S.op("dve", lambda e, dk=dk, bk=bk: e.scalar_tensor_tensor(out=S32[:, hl, dk, :], in0=S32[:, hl, dk, :], scalar=kdg[:, 2 + hl:3 + hl],
                                                                            in1=banks[bk][:], op0=ALU.mult, op1=ALU.add),
                     reads=[b_S32[hl][dk], cx.B("kdg"), bank_b[bk]], writes=[b_S32[hl][dk]])
                S.op("act", lambda e, dk=dk: e.copy(out=Sbf[:, hl, dk, :], in_=S32[:, hl, dk, :]), reads=[b_S32[hl][dk]], writes=[b_Sbf[hl][dk]])

        def out(hl):
            ob = obs[hl]
            S.op("pe", lambda e: e.matmul(banks[ob][:], lhsT=MT[hl], rhs=vt[bi][:, vsl[hl]], start=True, stop=False),
                 reads=[b_MT[hl], b_vt[bi]], writes=[bank_b[ob]])
            for dk in range(2):
                ch = 2 * hl + dk
                S.op("pe", lambda e, dk=dk, ch=ch: e.matmul(banks[ob][:], lhsT=qdT[p][:, ch, tsl], rhs=Sbf[:, hl, dk, :], start=False, stop=(dk == 1)),
                     reads=[b_qdT[p], b_Sbf[hl][dk]], writes=[bank_b[ob]])

        def epi(hl):
            ob = obs[hl]
            epi_t = ep[bi][hl]
            S.op("act", lambda e: e.activation(out=sq, in_=banks[ob][:], func=AF.Square, accum_out=epi_t[:, 0:1]),
                 reads=[bank_b[ob]], writes=[b_sq, b_ep[bi][hl]])
            S.op("dve", lambda e: e.tensor_scalar(out=epi_t[:, 1:2], in0=epi_t[:, 0:1], scalar1=1.0 / 128, scalar2=4e-5, op0=ALU.mult, op1=ALU.add),
                 reads=[b_ep[bi][hl]], writes=[b_ep[bi][hl]])
            S.op("pool", lambda e: e.tensor_tensor(out=epi_t[:, 2:3], in0=epi_t[:, 1:2], in1=mhalf[:, 0:1], op=ALU.pow),
                 reads=[b_ep[bi][hl], cx.B("mhalf")], writes=[b_ep[bi][hl]])
            S.op("dve", lambda e: e.scalar_tensor_tensor(out=yb[bi][hl], in0=banks[ob][:], scalar=epi_t[:, 2:3], in1=gs[tb % 3][:, vsl[hl]],
                                                          op0=ALU.mult, op1=ALU.mult),
                 reads=[bank_b[ob], b_ep[bi][hl], b_gs[tb % 3]], writes=[b_yb[bi][hl]])

        sc(0)
        dS(0)
        sc(1)
        out(0)
        upd(0)
        epi(0)
        dS(1)
        out(1)
        upd(1)
        epi(1)

    def transposes(tb):
        bi, t, yi = tb % 2, tb % 4, (tb // 4) % 2
        tsl = slice(t * 128, (t + 1) * 128)
        for hl in range(2):
            pb = next_pbank()
            pbv = banks[pb][:].bitcast(BF16)
            for ec in range(4):
                S.op("pe", lambda e, ec=ec, hl=hl, pbv=pbv: e.transpose(out=pbv[:, ec * 128:(ec + 1) * 128], in_=yb[bi][hl][:, ec * 128:(ec + 1) * 128],
                                                                         identity=ident),
                     reads=[b_yb[bi][hl], cx.B("ident")], writes=[bank_b[pb]])
            S.op("act", lambda e, hl=hl, pbv=pbv: e.copy(out=ystage[yi][:, hl * 4:(hl + 1) * 4, tsl], in_=pbv[:, 0:512].rearrange("p (c n) -> p c n", n=128)),
                 reads=[bank_b[pb]], writes=[b_ys[yi]])
        if t == 3:
            sb = tb // 4
            S.dma("sp", lambda e: e.dma_start(
                out=io["yg1_src"][sb // 2][:, :].rearrange("(c p) t -> p c t", p=128)[:, :, (sb % 2) * 512:(sb % 2 + 1) * 512],
                in_=ystage[yi]), ysem[yi], reads=[b_ys[yi]], writes=[b_ywr[yi]])
            if sb % 2 == 1:
                g = sb // 2
                S.dma("pool", lambda e: e.collective_compute("AllGather", ALU.bypass, replica_groups=PAIRS,
                                                             ins=[io["yg1_src"][g].ap().opt()], outs=[io["yg1_all"][g].ap().opt()]),
                      io["ccsem"], reads=[b_ywr[0], b_ywr[1]], writes=[io["b_cc1"][g]], inc=1)

    for tb in range(min(NXB, ntb)):
        load_x(tb)
    for f in sb_chunks(0):
        f()
    if nsb > 1:
        for t in range(4):
            chunk_norm(4 + t)()
    proj_tok(0)

    def sched(sb, t):
        out = []
        n1, n2 = sb + 1, sb + 2
        if n1 < nsb:
            if t == 0:
                out += [chunk_tr(4 * n1 + i) for i in range(4)]
            elif t == 1:
                out += [chunk_k(n1, 0), chunk_k(n1, 1), chunk_q(n1, 0), chunk_q(n1, 1)]
            elif t == 2:
                out += [chunk_k(n1, 2), chunk_k(n1, 3), chunk_q(n1, 2), chunk_q(n1, 3)]
        if n2 < nsb:
            if t == 2:
                out += [chunk_norm(4 * n2), chunk_norm(4 * n2 + 1)]
            elif t == 3:
                out += [chunk_norm(4 * n2 + 2), chunk_norm(4 * n2 + 3)]
        return out

    for tb in range(ntb):
        sb, t = tb // 4, tb % 4
        for f in sched(sb, t):
            f()
        if tb + 1 < ntb:
            proj_tok(tb + 1)
        if tb >= 1:
            transposes(tb - 1)
        core(tb)
    transposes(ntb - 1)


def l1_consts(hh):
    dmt = np.zeros((128, 2, 128), np.float32)
    qd = np.zeros((128, 2, 512), np.float32)
    kdg = np.zeros((128, 4), np.float32)
    j = np.arange(128, dtype=np.float64)[:, None]
    i = np.arange(128, dtype=np.float64)[None, :]
    for hl in range(2):
        lg = math.log(ret_gamma(2 * hh + hl))
        same = (j // 64) == (i // 64)
        later = (i // 64) > (j // 64)
        m = np.where(same, np.exp(lg * np.abs(i - j)), np.where(later, np.exp(lg * (i - j)), 0.0))
        dmt[:, hl, :] = m / 16.0
        qd[:, hl, :] = np.tile(np.exp(lg * np.arange(128, dtype=np.float64)), 4)[None, :]
        kdg[:, hl] = np.exp(lg * (128.0 - np.arange(128, dtype=np.float64))) / 16.0
        kdg[:, 2 + hl] = math.exp(lg * 128.0)
    return dmt.reshape(128, 256), qd.reshape(128, 1024), kdg


def build_fused(nsb=NSB):
    nc = bass.Bass("TRN2", target_bir_lowering=False)
    with ExitStack() as st:
        cx = Ctx(nc, st)
        S = cx.S
        io = {}
        shapes = {
            "x": [S_LEN, D], "ccol": [128, 8],
            "adaw0": [D, 3072], "adab0": [128, 3072], "adaw1": [D, 3072], "adab1": [128, 3072],
            "pre0": [128, D], "post0": [128, D], "pre1": [128, D], "post1": [128, D],
            "wq0": [D, 512], "wk0": [D, 512], "wv0": [D, 512], "wg0": [D, 512],
            "lamv": [128, 256], "subg": [128, 512], "kbtab": [128, 4 * 68], "dg0": [128, 4 * 128], "cbt": [128, 16],
            "wout0": [1024, D],
            "wq1": [D, 512], "wk1": [D, 512], "wv1": [D, 1024], "wg1": [D, 1024],
            "dmt": [128, 256], "qd": [128, 1024], "kdg": [128, 4], "wout1": [2048, D],
        }
        for n, shp in shapes.items():
            io[n] = cx.din(n, shp)
        io["ident"] = cx.din("ident", [128, 128], BF16)
        xo_d = cx.dout("xo", [S_LEN, D])
        io["yg0_src"] = [cx.dscratch("yg0_src%d" % g, [512, 2048], BF16) for g in range(4)]
        io["yg0_all"] = [cx.dscratch("yg0_all%d" % g, [1024, 2048], BF16) for g in range(4)]
        io["yg1_src"] = [cx.dscratch("yg1_src%d" % g, [1024, 1024], BF16) for g in range(8)]
        io["yg1_all"] = [cx.dscratch("yg1_all%d" % g, [2048, 1024], BF16) for g in range(8)]
        io["x1"] = cx.dscratch("x1s", [S_LEN, D], F32)
        for nme in ("wout0", "wq1", "wk1", "wv1", "wg1", "wout1"):
            io[nme + "_bf"] = cx.dscratch(nme + "_bf", shapes[nme], BF16)
        for nme in ("adaA1", "adaB1", "adaG0", "adaG1"):
            io[nme] = cx.dscratch(nme, [128, D], F32)
        io["b_cc0"] = [Buf() for _ in range(4)]
        io["b_cc1"] = [Buf() for _ in range(8)]
        io["ccsem"] = S.new_dma_sem()
        S.barrier_skip.add(io["ccsem"])
        cx.alloc_banks()

        phase_l0(cx, io, nsb)
        cx.new_phase()
        io["op_x0"], io["op_xo0"] = io["x"], io["x1"]
        io["op_ysrc0"] = lambda g4: (io["yg0_all"][g4 // 4][:, :].rearrange("(ec p) t -> p ec t", p=128)[:, :, (g4 % 4) * 512:(g4 % 4 + 1) * 512],
                                     io["b_cc0"][g4 // 4])
        phase_outproj(cx, io, 0, 1024, ntok=512 * nsb)
        cx.new_phase()
        phase_l1(cx, io, nsb)
        cx.new_phase()
        io["op_x1"], io["op_xo1"] = io["x1"], xo_d
        io["op_ysrc1"] = lambda g4: (io["yg1_all"][g4 // 2][:, :].rearrange("(ec p) t -> p ec t", p=128)[:, :, (g4 % 2) * 512:(g4 % 2 + 1) * 512],
                                     io["b_cc1"][g4 // 2])
        toks = phase_outproj(cx, io, 1, 2048, ntok=512 * nsb)
        S.wait_all("sp", toks)
        counts = S.emit(st)
    return nc, counts


_CACHE = {}


def make_in_maps(inp):
    ident = np.eye(128, dtype=np.float32).astype(ml_dtypes.bfloat16)
    w_in0, w_in1 = inp["da_w_in"][0], inp["ret_w_in"][0]
    lamv = np.concatenate([rep128(inp["da_lambda_q1"][0]), rep128(inp["da_lambda_k1"][0]),
                           rep128(inp["da_lambda_q2"][0]), rep128(inp["da_lambda_k2"][0])], axis=1)
    subg = np.ascontiguousarray(np.tile(rep128(inp["da_subln_gain"][0]), (1, 4)))
    wout0_rows = np.concatenate([np.arange(h * 128, (h + 1) * 128) for h in HL[0] + HL[1]])
    shared = {
        "adaw0": np.ascontiguousarray(inp["ada_w"][0]), "adab0": rep128(inp["ada_b"][0]),
        "adaw1": np.ascontiguousarray(inp["ada_w"][1]), "adab1": rep128(inp["ada_b"][1]),
        "pre0": rep128(inp["pre_gain"][0]), "post0": rep128(inp["post_gain"][0]),
        "pre1": rep128(inp["pre_gain"][1]), "post1": rep128(inp["post_gain"][1]),
        "lamv": lamv, "subg": subg, "ident": ident,
        "wout0": np.ascontiguousarray(inp["da_w_out"][0][wout0_rows]),
        "wout1": np.ascontiguousarray(inp["ret_w_out"][0]),
    }
    in_maps = []
    for core in range(8):
        b, hh = core // 2, core % 2
        hc = head_cols(hh, 128)
        kb, dg, cb = l0_consts(hh)
        dmt, qd, kdg = l1_consts(hh)
        qc = np.arange(hh * 512, (hh + 1) * 512)
        vc = np.arange(hh * 1024, (hh + 1) * 1024)
        m = dict(shared)
        m.update({
            "x": np.ascontiguousarray(inp["x"][b], dtype=np.float32),
            "ccol": np.ascontiguousarray(inp["c"][b].reshape(8, 128).T),
            "wq0": np.ascontiguousarray(w_in0[:, 0 + hc]), "wk0": np.ascontiguousarray(w_in0[:, 1024 + hc]),
            "wv0": np.ascontiguousarray(w_in0[:, 2048 + hc]), "wg0": np.ascontiguousarray(w_in0[:, 3072 + hc]),
            "kbtab": kb, "dg0": dg, "cbt": cb,
            "wq1": np.ascontiguousarray(w_in1[:, qc]), "wk1": np.ascontiguousarray(w_in1[:, 1024 + qc]),
            "wv1": np.ascontiguousarray(w_in1[:, 2048 + vc]), "wg1": np.ascontiguousarray(w_in1[:, 4096 + vc]),
            "dmt": dmt, "qd": qd, "kdg": kdg,
        })
        in_maps.append(m)
    return in_maps


def kernel(**inp):
    inp = {k: np.asarray(v) for k, v in inp.items()}
    if "fused" not in _CACHE:
        _CACHE["fused"] = build_fused()
    nc, _ = _CACHE["fused"]
    res = run_bass_kernel_spmd(nc, make_in_maps(inp), core_ids=list(range(8)))
    out = np.empty((4, S_LEN, D), np.float32)
    for b in range(4):
        out[b, :4096] = res.results[2 * b]["xo"][:4096]
        out[b, 4096:] = res.results[2 * b + 1]["xo"][4096:]
    return out
```

```python
import math
from contextlib import ExitStack

import numpy as np
import ml_dtypes

import concourse.bass as bass
import concourse.mybir as mybir
from concourse.bass_utils import run_bass_kernel_spmd

F32 = mybir.dt.float32
BF16 = mybir.dt.bfloat16
ALU = mybir.AluOpType
AF = mybir.ActivationFunctionType
ENGS = ("pe", "act", "dve", "pool", "sp")

S_LEN = 8192
D = 1024
NB = S_LEN // 128
NSB = S_LEN // 512
HL = ((7, 5, 3, 1), (6, 4, 2, 0))
T_SKIP = 40.0
SLOT_SLOPE_MIN = [min(2.0 ** -(HL[0][s] + 1), 2.0 ** -(HL[1][s] + 1)) for s in range(4)]
SLOT_WIDE = [max(2.0 ** -(HL[0][s] + 1), 2.0 ** -(HL[1][s] + 1)) <= 0.125 for s in range(4)]
SLOT_BACK = [min(NB, int((T_SKIP / SLOT_SLOPE_MIN[s] + 127) // 128)) for s in range(4)]
SLOT_RING = [min(NB, ((SLOT_BACK[s] + 4 + 3) // 4) * 4) for s in range(4)]
NEG_BIG = -30000.0
LAM_INIT0 = 0.8 - 0.6 * math.exp(-0.3 * 0)


class Buf:
    __slots__ = ("name", "w", "r", "excl")

    def __init__(self, name="", excl=False):
        self.name = name
        self.w = None
        self.r = []
        self.excl = excl


class Sched:
    def __init__(self, nc):
        self.nc = nc
        self.ops = {e: [] for e in ENGS}
        self.waited = {e: {} for e in ENGS}
        self.dma_sems = []
        self.barrier_skip = set()

    def _deps(self, eng, reads, writes):
        deps = []
        for b in list(reads) + list(writes):
            if b.w is not None:
                deps.append((b.w, "raw"))
        for b in writes:
            for t in b.r:
                deps.append((t, "war"))
        for b in reads:
            if b.excl:
                for t in b.r:
                    deps.append((t, "war"))
        wd = self.waited[eng]
        best = {}
        for t, kind in deps:
            if t[0] == "e" and t[1] == eng and eng == "pe":
                continue
            key = (t[0], t[1])
            if wd.get(key, -1) >= t[2]:
                continue
            if key not in best or best[key][2] < t[2]:
                best[key] = t
        waits = list(best.values())
        for t in waits:
            wd[(t[0], t[1])] = t[2]
            if t[0] == "e":
                self.ops[t[1]][t[2]]["marked"] = True
        return waits

    def op(self, eng, fn, reads=(), writes=()):
        waits = self._deps(eng, reads, writes)
        idx = len(self.ops[eng])
        self.ops[eng].append({"waits": waits, "fn": fn, "marked": False, "dma": None})
        tok = ("e", eng, idx)
        for b in reads:
            b.r.append(tok)
        for b in writes:
            b.w = tok
            b.r = []
        return tok

    def new_dma_sem(self):
        self.dma_sems.append(0)
        return len(self.dma_sems) - 1

    def dma(self, queue, fn, sem, reads=(), writes=(), inc=16):
        waits = self._deps(queue, reads, writes)
        self.dma_sems[sem] += inc
        tok = ("d", sem, self.dma_sems[sem])
        self.ops[queue].append({"waits": waits, "fn": fn, "marked": False, "dma": sem, "inc": inc})
        for b in reads:
            b.r.append(tok)
        for b in writes:
            b.w = tok
            b.r = []
        return tok

    def wait_all(self, eng, toks):
        for t in toks:
            if t[0] == "e":
                self.ops[t[1]][t[2]]["marked"] = True
        self.ops[eng].append({"waits": list(toks), "fn": None, "marked": False, "dma": None})

    def barrier(self):
        last = {}
        for e in ENGS:
            for i in range(len(self.ops[e]) - 1, -1, -1):
                o = self.ops[e][i]
                if o["dma"] is None and o["fn"] is not None:
                    last[e] = ("e", e, i)
                    break
        for e in ENGS:
            toks = [t for k, t in last.items() if k != e]
            toks += [("d", i, v) for i, v in enumerate(self.dma_sems) if v > 0 and i not in self.barrier_skip]
            wd = self.waited[e]
            toks = [t for t in toks if wd.get((t[0], t[1]), -1) < t[2]]
            for t in toks:
                wd[(t[0], t[1])] = t[2]
            self.wait_all(e, toks)

    def emit(self, stack):
        nc = self.nc
        esem = {e: stack.enter_context(nc.semaphore("s_" + e)) for e in ENGS}
        dsem = [stack.enter_context(nc.semaphore("d%d" % i)) for i in range(len(self.dma_sems))]
        pref = {}
        for e in ENGS:
            c = 0
            arr = []
            for o in self.ops[e]:
                if o["marked"]:
                    c += 1
                arr.append(c)
            pref[e] = arr
        block = stack.enter_context(nc.Block())

        def run(e_name):
            def body(e):
                for o in self.ops[e_name]:
                    for t in o["waits"]:
                        if t[0] == "e":
                            e.wait_ge(esem[t[1]], pref[t[1]][t[2]])
                        else:
                            e.wait_ge(dsem[t[1]], t[2])
                    if o["fn"] is None:
                        continue
                    ins = o["fn"](e)
                    if o["dma"] is not None:
                        ins.then_inc(dsem[o["dma"]], o["inc"])
                    elif o["marked"]:
                        ins.then_inc(esem[e_name], 1)
            return body

        block.tensor(run("pe"))
        block.scalar(run("act"))
        block.vector(run("dve"))
        block.gpsimd(run("pool"))
        block.sync(run("sp"))
        return {e: len(self.ops[e]) for e in ENGS}


class Ctx:
    ARENA_F32 = 47 * 1024

    def __init__(self, nc, st):
        self.nc = nc
        self.st = st
        self.S = Sched(nc)
        self.bufs = {}
        self.banks = []
        self.bank_b = []
        self.arena = st.enter_context(nc.sbuf_tensor("arena", [128, self.ARENA_F32], F32))
        self.off = 0
        self.phase = 0

    def new_phase(self):
        self.S.barrier()
        self.off = 0
        self.bufs = {}
        self.phase += 1

    def sb(self, name, shape, dt):
        n = 1
        for d in shape[1:]:
            n *= d
        nbytes = n * (2 if dt == BF16 else 4)
        nf = (nbytes + 31) // 32 * 8
        assert self.off + nf <= self.ARENA_F32, ("SBUF arena overflow", name, self.off, nf)
        ap = self.arena[:, self.off:self.off + nbytes // 4]
        self.off += nf
        if dt == BF16:
            ap = ap.bitcast(BF16)
        if len(shape) == 3:
            ap = ap.rearrange("p (a b) -> p a b", a=shape[1])
        elif len(shape) == 4:
            ap = ap.rearrange("p (a b c) -> p a b c", a=shape[1], b=shape[2])
        return ap

    def alloc_banks(self):
        for i in range(8):
            self.banks.append(self.st.enter_context(self.nc.psum_tensor("bank%d" % i, [128, 512], F32)))
            self.bank_b.append(Buf("bank%d" % i, excl=True))

    def B(self, name):
        if name not in self.bufs:
            self.bufs[name] = Buf(name)
        return self.bufs[name]

    def din(self, name, shape, dt=F32):
        return self.nc.dram_tensor(name, list(shape), dt, kind="ExternalInput").ap()

    def dscratch(self, name, shape, dt=F32):
        return self.nc.dram_tensor(name, list(shape), dt)

    def dout(self, name, shape, dt=F32):
        return self.nc.dram_tensor(name, list(shape), dt, kind="ExternalOutput").ap()

    def load_const(self, tile_ap, dram_ap, buf, queue="sp"):
        sem = self.S.new_dma_sem()
        return self.S.dma(queue, lambda e: e.dma_start(out=tile_ap, in_=dram_ap), sem, writes=[buf])


def emit_adaln(cx, ccol_d, adaw_d, adab_t, adab_b, wchunk, wchunk_b, ncols, consume):
    S = cx.S
    ccol = cx.sb("ccol", [128, 8], F32)
    cth = cx.sb("cth", [128, 8], F32)
    cond = cx.sb("cond", [128, 8], F32)
    crep = cx.sb("crep", [128, 8, 128], F32)
    b_c, b_crep = cx.B("ccol"), cx.B("crep")
    cx.load_const(ccol[:], ccol_d, b_c)
    S.op("act", lambda e: e.activation(out=cth[:], in_=ccol[:], func=AF.Tanh, scale=0.5), reads=[b_c], writes=[cx.B("cth")])
    S.op("dve", lambda e: e.scalar_tensor_tensor(out=cond[:], in0=cth[:], scalar=1.0, in1=ccol[:], op0=ALU.add, op1=ALU.mult),
         reads=[cx.B("cth"), b_c], writes=[cx.B("cond")])
    S.op("dve", lambda e: e.tensor_scalar(out=cond[:], in0=cond[:], scalar1=0.5, scalar2=None, op0=ALU.mult),
         reads=[cx.B("cond")], writes=[cx.B("cond")])
    for j in range(8):
        S.op("dve", lambda e, j=j: e.tensor_copy(out=crep[:, j, :], in_=cond[:, j:j + 1].to_broadcast([128, 128])),
             reads=[cx.B("cond")], writes=[b_crep])
    wsem = S.new_dma_sem()
    adaw_v = adaw_d.rearrange("(dc p) n -> p dc n", p=128)
    for ci in range(ncols // 512):
        S.dma("sp", lambda e, ci=ci: e.dma_start(out=wchunk, in_=adaw_v[:, :, ci * 512:(ci + 1) * 512]), wsem, writes=[wchunk_b])
        bk = ci % 2 + 6
        for j in range(8):
            S.op("pe", lambda e, j=j, bk=bk: e.matmul(cx.banks[bk][:], lhsT=crep[:, j, :], rhs=wchunk[:, j, :],
                                                        start=(j == 0), stop=(j == 7)),
                 reads=[b_crep, wchunk_b], writes=[cx.bank_b[bk]])
        consume(ci, cx.banks[bk], cx.bank_b[bk])


def emit_adaln_all(cx, io, A_t, B_t, b_A, b_B):
    S = cx.S
    banks, bank_b = cx.banks, cx.bank_b
    ccol = cx.sb("ccol", [128, 8], F32)
    cth = cx.sb("cth", [128, 8], F32)
    cond = cx.sb("cond", [128, 8], F32)
    crep = cx.sb("crep", [128, 8, 128], F32)
    adab = cx.sb("adab", [128, 3072], F32)
    g_pre = cx.sb("g_pre", [128, D], F32)
    g_post = cx.sb("g_post", [128, D], F32)
    tga = cx.sb("tga", [128, 512], F32)
    stg = [cx.sb("stg%d" % i, [128, D], F32) for i in range(3)]
    wch = [cx.sb("wch%d" % i, [128, 8, 512], F32) for i in range(2)]
    b_c, b_cth, b_cond, b_crep, b_adab, b_gpre, b_gpost, b_tga = [Buf() for _ in range(8)]
    b_stg = [Buf(), Buf(), Buf()]
    b_wch = [Buf(), Buf()]
    cx.load_const(ccol, io["ccol"], b_c)
    S.op("act", lambda e: e.activation(out=cth, in_=ccol, func=AF.Tanh, scale=0.5), reads=[b_c], writes=[b_cth])
    S.op("dve", lambda e: e.scalar_tensor_tensor(out=cond, in0=cth, scalar=1.0, in1=ccol, op0=ALU.add, op1=ALU.mult),
         reads=[b_cth, b_c], writes=[b_cond])
    S.op("dve", lambda e: e.tensor_scalar(out=cond, in0=cond, scalar1=0.5, scalar2=None, op0=ALU.mult), reads=[b_cond], writes=[b_cond])
    for j in range(8):
        S.op("dve", lambda e, j=j: e.tensor_copy(out=crep[:, j, :], in_=cond[:, j:j + 1].to_broadcast([128, 128])),
             reads=[b_cond], writes=[b_crep])
    wsem = [S.new_dma_sem(), S.new_dma_sem()]
    csem = [S.new_dma_sem() for _ in range(3)]
    ssem = [S.new_dma_sem() for _ in range(3)]
    n = 0
    for layer in range(2):
        S.dma("sp", lambda e, layer=layer: e.dma_start(out=adab, in_=io["adab%d" % layer]), csem[0], writes=[b_adab])
        S.dma("sp", lambda e, layer=layer: e.dma_start(out=g_pre, in_=io["pre%d" % layer]), csem[1], writes=[b_gpre])
        S.dma("sp", lambda e, layer=layer: e.dma_start(out=g_post, in_=io["post%d" % layer]), csem[2], writes=[b_gpost])
        adaw_v = io["adaw%d" % layer].rearrange("(dc p) n -> p dc n", p=128)
        for ci in range(6):
            wi = n % 2
            bk = 6 + n % 2
            n += 1
            S.dma("sp", lambda e, ci=ci, wi=wi, adaw_v=adaw_v: e.dma_start(out=wch[wi], in_=adaw_v[:, :, ci * 512:(ci + 1) * 512]), wsem[wi], writes=[b_wch[wi]])
            for j in range(8):
                S.op("pe", lambda e, j=j, bk=bk, wi=wi: e.matmul(banks[bk][:], lhsT=crep[:, j, :], rhs=wch[wi][:, j, :], start=(j == 0), stop=(j == 7)),
                     reads=[b_crep, b_wch[wi]], writes=[bank_b[bk]])
            kind, half = ci // 2, ci % 2
            cols = slice(half * 512, half * 512 + 512)
            acols = slice(ci * 512, ci * 512 + 512)
            if kind == 0:
                dst, dbuf = (B_t, b_B) if layer == 0 else (stg[1], b_stg[1])
                S.op("dve", lambda e, bk=bk, dst=dst, cols=cols, acols=acols: e.tensor_tensor(out=dst[:, cols], in0=banks[bk][:], in1=adab[:, acols], op=ALU.add),
                     reads=[bank_b[bk], b_adab], writes=[dbuf])
            else:
                S.op("dve", lambda e, bk=bk, acols=acols: e.tensor_tensor(out=tga, in0=banks[bk][:], in1=adab[:, acols], op=ALU.add),
                     reads=[bank_b[bk], b_adab], writes=[b_tga])
                if kind == 1:
                    dst, dbuf = (A_t, b_A) if layer == 0 else (stg[0], b_stg[0])
                    S.op("dve", lambda e, dst=dst, cols=cols: e.scalar_tensor_tensor(out=dst[:, cols], in0=tga, scalar=1.0, in1=g_pre[:, cols], op0=ALU.add, op1=ALU.mult),
                         reads=[b_tga, b_gpre], writes=[dbuf])
                else:
                    S.op("dve", lambda e, cols=cols: e.tensor_tensor(out=stg[2][:, cols], in0=tga, in1=g_post[:, cols], op=ALU.mult),
                         reads=[b_tga, b_gpost], writes=[b_stg[2]])
        if layer == 1:
            S.dma("sp", lambda e: e.dma_start(out=io["adaA1"][:, :], in_=stg[0]), ssem[0], reads=[b_stg[0]])
            S.dma("sp", lambda e: e.dma_start(out=io["adaB1"][:, :], in_=stg[1]), ssem[1], reads=[b_stg[1]])
        S.dma("sp", lambda e, layer=layer: e.dma_start(out=io["adaG%d" % layer][:, :], in_=stg[2]), ssem[2], reads=[b_stg[2]])


PAIRS = [[0, 1], [2, 3], [4, 5], [6, 7]]


def phase_l0(cx, io, nsb=NSB):
    S = cx.S
    x_d, ccol_d = io["x"], io["ccol"]
    adaw_d, adab_d, pg_d = io["adaw0"][:, 0:2048], io["adab0"][:, 0:2048], io["pre0"]
    w_d = [io[n] for n in ("wq0", "wk0", "wv0", "wg0")]
    banks, bank_b = cx.banks, cx.bank_b
    ring = [min(NB, SLOT_RING[s] + 4) for s in range(4)]
    ring_off = [0]
    for s in range(4):
        ring_off.append(ring_off[-1] + ring[s])
    RT = ring_off[-1]
    kT = cx.sb("kT", [128, RT * 128], BF16)
    Va = cx.sb("Va", [128, RT, 130], BF16)
    wbf = [cx.sb("wbf%d" % i, [128, 8, 512], BF16) for i in range(4)]
    A_t = cx.sb("A_t", [128, D], F32)
    B_t = cx.sb("B_t", [128, D], F32)
    kbt = cx.sb("kbt", [128, 4, 68], F32)
    dg0 = cx.sb("dg0", [128, 4, 128], F32)
    cbt = cx.sb("cbt", [128, 4, 4], F32)
    ident = cx.sb("ident", [128, 128], BF16)
    lsc = cx.sb("lsc", [128, 8], F32)
    subg = cx.sb("subg", [128, 512], F32)
    mhalf = cx.sb("mhalf", [128, 4], F32)
    b_w = [Buf() for _ in range(4)]
    b_A, b_B, b_lam, b_ones = Buf(), Buf(), Buf(), Buf()
    cx.load_const(kbt.rearrange("p a b -> p (a b)"), io["kbtab"], cx.B("kbt"))
    cx.load_const(dg0.rearrange("p a b -> p (a b)"), io["dg0"], cx.B("dg0"))
    cx.load_const(cbt.rearrange("p a b -> p (a b)"), io["cbt"], cx.B("cbt"))
    cx.load_const(ident, io["ident"], cx.B("ident"))
    cx.load_const(subg, io["subg"], cx.B("subg"))
    for i in range(4):
        sem = S.new_dma_sem()
        wv_ = w_d[i].rearrange("(dc p) n -> p dc n", p=128)
        S.dma("pool", lambda e, i=i, wv_=wv_: e.dma_start(out=wbf[i], in_=wv_), sem, writes=[b_w[i]])
    S.op("pool", lambda e: e.memset(mhalf, -0.5), writes=[cx.B("mhalf")])
    S.op("pool", lambda e: e.memset(Va[:, :, 128:130], 1.0), writes=[b_ones])
    S.op("dve", lambda e: e.tensor_scalar(out=subg, in0=subg, scalar1=0.5 * (1.0 - LAM_INIT0), scalar2=None, op0=ALU.mult),
         reads=[cx.B("subg")], writes=[cx.B("subg")])

    off0 = cx.off
    lamv = cx.sb("lamv", [128, 256], F32)
    ljunk = cx.sb("ljunk", [128, 64], F32)
    cx.load_const(lamv, io["lamv"], cx.B("lamv"))
    b_lj = Buf()
    S.op("dve", lambda e: e.scalar_tensor_tensor(out=ljunk, in0=lamv[:, 0:64], scalar=1.0, in1=lamv[:, 64:128],
                                                  op0=ALU.mult, op1=ALU.mult, accum_out=lsc[:, 0:1]),
         reads=[cx.B("lamv")], writes=[b_lj, b_lam])
    S.op("dve", lambda e: e.scalar_tensor_tensor(out=ljunk, in0=lamv[:, 128:192], scalar=1.0, in1=lamv[:, 192:256],
                                                  op0=ALU.mult, op1=ALU.mult, accum_out=lsc[:, 1:2]),
         reads=[cx.B("lamv")], writes=[b_lj, b_lam])
    S.op("act", lambda e: e.activation(out=lsc[:, 2:4], in_=lsc[:, 0:2], func=AF.Exp), reads=[b_lam], writes=[b_lam])
    S.op("dve", lambda e: e.tensor_tensor(out=lsc[:, 4:5], in0=lsc[:, 2:3], in1=lsc[:, 3:4], op=ALU.subtract),
         reads=[b_lam], writes=[b_lam])
    S.op("dve", lambda e: e.tensor_scalar(out=lsc[:, 5:6], in0=lsc[:, 4:5], scalar1=LAM_INIT0, scalar2=-1.0,
                                           op0=ALU.add, op1=ALU.mult), reads=[b_lam], writes=[b_lam])
    neglam = lsc[:, 5:6]

    emit_adaln_all(cx, io, A_t, B_t, b_A, b_B)
    S.barrier()
    cx.off = off0

    NXB = 4
    xbuf = cx.sb("xbuf", [128, NXB, 1024], F32)
    hT = cx.sb("hT", [128, 8, 512], BF16)
    hb = [cx.sb("hb%d" % i, [128, 1024], BF16) for i in range(NXB)]
    tmp = cx.sb("tmp", [128, 1024], F32)
    qT = [[cx.sb("qT%d%d" % (p, c), [128, 4, 512], BF16) for c in range(2)] for p in range(2)]
    gs = [cx.sb("gs%d" % p, [128, 4, 512], BF16) for p in range(2)]
    tg = cx.sb("tg", [128, 512], F32)
    tg2 = cx.sb("tg2", [128, 512], F32)
    PT = [cx.sb("PT%d" % i, [128, 512], BF16) for i in range(3)]
    dtmp = [cx.sb("dtmp%d" % i, [128, 128], F32) for i in range(2)]
    stat = [cx.sb("stat%d" % i, [128, 4], F32) for i in range(NXB)]
    o1_t = cx.sb("o1_t", [128, 4, 128], F32)
    o_t = cx.sb("o_t", [128, 4, 128], F32)
    sqj = cx.sb("sqj", [128, 128], F32)
    ep = cx.sb("ep", [128, 24], F32)
    yb = [cx.sb("yb%d" % i, [128, 4, 128], BF16) for i in range(2)]
    ystage = [cx.sb("ystage%d" % i, [128, 512], BF16) for i in range(2)]
    b_x = [Buf() for _ in range(NXB)]
    b_hT, b_tmp, b_tg, b_tg2 = Buf(), Buf(), Buf(), Buf()
    b_hb = [Buf() for _ in range(NXB)]
    b_qT = [Buf(), Buf()]
    b_gs = [Buf(), Buf()]
    b_PT = [Buf(), Buf(), Buf()]
    b_dtmp = [Buf(), Buf()]
    b_stat = [Buf() for _ in range(NXB)]
    b_kring = [[Buf() for _ in range(ring[s] // 4)] for s in range(4)]
    b_vring = [[Buf() for _ in range(ring[s] // 4)] for s in range(4)]
    b_o1, b_o, b_sqj, b_ep0, b_ep1 = Buf(), Buf(), Buf(), Buf(), Buf()
    b_yb = [Buf(), Buf()]
    deferred = []
    b_ys = [Buf(), Buf()]
    b_ywr = [Buf(), Buf()]
    xsem = [S.new_dma_sem() for _ in range(NXB)]
    ysem = [S.new_dma_sem(), S.new_dma_sem()]
    for p in range(2):
        S.op("pool", lambda e, p=p: e.memset(qT[p][0][64:128, :, :], 0.0), writes=[b_qT[p]])
        S.op("pool", lambda e, p=p: e.memset(qT[p][1][0:64, :, :], 0.0), writes=[b_qT[p]])

    def ring_col(s, j):
        return (ring_off[s] + (j % ring[s])) * 128

    def ring_blk(s, j):
        return ring_off[s] + (j % ring[s])

    def ring_grp(s, j):
        return (j // 4) % (ring[s] // 4)

    pb_i = [0]

    def next_pbank():
        return 7

    ST_BANKS = (0, 1, 6)
    ntb = 4 * nsb

    def load_x(tb):
        xi = tb % NXB
        S.dma("sp", lambda e: e.dma_start(out=xbuf[:, xi, :], in_=x_d[tb * 128:(tb + 1) * 128, :]), xsem[xi], writes=[b_x[xi]])

    def chunk_norm_a(tb):
        def f():
            xi = tb % NXB
            xt = xbuf[:, xi, :]
            stt = stat[xi]
            S.op("dve", lambda e: e.scalar_tensor_tensor(out=tmp, in0=xt, scalar=1.0, in1=xt, op0=ALU.mult, op1=ALU.mult, accum_out=stt[:, 0:1]),
                 reads=[b_x[xi]], writes=[b_tmp, b_stat[xi]])
            S.op("dve", lambda e: e.tensor_scalar(out=stt[:, 1:2], in0=stt[:, 0:1], scalar1=1.0 / D, scalar2=1e-6, op0=ALU.mult, op1=ALU.add),
                 reads=[b_stat[xi]], writes=[b_stat[xi]])
            S.op("pool", lambda e: e.tensor_tensor(out=stt[:, 2:3], in0=stt[:, 1:2], in1=mhalf[:, 0:1], op=ALU.pow),
                 reads=[b_stat[xi], cx.B("mhalf")], writes=[b_stat[xi]])
        return f

    def chunk_norm(tb):
        def f():
            xi = tb % NXB
            xt = xbuf[:, xi, :]
            stt = stat[xi]
            S.op("dve", lambda e: e.scalar_tensor_tensor(out=tmp, in0=xt, scalar=stt[:, 2:3], in1=A_t, op0=ALU.mult, op1=ALU.mult),
                 reads=[b_x[xi], b_stat[xi], b_A], writes=[b_tmp])
            S.op("pool", lambda e: e.tensor_tensor(out=hb[xi], in0=tmp, in1=B_t, op=ALU.add), reads=[b_tmp, b_B], writes=[b_hb[xi]])
            if tb + NXB < ntb:
                load_x(tb + NXB)
        return f

    def evac(sb, out_ap, in_ap, reads, writes):
        if sb < 8:
            S.op("act", lambda e: e.copy(out=out_ap, in_=in_ap), reads=reads, writes=writes)
        else:
            S.op("dve", lambda e: e.tensor_copy(out=out_ap, in_=in_ap), reads=reads, writes=writes)

    def chunk_tr(tb):
        def f():
            xi, t = tb % NXB, tb % 4
            pb = next_pbank()
            pbv = banks[pb][:].bitcast(BF16)
            for dc in range(8):
                S.op("pe", lambda e, dc=dc: e.transpose(out=pbv[:, dc * 128:(dc + 1) * 128], in_=hb[xi][:, dc * 128:(dc + 1) * 128], identity=ident),
                     reads=[b_hb[xi], cx.B("ident")], writes=[bank_b[pb]])
            evac(tb // 4, hT[:, :, t * 128:(t + 1) * 128], pbv.rearrange("p (dc n) -> p dc n", n=128), [bank_b[pb]], [b_hT])
        return f

    def chunk_q(sb, s):
        def f():
            p = sb % 2
            pb = next_pbank()
            for dc in range(8):
                S.op("pe", lambda e, dc=dc: e.matmul(banks[pb][:], lhsT=wbf[0][:, dc, s * 128:(s + 1) * 128], rhs=hT[:, dc, :], start=(dc == 0), stop=(dc == 7)),
                     reads=[b_w[0], b_hT], writes=[bank_b[pb]])
            evac(sb, qT[p][0][0:64, s, :], banks[pb][0:64, :], [bank_b[pb]], [b_qT[p]])
            evac(sb, qT[p][1][64:128, s, :], banks[pb][64:128, :], [bank_b[pb]], [b_qT[p]])
        return f

    def chunk_k(sb, s):
        def f():
            pb = next_pbank()
            for dc in range(8):
                S.op("pe", lambda e, dc=dc: e.matmul(banks[pb][:], lhsT=wbf[1][:, dc, s * 128:(s + 1) * 128], rhs=hT[:, dc, :], start=(dc == 0), stop=(dc == 7)),
                     reads=[b_w[1], b_hT], writes=[bank_b[pb]])
            c0 = ring_col(s, 4 * sb)
            evac(sb, kT[:, c0:c0 + 512], banks[pb][:], [bank_b[pb]], [b_kring[s][ring_grp(s, 4 * sb)]])
        return f

    def chunk_v(sb, t):
        def f():
            pb = next_pbank()
            for dc in range(8):
                S.op("pe", lambda e, dc=dc: e.matmul(banks[pb][:], lhsT=hT[:, dc, t * 128:(t + 1) * 128], rhs=wbf[2][:, dc, :], start=(dc == 0), stop=(dc == 7)),
                     reads=[b_w[2], b_hT], writes=[bank_b[pb]])
            for s in range(4):
                rb = ring_blk(s, 4 * sb + t)
                evac(sb, Va[:, rb, 0:128], banks[pb][:, s * 128:(s + 1) * 128], [bank_b[pb]], [b_vring[s][ring_grp(s, 4 * sb)]])
        return f

    def chunk_g(sb, t):
        def f():
            p = sb % 2
            pb = next_pbank()
            for dc in range(8):
                S.op("pe", lambda e, dc=dc: e.matmul(banks[pb][:], lhsT=hT[:, dc, t * 128:(t + 1) * 128], rhs=wbf[3][:, dc, :], start=(dc == 0), stop=(dc == 7)),
                     reads=[b_w[3], b_hT], writes=[bank_b[pb]])
            S.op("act", lambda e: e.activation(out=tg, in_=banks[pb][:], func=AF.Tanh, scale=0.5), reads=[bank_b[pb]], writes=[b_tg])
            S.op("dve", lambda e: e.scalar_tensor_tensor(out=tg2, in0=tg, scalar=1.0, in1=banks[pb][:], op0=ALU.add, op1=ALU.mult),
                 reads=[b_tg, bank_b[pb]], writes=[b_tg2])
            S.op("pool", lambda e: e.tensor_tensor(out=gs[p][:, t, :], in0=tg2, in1=subg, op=ALU.mult), reads=[b_tg2, cx.B("subg")], writes=[b_gs[p]])
        return f

    def proj_chunks(sb):
        tb0 = 4 * sb
        ch = [chunk_norm_a(tb0), chunk_norm_a(tb0 + 1), chunk_norm(tb0), chunk_norm_a(tb0 + 2), chunk_norm(tb0 + 1), chunk_norm_a(tb0 + 3),
              chunk_norm(tb0 + 2), chunk_norm(tb0 + 3)] + [chunk_tr(tb0 + t) for t in range(4)]
        ch += [chunk_k(sb, s) for s in range(4)] + [chunk_v(sb, t) for t in range(4)]
        ch += [chunk_q(sb, s) for s in range(4)] + [chunk_g(sb, t) for t in range(4)]
        return ch

    st_i, pt_i, dt_i = [0], [0], [0]
    accs = [(2, 3), (4, 5)]

    def acc_ap(c, r, lo, hi):
        return banks[accs[c][r // 2]][:, (r % 2) * 130 + lo:(r % 2) * 130 + hi]

    def emit_qk(tile):
        sb, s, c, idx, j = tile
        p = sb % 2
        r0 = max(0, j - 4 * sb)
        sti = ST_BANKS[st_i[0] % 3]
        st_i[0] += 1
        tile.append(sti)
        kc = ring_col(s, j)
        S.op("pe", lambda e: e.matmul(banks[sti][:, r0 * 128:512], lhsT=kT[:, kc:kc + 128], rhs=qT[p][c][:, s, r0 * 128:512], start=True, stop=True),
             reads=[b_kring[s][ring_grp(s, j)], b_qT[p]], writes=[bank_b[sti]])

    def emit_exp_pv(tile):
        sb, s, c, idx, j, sti = tile
        wide = SLOT_WIDE[s]
        r0 = max(0, j - 4 * sb)
        ST = banks[sti]
        pti = pt_i[0] % 3
        pt_i[0] += 1
        P = PT[pti]
        rstart = r0
        if j >= 4 * sb:
            di = dt_i[0] % 2
            dt_i[0] += 1
            S.op("dve", lambda e: e.scalar_tensor_tensor(out=dtmp[di], in0=ST[:, r0 * 128:(r0 + 1) * 128], scalar=0.125, in1=dg0[:, s, :], op0=ALU.mult, op1=ALU.add),
                 reads=[bank_b[sti], cx.B("dg0")], writes=[b_dtmp[di]])
            S.op("act", lambda e: e.activation(out=P[:, r0 * 128:(r0 + 1) * 128], in_=dtmp[di], func=AF.Exp, bias=cbt[:, s, r0:r0 + 1], scale=1.0),
                 reads=[b_dtmp[di], cx.B("cbt")], writes=[b_PT[pti]])
            rstart = r0 + 1
        if rstart < 4:
            if wide:
                ti = 4 * sb + 3 - j
                S.op("act", lambda e: e.activation(out=P[:, rstart * 128:512], in_=ST[:, rstart * 128:512], func=AF.Exp, bias=kbt[:, s, ti:ti + 1], scale=0.125),
                     reads=[bank_b[sti], cx.B("kbt")], writes=[b_PT[pti]])
            else:
                for r in range(rstart, 4):
                    ti = 4 * sb + r - j
                    S.op("act", lambda e, r=r, ti=ti: e.activation(out=P[:, r * 128:(r + 1) * 128], in_=ST[:, r * 128:(r + 1) * 128], func=AF.Exp,
                                                                     bias=kbt[:, s, ti:ti + 1], scale=0.125),
                         reads=[bank_b[sti], cx.B("kbt")], writes=[b_PT[pti]])
        rb = ring_blk(s, j)
        for r in range(r0, 4):
            S.op("pe", lambda e, r=r: e.matmul(acc_ap(c, r, 0, 130), lhsT=P[:, r * 128:(r + 1) * 128], rhs=Va[:, rb, :],
                                                start=(idx == 0 and r % 2 == 0), stop=True, skip_group_check=True),
                 reads=[b_PT[pti], b_vring[s][ring_grp(s, j)], b_ones], writes=[bank_b[accs[c][r // 2]]])

    def epi_c0(sb, s):
        for hb2 in range(2):
            bk = accs[0][hb2]
            S.op("dve", lambda e, bk=bk, hb2=hb2: e.reciprocal(out=ep[:, 2 * hb2:2 * hb2 + 2], in_=banks[bk][:, 128:259:130]),
                 reads=[bank_b[bk]], writes=[b_ep0])
        for r in range(4):
            S.op("dve", lambda e, r=r: e.tensor_scalar(out=o1_t[:, r, :], in0=acc_ap(0, r, 0, 128), scalar1=ep[:, r:r + 1], scalar2=None, op0=ALU.mult),
                 reads=[bank_b[accs[0][r // 2]], b_ep0], writes=[b_o1])

    def epi_c1(sb, s):
        p = sb % 2
        for hb2 in range(2):
            bk = accs[1][hb2]
            S.op("dve", lambda e, bk=bk, hb2=hb2: e.reciprocal(out=ep[:, 4 + 2 * hb2:4 + 2 * hb2 + 2], in_=banks[bk][:, 128:259:130]),
                 reads=[bank_b[bk]], writes=[b_ep1])
        S.op("dve", lambda e: e.tensor_scalar(out=ep[:, 8:12], in0=ep[:, 4:8], scalar1=neglam, scalar2=None, op0=ALU.mult),
             reads=[b_ep1, b_lam], writes=[b_ep1])
        for r in range(4):
            S.op("dve", lambda e, r=r: e.scalar_tensor_tensor(out=o_t[:, r, :], in0=acc_ap(1, r, 0, 128), scalar=ep[:, 8 + r:9 + r], in1=o1_t[:, r, :],
                                                               op0=ALU.mult, op1=ALU.add),
                 reads=[bank_b[accs[1][r // 2]], b_ep1, b_o1], writes=[b_o])
        for r in range(4):
            S.op("dve", lambda e, r=r: e.scalar_tensor_tensor(out=sqj, in0=o_t[:, r, :], scalar=1.0, in1=o_t[:, r, :], op0=ALU.mult, op1=ALU.mult,
                                                               accum_out=ep[:, 12 + r:13 + r]),
                 reads=[b_o], writes=[b_sqj, b_ep1])
        S.op("dve", lambda e: e.tensor_scalar(out=ep[:, 12:16], in0=ep[:, 12:16], scalar1=1.0 / 128, scalar2=1e-5, op0=ALU.mult, op1=ALU.add),
             reads=[b_ep1], writes=[b_ep1])
        S.op("pool", lambda e: e.tensor_tensor(out=ep[:, 16:20], in0=ep[:, 12:16], in1=mhalf[:, 0:4], op=ALU.pow),
             reads=[b_ep1, cx.B("mhalf")], writes=[b_ep1])
        yi = (sb * 4 + s) % 2
        ybt = yb[yi]

        def part_b():
            for r in range(4):
                S.op("dve", lambda e, r=r: e.scalar_tensor_tensor(out=ybt[:, r, :], in0=o_t[:, r, :], scalar=ep[:, 16 + r:17 + r],
                                                                   in1=gs[p][:, r, s * 128:(s + 1) * 128], op0=ALU.mult, op1=ALU.mult),
                     reads=[b_o, b_ep1, b_gs[p]], writes=[b_yb[yi]])
        deferred.append([4, part_b])

        def finish():
            pb = next_pbank()
            pbv = banks[pb][:].bitcast(BF16)
            for r in range(4):
                S.op("pe", lambda e, r=r: e.transpose(out=pbv[:, r * 128:(r + 1) * 128], in_=ybt[:, r, :], identity=ident),
                     reads=[b_yb[yi], cx.B("ident")], writes=[bank_b[pb]])
            S.op("dve", lambda e: e.tensor_copy(out=ystage[yi], in_=pbv[:, 0:512]), reads=[bank_b[pb]], writes=[b_ys[yi]])
            S.dma("sp", lambda e: e.dma_start(out=io["yg0_src"][sb // 4][s * 128:(s + 1) * 128, (sb % 4) * 512:(sb % 4 + 1) * 512], in_=ystage[yi]),
                  ysem[yi], reads=[b_ys[yi]], writes=[b_ywr[yi]])
            if sb % 4 == 3 and s == 3:
                g = sb // 4
                S.dma("pool", lambda e: e.collective_compute("AllGather", ALU.bypass, replica_groups=PAIRS,
                                                             ins=[io["yg0_src"][g].ap().opt()], outs=[io["yg0_all"][g].ap().opt()]),
                      io["ccsem"], reads=[b_ywr[0], b_ywr[1]], writes=[io["b_cc0"][g]], inc=1)
        deferred.append([12, finish])

    def tiles_of(sb):
        tl = []
        for s in range(4):
            jmin = max(0, 4 * sb - SLOT_BACK[s])
            for c in range(2):
                for idx, j in enumerate(range(jmin, 4 * sb + 4)):
                    tl.append([sb, s, c, idx, j])
        return tl

    for tb in range(min(NXB, ntb)):
        load_x(tb)
    for f in proj_chunks(0):
        f()
    precast = ["wout0", "wq1", "wk1", "wv1", "wg1", "wout1"]
    for sb in range(nsb):
        if (sb >= 5 or sb == nsb - 1) and precast:
            for nme in ([precast.pop(0)] if sb < nsb - 1 else list(precast)):
                sem = S.new_dma_sem()
                S.dma("pool", lambda e, nme=nme: e.dma_start(out=io[nme + "_bf"][:, :], in_=io[nme]), sem)
            if sb == nsb - 1:
                precast = []
        tiles = tiles_of(sb)
        chunks = proj_chunks(sb + 1) if sb + 1 < nsb else []
        nt = len(tiles)
        emit_qk(tiles[0])
        if nt > 1:
            emit_qk(tiles[1])
        done_chunks = 0
        for i, tile in enumerate(tiles):
            if i + 2 < nt:
                emit_qk(tiles[i + 2])
            emit_exp_pv(tile)
            last_of_group = (i + 1 == nt) or (tiles[i + 1][1] != tile[1]) or (tiles[i + 1][2] != tile[2])
            if last_of_group:
                if tile[2] == 0:
                    epi_c0(sb, tile[1])
                else:
                    epi_c1(sb, tile[1])
            for dfr in deferred:
                dfr[0] -= 1
            deferred.sort(key=lambda d: d[0])
            while deferred and deferred[0][0] <= 0:
                deferred.pop(0)[1]()
            want = (len(chunks) * (i + 1)) // nt
            while done_chunks < want:
                chunks[done_chunks]()
                done_chunks += 1
    deferred.sort(key=lambda d: d[0])
    while deferred:
        deferred.pop(0)[1]()


def l0_consts(hh):
    kb = np.zeros((128, 4, 68), np.float32)
    dg = np.zeros((128, 4, 128), np.float32)
    cb = np.zeros((128, 4, 4), np.float32)
    ki = np.arange(128, dtype=np.float64)[:, None]
    qi = np.arange(128, dtype=np.float64)[None, :]
    allowed = (ki // 64) <= (qi // 64)
    for s in range(4):
        slope = 2.0 ** -(HL[hh][s] + 1)
        t = np.arange(68, dtype=np.float64)[None, :]
        if SLOT_WIDE[s]:
            kb[:, s, :] = slope * (128.0 * (3 - t) + ki - 256.0)
            for r in range(4):
                cb[:, s, r] = slope * (128.0 * r - 192.0)
        else:
            kb[:, s, :] = slope * (-128.0 * t + ki - 64.0)
        d = -slope * np.abs(qi - ki) + slope * (qi - 64.0)
        dg[:, s, :] = np.where(allowed, d, NEG_BIG)
    return kb.reshape(128, 4 * 68), dg.reshape(128, 4 * 128), cb.reshape(128, 16)


def rep128(v):
    return np.ascontiguousarray(np.broadcast_to(np.asarray(v, np.float32).reshape(1, -1), (128, v.size)))


def head_cols(hh, width):
    return np.concatenate([np.arange(h * width, (h + 1) * width) for h in HL[hh]])


def phase_outproj(cx, io, layer, E, ntok=S_LEN):
    EC = E // 128
    if True:
        S = cx.S
        ccol_d = io["ccol"]
        adaw_d, adab_d, pg_d = io["adaw%d" % layer][:, 2048:3072], io["adab%d" % layer][:, 2048:3072], io["post%d" % layer]
        w_d = io["wout%d_bf" % layer][:, :]
        x_d, xo_d = io["op_x%d" % layer], io["op_xo%d" % layer]
        ysrc = io["op_ysrc%d" % layer]
        banks, bank_b = cx.banks, cx.bank_b
        wbf = cx.sb("wbf", [128, EC, D], BF16)
        GP = cx.sb("GP", [128, D], F32)
        NX = 4
        yT = [cx.sb("yT%d" % i, [128, EC, 512], BF16) for i in range(2)]
        xt = [cx.sb("xt%d" % i, [128, D], F32) for i in range(NX)]
        tmp = [cx.sb("tmp%d" % i, [128, D], F32) for i in range(2)]
        sq = cx.sb("sq", [128, D], F32)
        b_sq = Buf()
        xo = [cx.sb("xo%d" % i, [128, D], F32) for i in range(NX)]
        stat = [cx.sb("stat%d" % i, [128, 4], F32) for i in range(2)]
        mhalf = cx.sb("mhalf", [128, 1], F32)
        b_w, b_GP, b_tg = Buf(), Buf(), Buf()
        b_tmp = [Buf(), Buf()]
        b_yT, b_stat = [Buf(), Buf()], [Buf(), Buf()]
        b_xt, b_xo = [Buf() for _ in range(NX)], [Buf() for _ in range(NX)]
        cx.load_const(GP, io["adaG%d" % layer][:, :], b_GP)
        wsem = S.new_dma_sem()
        wv = w_d.rearrange("(ec p) n -> p ec n", p=128)
        for ec0 in range(0, EC, 4):
            S.dma("sp", lambda e, ec0=ec0: e.dma_start(out=wbf[:, ec0:ec0 + 4, :], in_=wv[:, ec0:ec0 + 4, :]), wsem, writes=[b_w])
        S.op("pool", lambda e: e.memset(mhalf[:], -0.5), writes=[cx.B("mhalf")])

        ysem = [S.new_dma_sem(), S.new_dma_sem()]
        xsem = [S.new_dma_sem() for _ in range(NX)]
        osem = [S.new_dma_sem() for _ in range(NX)]
        out_toks = {}
        nblk = ntok // 128

        def emit_loads(tb):
            if tb >= nblk:
                return
            if tb % 4 == 0:
                g4 = tb // 4
                yi = g4 % 2
                y_ap, y_buf = ysrc(g4)
                S.dma("sp", lambda e: e.dma_start(out=yT[yi][:], in_=y_ap), ysem[yi], reads=[y_buf], writes=[b_yT[yi]])
            xi = tb % NX
            S.dma("sp", lambda e: e.dma_start(out=xt[xi][:], in_=x_d[tb * 128:(tb + 1) * 128, :]), xsem[xi], writes=[b_xt[xi]])

        for tb in range(3):
            emit_loads(tb)
        for tb in range(nblk):
            emit_loads(tb + 3)
            g4, t = tb // 4, tb % 4
            yi = g4 % 2
            xi = tb % NX
            pi = tb % 2
            bk = (4 * pi, 4 * pi + 1)
            for half in range(2):
                for ec in range(EC):
                    S.op("pe", lambda e, half=half, ec=ec, yi=yi, t=t, bk=bk: e.matmul(
                        banks[bk[half]][:], lhsT=yT[yi][:, ec, t * 128:(t + 1) * 128], rhs=wbf[:, ec, half * 512:(half + 1) * 512],
                        start=(ec == 0), stop=(ec == EC - 1)), reads=[b_yT[yi], b_w], writes=[bank_b[bk[half]]])
            stt = stat[pi]
            for half in range(2):
                S.op("act", lambda e, half=half, bk=bk, stt=stt: e.activation(
                    out=sq[:, half * 512:(half + 1) * 512], in_=banks[bk[half]][:], func=AF.Square,
                    accum_out=stt[:, half:half + 1]), reads=[bank_b[bk[half]]], writes=[b_sq, b_stat[pi]])
            S.op("dve", lambda e, stt=stt: e.tensor_tensor(out=stt[:, 2:3], in0=stt[:, 0:1], in1=stt[:, 1:2], op=ALU.add),
                 reads=[b_stat[pi]], writes=[b_stat[pi]])
            S.op("dve", lambda e, stt=stt: e.tensor_scalar(out=stt[:, 2:3], in0=stt[:, 2:3], scalar1=1.0 / D, scalar2=1e-6, op0=ALU.mult, op1=ALU.add),
                 reads=[b_stat[pi]], writes=[b_stat[pi]])
            S.op("pool", lambda e, stt=stt: e.tensor_tensor(out=stt[:, 3:4], in0=stt[:, 2:3], in1=mhalf[:], op=ALU.pow),
                 reads=[b_stat[pi], cx.B("mhalf")], writes=[b_stat[pi]])
            for half in range(2):
                S.op("dve", lambda e, half=half, bk=bk, stt=stt, pi=pi: e.scalar_tensor_tensor(
                    out=tmp[pi][:, half * 512:(half + 1) * 512], in0=banks[bk[half]][:], scalar=stt[:, 3:4], in1=GP[:, half * 512:(half + 1) * 512],
                    op0=ALU.mult, op1=ALU.mult), reads=[bank_b[bk[half]], b_stat[pi], b_GP], writes=[b_tmp[pi]])
            S.op("pool", lambda e, xi=xi, pi=pi: e.tensor_tensor(out=xo[xi][:], in0=tmp[pi][:], in1=xt[xi][:], op=ALU.add),
                 reads=[b_tmp[pi], b_xt[xi]], writes=[b_xo[xi]])
            out_toks[xi] = S.dma("sp", lambda e, tb=tb, xi=xi: e.dma_start(out=xo_d[tb * 128:(tb + 1) * 128, :], in_=xo[xi][:]), osem[xi],
                                 reads=[b_xo[xi]])
        return list(out_toks.values())


RET_HEADS = 4


def ret_gamma(h):
    return float(np.float32(1.0) - np.float32(2.0) ** np.float32(-5.0 - h))


def phase_l1(cx, io, nsb=NSB):
    S = cx.S
    x_d, ccol_d = io["x1"], io["ccol"]
    adaw_d, adab_d, pg_d = io["adaw1"][:, 0:2048], io["adab1"][:, 0:2048], io["pre1"]
    banks, bank_b = cx.banks, cx.bank_b
    wq = cx.sb("wq", [128, 8, 512], BF16)
    wk = cx.sb("wk", [128, 8, 512], BF16)
    wv = cx.sb("wv", [128, 8, 1024], BF16)
    wg = cx.sb("wg", [128, 8, 1024], BF16)
    A_t = cx.sb("A_t", [128, D], F32)
    B_t = cx.sb("B_t", [128, D], F32)
    dmt = cx.sb("dmt", [128, 2, 128], F32)
    qd = cx.sb("qd", [128, 2, 512], F32)
    kdg = cx.sb("kdg", [128, 4], F32)
    ident = cx.sb("ident", [128, 128], BF16)
    mhalf = cx.sb("mhalf", [128, 4], F32)
    b_wq, b_wk, b_wv, b_wg, b_A, b_B = Buf(), Buf(), Buf(), Buf(), Buf(), Buf()
    cx.load_const(dmt.rearrange("p a b -> p (a b)"), io["dmt"], cx.B("dmt"))
    cx.load_const(qd.rearrange("p a b -> p (a b)"), io["qd"], cx.B("qd"))
    cx.load_const(kdg, io["kdg"], cx.B("kdg"))
    cx.load_const(ident, io["ident"], cx.B("ident"))
    for (wt, wd, bw) in ((wq, io["wq1_bf"], b_wq), (wk, io["wk1_bf"], b_wk), (wv, io["wv1_bf"], b_wv), (wg, io["wg1_bf"], b_wg)):
        sem = S.new_dma_sem()
        wvv = wd[:, :].rearrange("(dc p) n -> p dc n", p=128)
        for dc0 in range(0, 8, 4):
            S.dma("sp", lambda e, wt=wt, wvv=wvv, dc0=dc0: e.dma_start(out=wt[:, dc0:dc0 + 4, :], in_=wvv[:, dc0:dc0 + 4, :]), sem, writes=[bw])
    S.op("pool", lambda e: e.memset(mhalf, -0.5), writes=[cx.B("mhalf")])

    cx.load_const(A_t, io["adaA1"][:, :], b_A)
    cx.load_const(B_t, io["adaB1"][:, :], b_B)

    NXB = 4
    xbuf = cx.sb("xbuf", [128, NXB, 1024], F32)
    hT = [cx.sb("hT%d" % p, [128, 8, 512], BF16) for p in range(2)]
    hb = [cx.sb("hb%d" % i, [128, 1024], BF16) for i in range(NXB)]
    tmp = cx.sb("tmp", [128, 1024], F32)
    tg = cx.sb("tg", [128, 512], F32)
    qT = [cx.sb("qT%d" % p, [128, 4, 512], BF16) for p in range(2)]
    qdT = [cx.sb("qdT%d" % p, [128, 4, 512], BF16) for p in range(2)]
    kT = [cx.sb("kT%d" % p, [128, 4, 512], BF16) for p in range(2)]
    kd = [cx.sb("kd%d" % i, [128, 512], BF16) for i in range(2)]
    vt = [cx.sb("vt%d" % i, [128, 1024], BF16) for i in range(2)]
    gs = [cx.sb("gs%d" % i, [128, 1024], BF16) for i in range(3)]
    S32 = cx.sb("S32", [128, 2, 2, 512], F32)
    Sbf = cx.sb("Sbf", [128, 2, 2, 512], BF16)
    MT = [cx.sb("MT%d" % i, [128, 128], BF16) for i in range(2)]
    stat = [cx.sb("stat%d" % i, [128, 4], F32) for i in range(NXB)]
    ep = [[cx.sb("ep%d%d" % (i, hl), [128, 4], F32) for hl in range(2)] for i in range(2)]
    sq = cx.sb("sq", [128, 512], F32)
    yb = [[cx.sb("yb%d%d" % (i, hl), [128, 512], BF16) for hl in range(2)] for i in range(2)]
    ystage = [cx.sb("ystage%d" % i, [128, 8, 512], BF16) for i in range(2)]
    b_hT, b_qT, b_qdT, b_kT = [Buf(), Buf()], [Buf(), Buf()], [Buf(), Buf()], [Buf(), Buf()]
    b_tmp, b_tg, b_sq = Buf(), Buf(), Buf()
    b_x, b_hb, b_stat = [Buf() for _ in range(NXB)], [Buf() for _ in range(NXB)], [Buf() for _ in range(NXB)]
    b_kd, b_vt = [Buf(), Buf()], [Buf(), Buf()]
    b_gs = [Buf(), Buf(), Buf()]
    b_S32 = [[Buf(), Buf()], [Buf(), Buf()]]
    b_Sbf = [[Buf(), Buf()], [Buf(), Buf()]]
    b_MT = [Buf(), Buf()]
    b_ep = [[Buf(), Buf()], [Buf(), Buf()]]
    b_yb = [[Buf(), Buf()], [Buf(), Buf()]]
    b_ys, b_ywr = [Buf(), Buf()], [Buf(), Buf()]
    xsem = [S.new_dma_sem() for _ in range(NXB)]
    ysem = [S.new_dma_sem(), S.new_dma_sem()]
    for hl in range(2):
        for dk in range(2):
            S.op("pool", lambda e, hl=hl, dk=dk: e.memset(S32[:, hl, dk, :], 0.0), writes=[b_S32[hl][dk]])
            S.op("pool", lambda e, hl=hl, dk=dk: e.memset(Sbf[:, hl, dk, :], 0.0), writes=[b_Sbf[hl][dk]])

    pb_i = [0]

    PROT = (0, 1, 2, 6, 7)

    def next_pbank():
        b = PROT[pb_i[0] % 5]
        pb_i[0] += 1
        return b

    ob_i = [0]
    ntb = 4 * nsb

    def load_x(tb):
        xi = tb % NXB
        S.dma("sp", lambda e: e.dma_start(out=xbuf[:, xi, :], in_=x_d[tb * 128:(tb + 1) * 128, :]), xsem[xi], writes=[b_x[xi]])

    def chunk_norm(tb):
        def f():
            xi = tb % NXB
            xt = xbuf[:, xi, :]
            stt = stat[xi]
            S.op("dve", lambda e: e.scalar_tensor_tensor(out=tmp, in0=xt, scalar=1.0, in1=xt, op0=ALU.mult, op1=ALU.mult, accum_out=stt[:, 0:1]),
                 reads=[b_x[xi]], writes=[b_tmp, b_stat[xi]])
            S.op("dve", lambda e: e.tensor_scalar(out=stt[:, 1:2], in0=stt[:, 0:1], scalar1=1.0 / D, scalar2=1e-6, op0=ALU.mult, op1=ALU.add),
                 reads=[b_stat[xi]], writes=[b_stat[xi]])
            S.op("pool", lambda e: e.tensor_tensor(out=stt[:, 2:3], in0=stt[:, 1:2], in1=mhalf[:, 0:1], op=ALU.pow),
                 reads=[b_stat[xi], cx.B("mhalf")], writes=[b_stat[xi]])
            S.op("dve", lambda e: e.scalar_tensor_tensor(out=tmp, in0=xt, scalar=stt[:, 2:3], in1=A_t, op0=ALU.mult, op1=ALU.mult),
                 reads=[b_x[xi], b_stat[xi], b_A], writes=[b_tmp])
            S.op("pool", lambda e: e.tensor_tensor(out=hb[xi], in0=tmp, in1=B_t, op=ALU.add), reads=[b_tmp, b_B], writes=[b_hb[xi]])
            if tb + NXB < ntb:
                load_x(tb + NXB)
        return f

    def chunk_tr(tb):
        def f():
            xi, t, p = tb % NXB, tb % 4, (tb // 4) % 2
            pb = next_pbank()
            pbv = banks[pb][:].bitcast(BF16)
            for dc in range(8):
                S.op("pe", lambda e, dc=dc: e.transpose(out=pbv[:, dc * 128:(dc + 1) * 128], in_=hb[xi][:, dc * 128:(dc + 1) * 128], identity=ident),
                     reads=[b_hb[xi], cx.B("ident")], writes=[bank_b[pb]])
            S.op("act", lambda e: e.copy(out=hT[p][:, :, t * 128:(t + 1) * 128], in_=pbv.rearrange("p (dc n) -> p dc n", n=128)),
                 reads=[bank_b[pb]], writes=[b_hT[p]])
        return f

    def chunk_q(sb, ch):
        def f():
            p = sb % 2
            pb = next_pbank()
            for dc in range(8):
                S.op("pe", lambda e, dc=dc: e.matmul(banks[pb][:], lhsT=wq[:, dc, ch * 128:(ch + 1) * 128], rhs=hT[p][:, dc, :], start=(dc == 0), stop=(dc == 7)),
                     reads=[b_wq, b_hT[p]], writes=[bank_b[pb]])
            S.op("act", lambda e: e.copy(out=qT[p][:, ch, :], in_=banks[pb][:]), reads=[bank_b[pb]], writes=[b_qT[p]])
            S.op("dve", lambda e: e.tensor_tensor(out=qdT[p][:, ch, :], in0=banks[pb][:], in1=qd[:, ch // 2, :], op=ALU.mult),
                 reads=[bank_b[pb], cx.B("qd")], writes=[b_qdT[p]])
        return f

    def chunk_k(sb, ch):
        def f():
            p = sb % 2
            pb = next_pbank()
            for dc in range(8):
                S.op("pe", lambda e, dc=dc: e.matmul(banks[pb][:], lhsT=wk[:, dc, ch * 128:(ch + 1) * 128], rhs=hT[p][:, dc, :], start=(dc == 0), stop=(dc == 7)),
                     reads=[b_wk, b_hT[p]], writes=[bank_b[pb]])
            S.op("act", lambda e: e.copy(out=kT[p][:, ch, :], in_=banks[pb][:]), reads=[bank_b[pb]], writes=[b_kT[p]])
        return f

    def sb_chunks(sb):
        tb0 = 4 * sb
        ch = [chunk_norm(tb0 + t) for t in range(4)] + [chunk_tr(tb0 + t) for t in range(4)]
        ch += [chunk_q(sb, c) for c in range(4)] + [chunk_k(sb, c) for c in range(4)]
        return ch

    def proj_tok(tb):
        p, t, bi = (tb // 4) % 2, tb % 4, tb % 2
        gi = tb % 3
        tsl = slice(t * 128, (t + 1) * 128)
        pb = next_pbank()
        pbk = banks[pb][:].bitcast(BF16)
        for ch in range(4):
            S.op("pe", lambda e, ch=ch, pbk=pbk: e.transpose(out=pbk[:, ch * 128:(ch + 1) * 128], in_=kT[p][:, ch, tsl], identity=ident),
                 reads=[b_kT[p], cx.B("ident")], writes=[bank_b[pb]])
        for hl in range(2):
            S.op("dve", lambda e, hl=hl, pbk=pbk: e.tensor_scalar(out=kd[bi][:, hl * 256:(hl + 1) * 256], in0=pbk[:, hl * 256:(hl + 1) * 256],
                                                                  scalar1=kdg[:, hl:hl + 1], scalar2=None, op0=ALU.mult),
                 reads=[bank_b[pb], cx.B("kdg")], writes=[b_kd[bi]])
        for half in range(2):
            pb = next_pbank()
            for dc in range(8):
                S.op("pe", lambda e, dc=dc, pb=pb, half=half: e.matmul(banks[pb][:], lhsT=hT[p][:, dc, tsl], rhs=wv[:, dc, half * 512:(half + 1) * 512],
                                                                         start=(dc == 0), stop=(dc == 7)),
                     reads=[b_wv, b_hT[p]], writes=[bank_b[pb]])
            S.op("act", lambda e, pb=pb, half=half: e.copy(out=vt[bi][:, half * 512:(half + 1) * 512], in_=banks[pb][:]),
                 reads=[bank_b[pb]], writes=[b_vt[bi]])
        for half in range(2):
            pb = next_pbank()
            for dc in range(8):
                S.op("pe", lambda e, dc=dc, pb=pb, half=half: e.matmul(banks[pb][:], lhsT=hT[p][:, dc, tsl], rhs=wg[:, dc, half * 512:(half + 1) * 512],
                                                                         start=(dc == 0), stop=(dc == 7)),
                     reads=[b_wg, b_hT[p]], writes=[bank_b[pb]])
            S.op("act", lambda e, pb=pb: e.activation(out=tg, in_=banks[pb][:], func=AF.Tanh, scale=0.5), reads=[bank_b[pb]], writes=[b_tg])
            S.op("dve", lambda e, pb=pb, half=half: e.scalar_tensor_tensor(out=gs[gi][:, half * 512:(half + 1) * 512], in0=tg, scalar=1.0, in1=banks[pb][:],
                                                                            op0=ALU.add, op1=ALU.mult),
                 reads=[b_tg, bank_b[pb]], writes=[b_gs[gi]])

    def core(tb):
        p, t, bi = (tb // 4) % 2, tb % 4, tb % 2
        tsl = slice(t * 128, (t + 1) * 128)
        obs = []
        for hl in range(2):
            ob = 4 + ob_i[0] % 2
            ob_i[0] += 1
            obs.append(ob)
        vsl = [slice(hl * 512, (hl + 1) * 512) for hl in range(2)]

        def sc(hl):
            sreg = banks[3][:, hl * 128:(hl + 1) * 128]
            for dk in range(2):
                ch = 2 * hl + dk
                S.op("pe", lambda e, ch=ch, dk=dk: e.matmul(sreg, lhsT=kT[p][:, ch, tsl], rhs=qT[p][:, ch, tsl], start=(dk == 0), stop=(dk == 1)),
                     reads=[b_kT[p], b_qT[p]], writes=[bank_b[3]])
            S.op("dve", lambda e: e.tensor_tensor(out=MT[hl], in0=sreg, in1=dmt[:, hl, :], op=ALU.mult), reads=[bank_b[3], cx.B("dmt")], writes=[b_MT[hl]])

        dsb = {}

        def dS(hl):
            for dk in range(2):
                bk = next_pbank()
                dsb[(hl, dk)] = bk
                S.op("pe", lambda e, dk=dk, bk=bk: e.matmul(banks[bk][:], lhsT=kd[bi][:, hl * 256 + dk * 128:hl * 256 + (dk + 1) * 128], rhs=vt[bi][:, vsl[hl]],
                                                             start=True, stop=True),
                     reads=[b_kd[bi], b_vt[bi]], writes=[bank_b[bk]])

        def upd(hl):
            for dk in range(2):
                bk = dsb[(hl, dk)]
                S.op("dve", lambda e, dk=dk, bk=bk: e.scalar_tensor_tensor(out=S32[:, hl, dk, :], in0=S32[:, hl, dk, :], scalar=kdg[:, 2 + hl:3 + hl],
                                                                            in1=banks[bk][:], op0=ALU.mult, op1=ALU.add),
                     reads=[b_S32[hl][dk], cx.B("kdg"), bank_b[bk]], writes=[b_S32[hl][dk]])
                S.op("act", lambda e, dk=dk: e.copy(out=Sbf[:, hl, dk, :], in_=S32[:, hl, dk, :]), reads=[b_S32[hl][dk]], writes=[b_Sbf[hl][dk]])

        def out(hl):
            ob = obs[hl]
            S.op("pe", lambda e: e.matmul(banks[ob][:], lhsT=MT[hl], rhs=vt[bi][:, vsl[hl]], start=True, stop=False),
                 reads=[b_MT[hl], b_vt[bi]], writes=[bank_b[ob]])
            for dk in range(2):
                ch = 2 * hl + dk
                S.op("pe", lambda e, dk=dk, ch=ch: e.matmul(banks[ob][:], lhsT=qdT[p][:, ch, tsl], rhs=Sbf[:, hl, dk, :], start=False, stop=(dk == 1)),
                     reads=[b_qdT[p], b_Sbf[hl][dk]], writes=[bank_b[ob]])

        def epi(hl):
            ob = obs[hl]
            epi_t = ep[bi][hl]
            S.op("act", lambda e: e.activation(out=sq, in_=banks[ob][:], func=AF.Square, accum_out=epi_t[:, 0:1]),
                 reads=[bank_b[ob]], writes=[b_sq, b_ep[bi][hl]])
            S.op("dve", lambda e: e.tensor_scalar(out=epi_t[:, 1:2], in0=epi_t[:, 0:1], scalar1=1.0 / 128, scalar2=4e-5, op0=ALU.mult, op1=ALU.add),
                 reads=[b_ep[bi][hl]], writes=[b_ep[bi][hl]])
            S.op("pool", lambda e: e.tensor_tensor(out=epi_t[:, 2:3], in0=epi_t[:, 1:2], in1=mhalf[:, 0:1], op=ALU.pow),
                 reads=[b_ep[bi][hl], cx.B("mhalf")], writes=[b_ep[bi][hl]])
            S.op("dve", lambda e: e.scalar_tensor_tensor(out=yb[bi][hl], in0=banks[ob][:], scalar=epi_t[:, 2:3], in1=gs[tb % 3][:, vsl[hl]],
                                                          op0=ALU.mult, op1=ALU.mult),
                 reads=[bank_b[ob], b_ep[bi][hl], b_gs[tb % 3]], writes=[b_yb[bi][hl]])

        sc(0)
        dS(0)
        sc(1)
        out(0)
        upd(0)
        epi(0)
        dS(1)
        out(1)
        upd(1)
        epi(1)

    def transposes(tb):
        bi, t, yi = tb % 2, tb % 4, (tb // 4) % 2
        tsl = slice(t * 128, (t + 1) * 128)
        for hl in range(2):
            pb = next_pbank()
            pbv = banks[pb][:].bitcast(BF16)
            for ec in range(4):
                S.op("pe", lambda e, ec=ec, hl=hl, pbv=pbv: e.transpose(out=pbv[:, ec * 128:(ec + 1) * 128], in_=yb[bi][hl][:, ec * 128:(ec + 1) * 128],
                                                                         identity=ident),
                     reads=[b_yb[bi][hl], cx.B("ident")], writes=[bank_b[pb]])
            S.op("act", lambda e, hl=hl, pbv=pbv: e.copy(out=ystage[yi][:, hl * 4:(hl + 1) * 4, tsl], in_=pbv[:, 0:512].rearrange("p (c n) -> p c n", n=128)),
                 reads=[bank_b[pb]], writes=[b_ys[yi]])
        if t == 3:
            sb = tb // 4
            S.dma("sp", lambda e: e.dma_start(
                out=io["yg1_src"][sb // 2][:, :].rearrange("(c p) t -> p c t", p=128)[:, :, (sb % 2) * 512:(sb % 2 + 1) * 512],
                in_=ystage[yi]), ysem[yi], reads=[b_ys[yi]], writes=[b_ywr[yi]])
            if sb % 2 == 1:
                g = sb // 2
                S.dma("pool", lambda e: e.collective_compute("AllGather", ALU.bypass, replica_groups=PAIRS,
                                                             ins=[io["yg1_src"][g].ap().opt()], outs=[io["yg1_all"][g].ap().opt()]),
                      io["ccsem"], reads=[b_ywr[0], b_ywr[1]], writes=[io["b_cc1"][g]], inc=1)

    for tb in range(min(NXB, ntb)):
        load_x(tb)
    for f in sb_chunks(0):
        f()
    if nsb > 1:
        for t in range(4):
            chunk_norm(4 + t)()
    proj_tok(0)

    def sched(sb, t):
        out = []
        n1, n2 = sb + 1, sb + 2
        if n1 < nsb:
            if t == 0:
                out += [chunk_tr(4 * n1 + i) for i in range(4)]
            elif t == 1:
                out += [chunk_k(n1, 0), chunk_k(n1, 1), chunk_q(n1, 0), chunk_q(n1, 1)]
            elif t == 2:
                out += [chunk_k(n1, 2), chunk_k(n1, 3), chunk_q(n1, 2), chunk_q(n1, 3)]
        if n2 < nsb:
            if t == 1:
                out += [chunk_norm(4 * n2), chunk_norm(4 * n2 + 1)]
            elif t == 2:
                out += [chunk_norm(4 * n2 + 2), chunk_norm(4 * n2 + 3)]
        return out

    for tb in range(ntb):
        sb, t = tb // 4, tb % 4
        for f in sched(sb, t):
            f()
        if tb + 1 < ntb:
            proj_tok(tb + 1)
        if tb >= 1:
            transposes(tb - 1)
        core(tb)
    transposes(ntb - 1)


def l1_consts(hh):
    dmt = np.zeros((128, 2, 128), np.float32)
    qd = np.zeros((128, 2, 512), np.float32)
    kdg = np.zeros((128, 4), np.float32)
    j = np.arange(128, dtype=np.float64)[:, None]
    i = np.arange(128, dtype=np.float64)[None, :]
    for hl in range(2):
        lg = math.log(ret_gamma(2 * hh + hl))
        same = (j // 64) == (i // 64)
        later = (i // 64) > (j // 64)
        m = np.where(same, np.exp(lg * np.abs(i - j)), np.where(later, np.exp(lg * (i - j)), 0.0))
        dmt[:, hl, :] = m / 16.0
        qd[:, hl, :] = np.tile(np.exp(lg * np.arange(128, dtype=np.float64)), 4)[None, :]
        kdg[:, hl] = np.exp(lg * (128.0 - np.arange(128, dtype=np.float64))) / 16.0
        kdg[:, 2 + hl] = math.exp(lg * 128.0)
    return dmt.reshape(128, 256), qd.reshape(128, 1024), kdg


def build_fused(nsb=NSB):
    nc = bass.Bass("TRN2", target_bir_lowering=False)
    with ExitStack() as st:
        cx = Ctx(nc, st)
        S = cx.S
        io = {}
        shapes = {
            "x": [S_LEN, D], "ccol": [128, 8],
            "adaw0": [D, 3072], "adab0": [128, 3072], "adaw1": [D, 3072], "adab1": [128, 3072],
            "pre0": [128, D], "post0": [128, D], "pre1": [128, D], "post1": [128, D],
            "wq0": [D, 512], "wk0": [D, 512], "wv0": [D, 512], "wg0": [D, 512],
            "lamv": [128, 256], "subg": [128, 512], "kbtab": [128, 4 * 68], "dg0": [128, 4 * 128], "cbt": [128, 16],
            "wout0": [1024, D],
            "wq1": [D, 512], "wk1": [D, 512], "wv1": [D, 1024], "wg1": [D, 1024],
            "dmt": [128, 256], "qd": [128, 1024], "kdg": [128, 4], "wout1": [2048, D],
        }
        for n, shp in shapes.items():
            io[n] = cx.din(n, shp)
        io["ident"] = cx.din("ident", [128, 128], BF16)
        xo_d = cx.dout("xo", [S_LEN, D])
        io["yg0_src"] = [cx.dscratch("yg0_src%d" % g, [512, 2048], BF16) for g in range(4)]
        io["yg0_all"] = [cx.dscratch("yg0_all%d" % g, [1024, 2048], BF16) for g in range(4)]
        io["yg1_src"] = [cx.dscratch("yg1_src%d" % g, [1024, 1024], BF16) for g in range(8)]
        io["yg1_all"] = [cx.dscratch("yg1_all%d" % g, [2048, 1024], BF16) for g in range(8)]
        io["x1"] = cx.dscratch("x1s", [S_LEN, D], F32)
        for nme in ("wout0", "wq1", "wk1", "wv1", "wg1", "wout1"):
            io[nme + "_bf"] = cx.dscratch(nme + "_bf", shapes[nme], BF16)
        for nme in ("adaA1", "adaB1", "adaG0", "adaG1"):
            io[nme] = cx.dscratch(nme, [128, D], F32)
        io["b_cc0"] = [Buf() for _ in range(4)]
        io["b_cc1"] = [Buf() for _ in range(8)]
        io["ccsem"] = S.new_dma_sem()
        S.barrier_skip.add(io["ccsem"])
        cx.alloc_banks()

        phase_l0(cx, io, nsb)
        cx.new_phase()
        io["op_x0"], io["op_xo0"] = io["x"], io["x1"]
        io["op_ysrc0"] = lambda g4: (io["yg0_all"][g4 // 4][:, :].rearrange("(ec p) t -> p ec t", p=128)[:, :, (g4 % 4) * 512:(g4 % 4 + 1) * 512],
                                     io["b_cc0"][g4 // 4])
        phase_outproj(cx, io, 0, 1024, ntok=512 * nsb)
        cx.new_phase()
        phase_l1(cx, io, nsb)
        cx.new_phase()
        io["op_x1"], io["op_xo1"] = io["x1"], xo_d
        io["op_ysrc1"] = lambda g4: (io["yg1_all"][g4 // 2][:, :].rearrange("(ec p) t -> p ec t", p=128)[:, :, (g4 % 2) * 512:(g4 % 2 + 1) * 512],
                                     io["b_cc1"][g4 // 2])
        toks = phase_outproj(cx, io, 1, 2048, ntok=512 * nsb)
        S.wait_all("sp", toks)
        counts = S.emit(st)
    return nc, counts


_CACHE = {}


def make_in_maps(inp):
    ident = np.eye(128, dtype=np.float32).astype(ml_dtypes.bfloat16)
    w_in0, w_in1 = inp["da_w_in"][0], inp["ret_w_in"][0]
    lamv = np.concatenate([rep128(inp["da_lambda_q1"][0]), rep128(inp["da_lambda_k1"][0]),
                           rep128(inp["da_lambda_q2"][0]), rep128(inp["da_lambda_k2"][0])], axis=1)
    subg = np.ascontiguousarray(np.tile(rep128(inp["da_subln_gain"][0]), (1, 4)))
    wout0_rows = np.concatenate([np.arange(h * 128, (h + 1) * 128) for h in HL[0] + HL[1]])
    shared = {
        "adaw0": np.ascontiguousarray(inp["ada_w"][0]), "adab0": rep128(inp["ada_b"][0]),
        "adaw1": np.ascontiguousarray(inp["ada_w"][1]), "adab1": rep128(inp["ada_b"][1]),
        "pre0": rep128(inp["pre_gain"][0]), "post0": rep128(inp["post_gain"][0]),
        "pre1": rep128(inp["pre_gain"][1]), "post1": rep128(inp["post_gain"][1]),
        "lamv": lamv, "subg": subg, "ident": ident,
        "wout0": np.ascontiguousarray(inp["da_w_out"][0][wout0_rows]),
        "wout1": np.ascontiguousarray(inp["ret_w_out"][0]),
    }
    in_maps = []
    for core in range(8):
        b, hh = core // 2, core % 2
        hc = head_cols(hh, 128)
        kb, dg, cb = l0_consts(hh)
        dmt, qd, kdg = l1_consts(hh)
        qc = np.arange(hh * 512, (hh + 1) * 512)
        vc = np.arange(hh * 1024, (hh + 1) * 1024)
        m = dict(shared)
        m.update({
            "x": np.ascontiguousarray(inp["x"][b], dtype=np.float32),
            "ccol": np.ascontiguousarray(inp["c"][b].reshape(8, 128).T),
            "wq0": np.ascontiguousarray(w_in0[:, 0 + hc]), "wk0": np.ascontiguousarray(w_in0[:, 1024 + hc]),
            "wv0": np.ascontiguousarray(w_in0[:, 2048 + hc]), "wg0": np.ascontiguousarray(w_in0[:, 3072 + hc]),
            "kbtab": kb, "dg0": dg, "cbt": cb,
            "wq1": np.ascontiguousarray(w_in1[:, qc]), "wk1": np.ascontiguousarray(w_in1[:, 1024 + qc]),
            "wv1": np.ascontiguousarray(w_in1[:, 2048 + vc]), "wg1": np.ascontiguousarray(w_in1[:, 4096 + vc]),
            "dmt": dmt, "qd": qd, "kdg": kdg,
        })
        in_maps.append(m)
    return in_maps


def kernel(**inp):
    inp = {k: np.asarray(v) for k, v in inp.items()}
    if "fused" not in _CACHE:
        _CACHE["fused"] = build_fused()
    nc, _ = _CACHE["fused"]
    res = run_bass_kernel_spmd(nc, make_in_maps(inp), core_ids=list(range(8)))
    out = np.empty((4, S_LEN, D), np.float32)
    for b in range(4):
        out[b, :4096] = res.results[2 * b]["xo"][:4096]
        out[b, 4096:] = res.results[2 * b + 1]["xo"][4096:]
    return out
```

```python
import math
from contextlib import ExitStack

import numpy as np
import ml_dtypes

import concourse.bass as bass
import concourse.mybir as mybir
from concourse.bass_utils import run_bass_kernel_spmd

F32 = mybir.dt.float32
BF16 = mybir.dt.bfloat16
ALU = mybir.AluOpType
AF = mybir.ActivationFunctionType
ENGS = ("pe", "act", "dve", "pool", "sp")

S_LEN = 8192
D = 1024
NB = S_LEN // 128
NSB = S_LEN // 512
HL = ((7, 5, 3, 1), (6, 4, 2, 0))
T_SKIP = 40.0
SLOT_SLOPE_MIN = [min(2.0 ** -(HL[0][s] + 1), 2.0 ** -(HL[1][s] + 1)) for s in range(4)]
SLOT_WIDE = [max(2.0 ** -(HL[0][s] + 1), 2.0 ** -(HL[1][s] + 1)) <= 0.125 for s in range(4)]
SLOT_BACK = [min(NB, int((T_SKIP / SLOT_SLOPE_MIN[s] + 127) // 128)) for s in range(4)]
SLOT_RING = [min(NB, ((SLOT_BACK[s] + 4 + 3) // 4) * 4) for s in range(4)]
NEG_BIG = -30000.0
LAM_INIT0 = 0.8 - 0.6 * math.exp(-0.3 * 0)


class Buf:
    __slots__ = ("name", "w", "r", "excl")

    def __init__(self, name="", excl=False):
        self.name = name
        self.w = None
        self.r = []
        self.excl = excl


class Sched:
    def __init__(self, nc):
        self.nc = nc
        self.ops = {e: [] for e in ENGS}
        self.waited = {e: {} for e in ENGS}
        self.dma_sems = []
        self.barrier_skip = set()

    def _deps(self, eng, reads, writes):
        deps = []
        for b in list(reads) + list(writes):
            if b.w is not None:
                deps.append((b.w, "raw"))
        for b in writes:
            for t in b.r:
                deps.append((t, "war"))
        for b in reads:
            if b.excl:
                for t in b.r:
                    deps.append((t, "war"))
        wd = self.waited[eng]
        best = {}
        for t, kind in deps:
            if t[0] == "e" and t[1] == eng and eng == "pe":
                continue
            key = (t[0], t[1])
            if wd.get(key, -1) >= t[2]:
                continue
            if key not in best or best[key][2] < t[2]:
                best[key] = t
        waits = list(best.values())
        for t in waits:
            wd[(t[0], t[1])] = t[2]
            if t[0] == "e":
                self.ops[t[1]][t[2]]["marked"] = True
        return waits

    def op(self, eng, fn, reads=(), writes=()):
        waits = self._deps(eng, reads, writes)
        idx = len(self.ops[eng])
        self.ops[eng].append({"waits": waits, "fn": fn, "marked": False, "dma": None})
        tok = ("e", eng, idx)
        for b in reads:
            b.r.append(tok)
        for b in writes:
            b.w = tok
            b.r = []
        return tok

    def new_dma_sem(self):
        self.dma_sems.append(0)
        return len(self.dma_sems) - 1

    def dma(self, queue, fn, sem, reads=(), writes=(), inc=16):
        waits = self._deps(queue, reads, writes)
        self.dma_sems[sem] += inc
        tok = ("d", sem, self.dma_sems[sem])
        self.ops[queue].append({"waits": waits, "fn": fn, "marked": False, "dma": sem, "inc": inc})
        for b in reads:
            b.r.append(tok)
        for b in writes:
            b.w = tok
            b.r = []
        return tok

    def wait_all(self, eng, toks):
        for t in toks:
            if t[0] == "e":
                self.ops[t[1]][t[2]]["marked"] = True
        self.ops[eng].append({"waits": list(toks), "fn": None, "marked": False, "dma": None})

    def barrier(self):
        last = {}
        for e in ENGS:
            for i in range(len(self.ops[e]) - 1, -1, -1):
                o = self.ops[e][i]
                if o["dma"] is None and o["fn"] is not None:
                    last[e] = ("e", e, i)
                    break
        for e in ENGS:
            toks = [t for k, t in last.items() if k != e]
            toks += [("d", i, v) for i, v in enumerate(self.dma_sems) if v > 0 and i not in self.barrier_skip]
            wd = self.waited[e]
            toks = [t for t in toks if wd.get((t[0], t[1]), -1) < t[2]]
            for t in toks:
                wd[(t[0], t[1])] = t[2]
            self.wait_all(e, toks)

    def emit(self, stack):
        nc = self.nc
        esem = {e: stack.enter_context(nc.semaphore("s_" + e)) for e in ENGS}
        dsem = [stack.enter_context(nc.semaphore("d%d" % i)) for i in range(len(self.dma_sems))]
        pref = {}
        for e in ENGS:
            c = 0
            arr = []
            for o in self.ops[e]:
                if o["marked"]:
                    c += 1
                arr.append(c)
            pref[e] = arr
        block = stack.enter_context(nc.Block())

        def run(e_name):
            def body(e):
                for o in self.ops[e_name]:
                    for t in o["waits"]:
                        if t[0] == "e":
                            e.wait_ge(esem[t[1]], pref[t[1]][t[2]])
                        else:
                            e.wait_ge(dsem[t[1]], t[2])
                    if o["fn"] is None:
                        continue
                    ins = o["fn"](e)
                    if o["dma"] is not None:
                        ins.then_inc(dsem[o["dma"]], o["inc"])
                    elif o["marked"]:
                        ins.then_inc(esem[e_name], 1)
            return body

        block.tensor(run("pe"))
        block.scalar(run("act"))
        block.vector(run("dve"))
        block.gpsimd(run("pool"))
        block.sync(run("sp"))
        return {e: len(self.ops[e]) for e in ENGS}


class Ctx:
    ARENA_F32 = 47 * 1024

    def __init__(self, nc, st):
        self.nc = nc
        self.st = st
        self.S = Sched(nc)
        self.bufs = {}
        self.banks = []
        self.bank_b = []
        self.arena = st.enter_context(nc.sbuf_tensor("arena", [128, self.ARENA_F32], F32))
        self.off = 0
        self.phase = 0

    def new_phase(self):
        self.S.barrier()
        self.off = 0
        self.bufs = {}
        self.phase += 1

    def sb(self, name, shape, dt):
        n = 1
        for d in shape[1:]:
            n *= d
        nbytes = n * (2 if dt == BF16 else 4)
        nf = (nbytes + 31) // 32 * 8
        assert self.off + nf <= self.ARENA_F32, ("SBUF arena overflow", name, self.off, nf)
        ap = self.arena[:, self.off:self.off + nbytes // 4]
        self.off += nf
        if dt == BF16:
            ap = ap.bitcast(BF16)
        if len(shape) == 3:
            ap = ap.rearrange("p (a b) -> p a b", a=shape[1])
        elif len(shape) == 4:
            ap = ap.rearrange("p (a b c) -> p a b c", a=shape[1], b=shape[2])
        return ap

    def alloc_banks(self):
        for i in range(8):
            self.banks.append(self.st.enter_context(self.nc.psum_tensor("bank%d" % i, [128, 512], F32)))
            self.bank_b.append(Buf("bank%d" % i, excl=True))

    def B(self, name):
        if name not in self.bufs:
            self.bufs[name] = Buf(name)
        return self.bufs[name]

    def din(self, name, shape, dt=F32):
        return self.nc.dram_tensor(name, list(shape), dt, kind="ExternalInput").ap()

    def dscratch(self, name, shape, dt=F32):
        return self.nc.dram_tensor(name, list(shape), dt)

    def dout(self, name, shape, dt=F32):
        return self.nc.dram_tensor(name, list(shape), dt, kind="ExternalOutput").ap()

    def load_const(self, tile_ap, dram_ap, buf, queue="sp"):
        sem = self.S.new_dma_sem()
        return self.S.dma(queue, lambda e: e.dma_start(out=tile_ap, in_=dram_ap), sem, writes=[buf])


def emit_adaln(cx, ccol_d, adaw_d, adab_t, adab_b, wchunk, wchunk_b, ncols, consume):
    S = cx.S
    ccol = cx.sb("ccol", [128, 8], F32)
    cth = cx.sb("cth", [128, 8], F32)
    cond = cx.sb("cond", [128, 8], F32)
    crep = cx.sb("crep", [128, 8, 128], F32)
    b_c, b_crep = cx.B("ccol"), cx.B("crep")
    cx.load_const(ccol[:], ccol_d, b_c)
    S.op("act", lambda e: e.activation(out=cth[:], in_=ccol[:], func=AF.Tanh, scale=0.5), reads=[b_c], writes=[cx.B("cth")])
    S.op("dve", lambda e: e.scalar_tensor_tensor(out=cond[:], in0=cth[:], scalar=1.0, in1=ccol[:], op0=ALU.add, op1=ALU.mult),
         reads=[cx.B("cth"), b_c], writes=[cx.B("cond")])
    S.op("dve", lambda e: e.tensor_scalar(out=cond[:], in0=cond[:], scalar1=0.5, scalar2=None, op0=ALU.mult),
         reads=[cx.B("cond")], writes=[cx.B("cond")])
    for j in range(8):
        S.op("dve", lambda e, j=j: e.tensor_copy(out=crep[:, j, :], in_=cond[:, j:j + 1].to_broadcast([128, 128])),
             reads=[cx.B("cond")], writes=[b_crep])
    wsem = S.new_dma_sem()
    adaw_v = adaw_d.rearrange("(dc p) n -> p dc n", p=128)
    for ci in range(ncols // 512):
        S.dma("sp", lambda e, ci=ci: e.dma_start(out=wchunk, in_=adaw_v[:, :, ci * 512:(ci + 1) * 512]), wsem, writes=[wchunk_b])
        bk = ci % 2 + 6
        for j in range(8):
            S.op("pe", lambda e, j=j, bk=bk: e.matmul(cx.banks[bk][:], lhsT=crep[:, j, :], rhs=wchunk[:, j, :],
                                                        start=(j == 0), stop=(j == 7)),
                 reads=[b_crep, wchunk_b], writes=[cx.bank_b[bk]])
        consume(ci, cx.banks[bk], cx.bank_b[bk])


def emit_adaln_all(cx, io, A_t, B_t, b_A, b_B):
    S = cx.S
    banks, bank_b = cx.banks, cx.bank_b
    ccol = cx.sb("ccol", [128, 8], F32)
    cth = cx.sb("cth", [128, 8], F32)
    cond = cx.sb("cond", [128, 8], F32)
    crep = cx.sb("crep", [128, 8, 128], F32)
    adab = cx.sb("adab", [128, 3072], F32)
    g_pre = cx.sb("g_pre", [128, D], F32)
    g_post = cx.sb("g_post", [128, D], F32)
    tga = cx.sb("tga", [128, 512], F32)
    stg = [cx.sb("stg%d" % i, [128, D], F32) for i in range(3)]
    wch = [cx.sb("wch%d" % i, [128, 8, 512], F32) for i in range(2)]
    b_c, b_cth, b_cond, b_crep, b_adab, b_gpre, b_gpost, b_tga = [Buf() for _ in range(8)]
    b_stg = [Buf(), Buf(), Buf()]
    b_wch = [Buf(), Buf()]
    cx.load_const(ccol, io["ccol"], b_c)
    S.op("act", lambda e: e.activation(out=cth, in_=ccol, func=AF.Tanh, scale=0.5), reads=[b_c], writes=[b_cth])
    S.op("dve", lambda e: e.scalar_tensor_tensor(out=cond, in0=cth, scalar=1.0, in1=ccol, op0=ALU.add, op1=ALU.mult),
         reads=[b_cth, b_c], writes=[b_cond])
    S.op("dve", lambda e: e.tensor_scalar(out=cond, in0=cond, scalar1=0.5, scalar2=None, op0=ALU.mult), reads=[b_cond], writes=[b_cond])
    for j in range(8):
        S.op("dve", lambda e, j=j: e.tensor_copy(out=crep[:, j, :], in_=cond[:, j:j + 1].to_broadcast([128, 128])),
             reads=[b_cond], writes=[b_crep])
    wsem = [S.new_dma_sem(), S.new_dma_sem()]
    csem = [S.new_dma_sem() for _ in range(3)]
    ssem = [S.new_dma_sem() for _ in range(3)]
    n = 0
    for layer in range(2):
        S.dma("sp", lambda e, layer=layer: e.dma_start(out=adab, in_=io["adab%d" % layer]), csem[0], writes=[b_adab])
        S.dma("sp", lambda e, layer=layer: e.dma_start(out=g_pre, in_=io["pre%d" % layer]), csem[1], writes=[b_gpre])
        S.dma("sp", lambda e, layer=layer: e.dma_start(out=g_post, in_=io["post%d" % layer]), csem[2], writes=[b_gpost])
        adaw_v = io["adaw%d" % layer].rearrange("(dc p) n -> p dc n", p=128)
        for ci in range(6):
            wi = n % 2
            bk = 6 + n % 2
            n += 1
            S.dma("sp", lambda e, ci=ci, wi=wi, adaw_v=adaw_v: e.dma_start(out=wch[wi], in_=adaw_v[:, :, ci * 512:(ci + 1) * 512]), wsem[wi], writes=[b_wch[wi]])
            for j in range(8):
                S.op("pe", lambda e, j=j, bk=bk, wi=wi: e.matmul(banks[bk][:], lhsT=crep[:, j, :], rhs=wch[wi][:, j, :], start=(j == 0), stop=(j == 7)),
                     reads=[b_crep, b_wch[wi]], writes=[bank_b[bk]])
            kind, half = ci // 2, ci % 2
            cols = slice(half * 512, half * 512 + 512)
            acols = slice(ci * 512, ci * 512 + 512)
            if kind == 0:
                dst, dbuf = (B_t, b_B) if layer == 0 else (stg[1], b_stg[1])
                S.op("dve", lambda e, bk=bk, dst=dst, cols=cols, acols=acols: e.tensor_tensor(out=dst[:, cols], in0=banks[bk][:], in1=adab[:, acols], op=ALU.add),
                     reads=[bank_b[bk], b_adab], writes=[dbuf])
            else:
                S.op("dve", lambda e, bk=bk, acols=acols: e.tensor_tensor(out=tga, in0=banks[bk][:], in1=adab[:, acols], op=ALU.add),
                     reads=[bank_b[bk], b_adab], writes=[b_tga])
                if kind == 1:
                    dst, dbuf = (A_t, b_A) if layer == 0 else (stg[0], b_stg[0])
                    S.op("dve", lambda e, dst=dst, cols=cols: e.scalar_tensor_tensor(out=dst[:, cols], in0=tga, scalar=1.0, in1=g_pre[:, cols], op0=ALU.add, op1=ALU.mult),
                         reads=[b_tga, b_gpre], writes=[dbuf])
                else:
                    S.op("dve", lambda e, cols=cols: e.tensor_tensor(out=stg[2][:, cols], in0=tga, in1=g_post[:, cols], op=ALU.mult),
                         reads=[b_tga, b_gpost], writes=[b_stg[2]])
        if layer == 1:
            S.dma("sp", lambda e: e.dma_start(out=io["adaA1"][:, :], in_=stg[0]), ssem[0], reads=[b_stg[0]])
            S.dma("sp", lambda e: e.dma_start(out=io["adaB1"][:, :], in_=stg[1]), ssem[1], reads=[b_stg[1]])
        S.dma("sp", lambda e, layer=layer: e.dma_start(out=io["adaG%d" % layer][:, :], in_=stg[2]), ssem[2], reads=[b_stg[2]])


PAIRS = [[0, 1], [2, 3], [4, 5], [6, 7]]


def phase_l0(cx, io, nsb=NSB):
    S = cx.S
    x_d, ccol_d = io["x"], io["ccol"]
    adaw_d, adab_d, pg_d = io["adaw0"][:, 0:2048], io["adab0"][:, 0:2048], io["pre0"]
    w_d = [io[n] for n in ("wq0", "wk0", "wv0", "wg0")]
    banks, bank_b = cx.banks, cx.bank_b
    ring = [min(NB, SLOT_RING[s] + 4) for s in range(4)]
    ring_off = [0]
    for s in range(4):
        ring_off.append(ring_off[-1] + ring[s])
    RT = ring_off[-1]
    kT = cx.sb("kT", [128, RT * 128], BF16)
    Va = cx.sb("Va", [128, RT, 130], BF16)
    wbf = [cx.sb("wbf%d" % i, [128, 8, 512], BF16) for i in range(4)]
    A_t = cx.sb("A_t", [128, D], F32)
    B_t = cx.sb("B_t", [128, D], F32)
    kbt = cx.sb("kbt", [128, 4, 68], F32)
    dg0 = cx.sb("dg0", [128, 4, 128], F32)
    cbt = cx.sb("cbt", [128, 4, 4], F32)
    ident = cx.sb("ident", [128, 128], BF16)
    lsc = cx.sb("lsc", [128, 8], F32)
    subg = cx.sb("subg", [128, 512], F32)
    mhalf = cx.sb("mhalf", [128, 4], F32)
    b_w = [Buf() for _ in range(4)]
    b_A, b_B, b_lam, b_ones = Buf(), Buf(), Buf(), Buf()
    cx.load_const(kbt.rearrange("p a b -> p (a b)"), io["kbtab"], cx.B("kbt"))
    cx.load_const(dg0.rearrange("p a b -> p (a b)"), io["dg0"], cx.B("dg0"))
    cx.load_const(cbt.rearrange("p a b -> p (a b)"), io["cbt"], cx.B("cbt"))
    cx.load_const(ident, io["ident"], cx.B("ident"))
    cx.load_const(subg, io["subg"], cx.B("subg"))
    for i in range(4):
        sem = S.new_dma_sem()
        wv_ = w_d[i].rearrange("(dc p) n -> p dc n", p=128)
        S.dma("pool", lambda e, i=i, wv_=wv_: e.dma_start(out=wbf[i], in_=wv_), sem, writes=[b_w[i]])
    S.op("pool", lambda e: e.memset(mhalf, -0.5), writes=[cx.B("mhalf")])
    S.op("pool", lambda e: e.memset(Va[:, :, 128:130], 1.0), writes=[b_ones])
    S.op("dve", lambda e: e.tensor_scalar(out=subg, in0=subg, scalar1=0.5 * (1.0 - LAM_INIT0), scalar2=None, op0=ALU.mult),
         reads=[cx.B("subg")], writes=[cx.B("subg")])

    off0 = cx.off
    lamv = cx.sb("lamv", [128, 256], F32)
    ljunk = cx.sb("ljunk", [128, 64], F32)
    cx.load_const(lamv, io["lamv"], cx.B("lamv"))
    b_lj = Buf()
    S.op("dve", lambda e: e.scalar_tensor_tensor(out=ljunk, in0=lamv[:, 0:64], scalar=1.0, in1=lamv[:, 64:128],
                                                  op0=ALU.mult, op1=ALU.mult, accum_out=lsc[:, 0:1]),
         reads=[cx.B("lamv")], writes=[b_lj, b_lam])
    S.op("dve", lambda e: e.scalar_tensor_tensor(out=ljunk, in0=lamv[:, 128:192], scalar=1.0, in1=lamv[:, 192:256],
                                                  op0=ALU.mult, op1=ALU.mult, accum_out=lsc[:, 1:2]),
         reads=[cx.B("lamv")], writes=[b_lj, b_lam])
    S.op("act", lambda e: e.activation(out=lsc[:, 2:4], in_=lsc[:, 0:2], func=AF.Exp), reads=[b_lam], writes=[b_lam])
    S.op("dve", lambda e: e.tensor_tensor(out=lsc[:, 4:5], in0=lsc[:, 2:3], in1=lsc[:, 3:4], op=ALU.subtract),
         reads=[b_lam], writes=[b_lam])
    S.op("dve", lambda e: e.tensor_scalar(out=lsc[:, 5:6], in0=lsc[:, 4:5], scalar1=LAM_INIT0, scalar2=-1.0,
                                           op0=ALU.add, op1=ALU.mult), reads=[b_lam], writes=[b_lam])
    neglam = lsc[:, 5:6]

    emit_adaln_all(cx, io, A_t, B_t, b_A, b_B)
    S.barrier()
    cx.off = off0

    NXB = 4
    xbuf = cx.sb("xbuf", [128, NXB, 1024], F32)
    hT = cx.sb("hT", [128, 8, 512], BF16)
    hb = [cx.sb("hb%d" % i, [128, 1024], BF16) for i in range(NXB)]
    tmp = cx.sb("tmp", [128, 1024], F32)
    qT = [[cx.sb("qT%d%d" % (p, c), [128, 4, 512], BF16) for c in range(2)] for p in range(2)]
    gs = [cx.sb("gs%d" % p, [128, 4, 512], BF16) for p in range(2)]
    tg = cx.sb("tg", [128, 512], F32)
    tg2 = cx.sb("tg2", [128, 512], F32)
    PT = [cx.sb("PT%d" % i, [128, 512], BF16) for i in range(3)]
    dtmp = [cx.sb("dtmp%d" % i, [128, 128], F32) for i in range(2)]
    stat = [cx.sb("stat%d" % i, [128, 4], F32) for i in range(NXB)]
    o1_t = cx.sb("o1_t", [128, 4, 128], F32)
    o_t = cx.sb("o_t", [128, 4, 128], F32)
    sqj = cx.sb("sqj", [128, 128], F32)
    ep = cx.sb("ep", [128, 24], F32)
    yb = [cx.sb("yb%d" % i, [128, 4, 128], BF16) for i in range(2)]
    ystage = [cx.sb("ystage%d" % i, [128, 512], BF16) for i in range(2)]
    b_x = [Buf() for _ in range(NXB)]
    b_hT, b_tmp, b_tg, b_tg2 = Buf(), Buf(), Buf(), Buf()
    b_hb = [Buf() for _ in range(NXB)]
    b_qT = [Buf(), Buf()]
    b_gs = [Buf(), Buf()]
    b_PT = [Buf(), Buf(), Buf()]
    b_dtmp = [Buf(), Buf()]
    b_stat = [Buf() for _ in range(NXB)]
    b_kring = [[Buf() for _ in range(ring[s] // 4)] for s in range(4)]
    b_vring = [[Buf() for _ in range(ring[s] // 4)] for s in range(4)]
    b_o1, b_o, b_sqj, b_ep0, b_ep1 = Buf(), Buf(), Buf(), Buf(), Buf()
    b_yb = [Buf(), Buf()]
    deferred = []
    b_ys = [Buf(), Buf()]
    b_ywr = [Buf(), Buf()]
    xsem = [S.new_dma_sem() for _ in range(NXB)]
    ysem = [S.new_dma_sem(), S.new_dma_sem()]
    for p in range(2):
        S.op("pool", lambda e, p=p: e.memset(qT[p][0][64:128, :, :], 0.0), writes=[b_qT[p]])
        S.op("pool", lambda e, p=p: e.memset(qT[p][1][0:64, :, :], 0.0), writes=[b_qT[p]])

    def ring_col(s, j):
        return (ring_off[s] + (j % ring[s])) * 128

    def ring_blk(s, j):
        return ring_off[s] + (j % ring[s])

    def ring_grp(s, j):
        return (j // 4) % (ring[s] // 4)

    pb_i = [0]

    def next_pbank():
        return 7

    ST_BANKS = (0, 1, 6)
    ntb = 4 * nsb

    def load_x(tb):
        xi = tb % NXB
        S.dma("sp", lambda e: e.dma_start(out=xbuf[:, xi, :], in_=x_d[tb * 128:(tb + 1) * 128, :]), xsem[xi], writes=[b_x[xi]])

    def chunk_norm_a(tb):
        def f():
            xi = tb % NXB
            xt = xbuf[:, xi, :]
            stt = stat[xi]
            S.op("dve", lambda e: e.scalar_tensor_tensor(out=tmp, in0=xt, scalar=1.0, in1=xt, op0=ALU.mult, op1=ALU.mult, accum_out=stt[:, 0:1]),
                 reads=[b_x[xi]], writes=[b_tmp, b_stat[xi]])
            S.op("dve", lambda e: e.tensor_scalar(out=stt[:, 1:2], in0=stt[:, 0:1], scalar1=1.0 / D, scalar2=1e-6, op0=ALU.mult, op1=ALU.add),
                 reads=[b_stat[xi]], writes=[b_stat[xi]])
            S.op("pool", lambda e: e.tensor_tensor(out=stt[:, 2:3], in0=stt[:, 1:2], in1=mhalf[:, 0:1], op=ALU.pow),
                 reads=[b_stat[xi], cx.B("mhalf")], writes=[b_stat[xi]])
        return f

    def chunk_norm(tb):
        def f():
            xi = tb % NXB
            xt = xbuf[:, xi, :]
            stt = stat[xi]
            S.op("dve", lambda e: e.scalar_tensor_tensor(out=tmp, in0=xt, scalar=stt[:, 2:3], in1=A_t, op0=ALU.mult, op1=ALU.mult),
                 reads=[b_x[xi], b_stat[xi], b_A], writes=[b_tmp])
            S.op("pool", lambda e: e.tensor_tensor(out=hb[xi], in0=tmp, in1=B_t, op=ALU.add), reads=[b_tmp, b_B], writes=[b_hb[xi]])
            if tb + NXB < ntb:
                load_x(tb + NXB)
        return f

    def evac(sb, out_ap, in_ap, reads, writes):
        if sb < 8:
            S.op("act", lambda e: e.copy(out=out_ap, in_=in_ap), reads=reads, writes=writes)
        else:
            S.op("dve", lambda e: e.tensor_copy(out=out_ap, in_=in_ap), reads=reads, writes=writes)

    def chunk_tr(tb):
        def f():
            xi, t = tb % NXB, tb % 4
            pb = next_pbank()
            pbv = banks[pb][:].bitcast(BF16)
            for dc in range(8):
                S.op("pe", lambda e, dc=dc: e.transpose(out=pbv[:, dc * 128:(dc + 1) * 128], in_=hb[xi][:, dc * 128:(dc + 1) * 128], identity=ident),
                     reads=[b_hb[xi], cx.B("ident")], writes=[bank_b[pb]])
            evac(tb // 4, hT[:, :, t * 128:(t + 1) * 128], pbv.rearrange("p (dc n) -> p dc n", n=128), [bank_b[pb]], [b_hT])
        return f

    def chunk_q(sb, s):
        def f():
            p = sb % 2
            pb = next_pbank()
            for dc in range(8):
                S.op("pe", lambda e, dc=dc: e.matmul(banks[pb][:], lhsT=wbf[0][:, dc, s * 128:(s + 1) * 128], rhs=hT[:, dc, :], start=(dc == 0), stop=(dc == 7)),
                     reads=[b_w[0], b_hT], writes=[bank_b[pb]])
            evac(sb, qT[p][0][0:64, s, :], banks[pb][0:64, :], [bank_b[pb]], [b_qT[p]])
            evac(sb, qT[p][1][64:128, s, :], banks[pb][64:128, :], [bank_b[pb]], [b_qT[p]])
        return f

    def chunk_k(sb, s):
        def f():
            pb = next_pbank()
            for dc in range(8):
                S.op("pe", lambda e, dc=dc: e.matmul(banks[pb][:], lhsT=wbf[1][:, dc, s * 128:(s + 1) * 128], rhs=hT[:, dc, :], start=(dc == 0), stop=(dc == 7)),
                     reads=[b_w[1], b_hT], writes=[bank_b[pb]])
            c0 = ring_col(s, 4 * sb)
            evac(sb, kT[:, c0:c0 + 512], banks[pb][:], [bank_b[pb]], [b_kring[s][ring_grp(s, 4 * sb)]])
        return f

    def chunk_v(sb, t):
        def f():
            pb = next_pbank()
            for dc in range(8):
                S.op("pe", lambda e, dc=dc: e.matmul(banks[pb][:], lhsT=hT[:, dc, t * 128:(t + 1) * 128], rhs=wbf[2][:, dc, :], start=(dc == 0), stop=(dc == 7)),
                     reads=[b_w[2], b_hT], writes=[bank_b[pb]])
            for s in range(4):
                rb = ring_blk(s, 4 * sb + t)
                evac(sb, Va[:, rb, 0:128], banks[pb][:, s * 128:(s + 1) * 128], [bank_b[pb]], [b_vring[s][ring_grp(s, 4 * sb)]])
        return f

    def chunk_g(sb, t):
        def f():
            p = sb % 2
            pb = next_pbank()
            for dc in range(8):
                S.op("pe", lambda e, dc=dc: e.matmul(banks[pb][:], lhsT=hT[:, dc, t * 128:(t + 1) * 128], rhs=wbf[3][:, dc, :], start=(dc == 0), stop=(dc == 7)),
                     reads=[b_w[3], b_hT], writes=[bank_b[pb]])
            S.op("act", lambda e: e.activation(out=tg, in_=banks[pb][:], func=AF.Tanh, scale=0.5), reads=[bank_b[pb]], writes=[b_tg])
            S.op("dve", lambda e: e.scalar_tensor_tensor(out=tg2, in0=tg, scalar=1.0, in1=banks[pb][:], op0=ALU.add, op1=ALU.mult),
                 reads=[b_tg, bank_b[pb]], writes=[b_tg2])
            S.op("pool", lambda e: e.tensor_tensor(out=gs[p][:, t, :], in0=tg2, in1=subg, op=ALU.mult), reads=[b_tg2, cx.B("subg")], writes=[b_gs[p]])
        return f

    def proj_chunks(sb):
        tb0 = 4 * sb
        ch = [chunk_norm_a(tb0), chunk_norm_a(tb0 + 1), chunk_norm(tb0), chunk_norm_a(tb0 + 2), chunk_norm(tb0 + 1), chunk_norm_a(tb0 + 3),
              chunk_norm(tb0 + 2), chunk_norm(tb0 + 3)]
        ch += [chunk_tr(tb0), chunk_tr(tb0 + 1), chunk_v(sb, 0), chunk_tr(tb0 + 2), chunk_g(sb, 0), chunk_tr(tb0 + 3),
               chunk_v(sb, 1), chunk_g(sb, 1), chunk_v(sb, 2), chunk_g(sb, 2), chunk_v(sb, 3), chunk_g(sb, 3)]
        ch += [chunk_k(sb, s) for s in range(4)] + [chunk_q(sb, s) for s in range(4)]
        return ch

    st_i, pt_i, dt_i = [0], [0], [0]
    accs = [(2, 3), (4, 5)]

    def acc_ap(c, r, lo, hi):
        return banks[accs[c][r // 2]][:, (r % 2) * 130 + lo:(r % 2) * 130 + hi]

    def emit_qk(tile):
        sb, s, c, idx, j = tile
        p = sb % 2
        r0 = max(0, j - 4 * sb)
        sti = ST_BANKS[st_i[0] % 3]
        st_i[0] += 1
        tile.append(sti)
        kc = ring_col(s, j)
        S.op("pe", lambda e: e.matmul(banks[sti][:, r0 * 128:512], lhsT=kT[:, kc:kc + 128], rhs=qT[p][c][:, s, r0 * 128:512], start=True, stop=True),
             reads=[b_kring[s][ring_grp(s, j)], b_qT[p]], writes=[bank_b[sti]])

    def emit_exp_pv(tile):
        sb, s, c, idx, j, sti = tile
        wide = SLOT_WIDE[s]
        r0 = max(0, j - 4 * sb)
        ST = banks[sti]
        pti = pt_i[0] % 3
        pt_i[0] += 1
        P = PT[pti]
        rstart = r0
        if j >= 4 * sb:
            di = dt_i[0] % 2
            dt_i[0] += 1
            S.op("dve", lambda e: e.scalar_tensor_tensor(out=dtmp[di], in0=ST[:, r0 * 128:(r0 + 1) * 128], scalar=0.125, in1=dg0[:, s, :], op0=ALU.mult, op1=ALU.add),
                 reads=[bank_b[sti], cx.B("dg0")], writes=[b_dtmp[di]])
            S.op("act", lambda e: e.activation(out=P[:, r0 * 128:(r0 + 1) * 128], in_=dtmp[di], func=AF.Exp, bias=cbt[:, s, r0:r0 + 1], scale=1.0),
                 reads=[b_dtmp[di], cx.B("cbt")], writes=[b_PT[pti]])
            rstart = r0 + 1
        if rstart < 4:
            if wide:
                ti = 4 * sb + 3 - j
                S.op("act", lambda e: e.activation(out=P[:, rstart * 128:512], in_=ST[:, rstart * 128:512], func=AF.Exp, bias=kbt[:, s, ti:ti + 1], scale=0.125),
                     reads=[bank_b[sti], cx.B("kbt")], writes=[b_PT[pti]])
            else:
                for r in range(rstart, 4):
                    ti = 4 * sb + r - j
                    S.op("act", lambda e, r=r, ti=ti: e.activation(out=P[:, r * 128:(r + 1) * 128], in_=ST[:, r * 128:(r + 1) * 128], func=AF.Exp,
                                                                     bias=kbt[:, s, ti:ti + 1], scale=0.125),
                         reads=[bank_b[sti], cx.B("kbt")], writes=[b_PT[pti]])
        rb = ring_blk(s, j)
        for r in range(r0, 4):
            S.op("pe", lambda e, r=r: e.matmul(acc_ap(c, r, 0, 130), lhsT=P[:, r * 128:(r + 1) * 128], rhs=Va[:, rb, :],
                                                start=(idx == 0 and r % 2 == 0), stop=True, skip_group_check=True),
                 reads=[b_PT[pti], b_vring[s][ring_grp(s, j)], b_ones], writes=[bank_b[accs[c][r // 2]]])

    def epi_c0(sb, s):
        for hb2 in range(2):
            bk = accs[0][hb2]
            S.op("dve", lambda e, bk=bk, hb2=hb2: e.reciprocal(out=ep[:, 2 * hb2:2 * hb2 + 2], in_=banks[bk][:, 128:259:130]),
                 reads=[bank_b[bk]], writes=[b_ep0])
        for r in range(4):
            S.op("dve", lambda e, r=r: e.tensor_scalar(out=o1_t[:, r, :], in0=acc_ap(0, r, 0, 128), scalar1=ep[:, r:r + 1], scalar2=None, op0=ALU.mult),
                 reads=[bank_b[accs[0][r // 2]], b_ep0], writes=[b_o1])

    def epi_c1(sb, s):
        p = sb % 2
        for hb2 in range(2):
            bk = accs[1][hb2]
            S.op("dve", lambda e, bk=bk, hb2=hb2: e.reciprocal(out=ep[:, 4 + 2 * hb2:4 + 2 * hb2 + 2], in_=banks[bk][:, 128:259:130]),
                 reads=[bank_b[bk]], writes=[b_ep1])
        S.op("dve", lambda e: e.tensor_scalar(out=ep[:, 8:12], in0=ep[:, 4:8], scalar1=neglam, scalar2=None, op0=ALU.mult),
             reads=[b_ep1, b_lam], writes=[b_ep1])
        for r in range(4):
            S.op("dve", lambda e, r=r: e.scalar_tensor_tensor(out=o_t[:, r, :], in0=acc_ap(1, r, 0, 128), scalar=ep[:, 8 + r:9 + r], in1=o1_t[:, r, :],
                                                               op0=ALU.mult, op1=ALU.add),
                 reads=[bank_b[accs[1][r // 2]], b_ep1, b_o1], writes=[b_o])
        for r in range(4):
            S.op("dve", lambda e, r=r: e.scalar_tensor_tensor(out=sqj, in0=o_t[:, r, :], scalar=1.0, in1=o_t[:, r, :], op0=ALU.mult, op1=ALU.mult,
                                                               accum_out=ep[:, 12 + r:13 + r]),
                 reads=[b_o], writes=[b_sqj, b_ep1])
        S.op("dve", lambda e: e.tensor_scalar(out=ep[:, 12:16], in0=ep[:, 12:16], scalar1=1.0 / 128, scalar2=1e-5, op0=ALU.mult, op1=ALU.add),
             reads=[b_ep1], writes=[b_ep1])
        S.op("pool", lambda e: e.tensor_tensor(out=ep[:, 16:20], in0=ep[:, 12:16], in1=mhalf[:, 0:4], op=ALU.pow),
             reads=[b_ep1, cx.B("mhalf")], writes=[b_ep1])
        yi = (sb * 4 + s) % 2
        ybt = yb[yi]

        def part_b():
            for r in range(4):
                S.op("dve", lambda e, r=r: e.scalar_tensor_tensor(out=ybt[:, r, :], in0=o_t[:, r, :], scalar=ep[:, 16 + r:17 + r],
                                                                   in1=gs[p][:, r, s * 128:(s + 1) * 128], op0=ALU.mult, op1=ALU.mult),
                     reads=[b_o, b_ep1, b_gs[p]], writes=[b_yb[yi]])
        deferred.append([4, part_b])

        def finish():
            pb = next_pbank()
            pbv = banks[pb][:].bitcast(BF16)
            for r in range(4):
                S.op("pe", lambda e, r=r: e.transpose(out=pbv[:, r * 128:(r + 1) * 128], in_=ybt[:, r, :], identity=ident),
                     reads=[b_yb[yi], cx.B("ident")], writes=[bank_b[pb]])
            S.op("dve", lambda e: e.tensor_copy(out=ystage[yi], in_=pbv[:, 0:512]), reads=[bank_b[pb]], writes=[b_ys[yi]])
            S.dma("sp", lambda e: e.dma_start(out=io["yg0_src"][sb // 4][s * 128:(s + 1) * 128, (sb % 4) * 512:(sb % 4 + 1) * 512], in_=ystage[yi]),
                  ysem[yi], reads=[b_ys[yi]], writes=[b_ywr[yi]])
            if sb % 4 == 3 and s == 3:
                g = sb // 4
                S.dma("pool", lambda e: e.collective_compute("AllGather", ALU.bypass, replica_groups=PAIRS,
                                                             ins=[io["yg0_src"][g].ap().opt()], outs=[io["yg0_all"][g].ap().opt()]),
                      io["ccsem"], reads=[b_ywr[0], b_ywr[1]], writes=[io["b_cc0"][g]], inc=1)
        deferred.append([12, finish])

    def tiles_of(sb):
        tl = []
        for s in range(4):
            jmin = max(0, 4 * sb - SLOT_BACK[s])
            for c in range(2):
                for idx, j in enumerate(range(jmin, 4 * sb + 4)):
                    tl.append([sb, s, c, idx, j])
        return tl

    for tb in range(min(NXB, ntb)):
        load_x(tb)
    for f in proj_chunks(0):
        f()
    precast = ["wout0", "wq1", "wk1", "wv1", "wg1", "wout1"]
    for sb in range(nsb):
        if (sb >= 5 or sb == nsb - 1) and precast:
            for nme in ([precast.pop(0)] if sb < nsb - 1 else list(precast)):
                sem = S.new_dma_sem()
                S.dma("pool", lambda e, nme=nme: e.dma_start(out=io[nme + "_bf"][:, :], in_=io[nme]), sem)
            if sb == nsb - 1:
                precast = []
        tiles = tiles_of(sb)
        chunks = proj_chunks(sb + 1) if sb + 1 < nsb else []
        nt = len(tiles)
        emit_qk(tiles[0])
        if nt > 1:
            emit_qk(tiles[1])
        done_chunks = 0
        for i, tile in enumerate(tiles):
            if i + 2 < nt:
                emit_qk(tiles[i + 2])
            emit_exp_pv(tile)
            last_of_group = (i + 1 == nt) or (tiles[i + 1][1] != tile[1]) or (tiles[i + 1][2] != tile[2])
            if last_of_group:
                if tile[2] == 0:
                    epi_c0(sb, tile[1])
                else:
                    epi_c1(sb, tile[1])
            for dfr in deferred:
                dfr[0] -= 1
            deferred.sort(key=lambda d: d[0])
            while deferred and deferred[0][0] <= 0:
                deferred.pop(0)[1]()
            want = (len(chunks) * (i + 1)) // nt
            while done_chunks < want:
                chunks[done_chunks]()
                done_chunks += 1
    deferred.sort(key=lambda d: d[0])
    while deferred:
        deferred.pop(0)[1]()


def l0_consts(hh):
    kb = np.zeros((128, 4, 68), np.float32)
    dg = np.zeros((128, 4, 128), np.float32)
    cb = np.zeros((128, 4, 4), np.float32)
    ki = np.arange(128, dtype=np.float64)[:, None]
    qi = np.arange(128, dtype=np.float64)[None, :]
    allowed = (ki // 64) <= (qi // 64)
    for s in range(4):
        slope = 2.0 ** -(HL[hh][s] + 1)
        t = np.arange(68, dtype=np.float64)[None, :]
        if SLOT_WIDE[s]:
            kb[:, s, :] = slope * (128.0 * (3 - t) + ki - 256.0)
            for r in range(4):
                cb[:, s, r] = slope * (128.0 * r - 192.0)
        else:
            kb[:, s, :] = slope * (-128.0 * t + ki - 64.0)
        d = -slope * np.abs(qi - ki) + slope * (qi - 64.0)
        dg[:, s, :] = np.where(allowed, d, NEG_BIG)
    return kb.reshape(128, 4 * 68), dg.reshape(128, 4 * 128), cb.reshape(128, 16)


def rep128(v):
    return np.ascontiguousarray(np.broadcast_to(np.asarray(v, np.float32).reshape(1, -1), (128, v.size)))


def head_cols(hh, width):
    return np.concatenate([np.arange(h * width, (h + 1) * width) for h in HL[hh]])


def phase_outproj(cx, io, layer, E, ntok=S_LEN):
    EC = E // 128
    if True:
        S = cx.S
        ccol_d = io["ccol"]
        adaw_d, adab_d, pg_d = io["adaw%d" % layer][:, 2048:3072], io["adab%d" % layer][:, 2048:3072], io["post%d" % layer]
        w_d = io["wout%d_bf" % layer][:, :]
        x_d, xo_d = io["op_x%d" % layer], io["op_xo%d" % layer]
        ysrc = io["op_ysrc%d" % layer]
        banks, bank_b = cx.banks, cx.bank_b
        wbf = cx.sb("wbf", [128, EC, D], BF16)
        GP = cx.sb("GP", [128, D], F32)
        NX = 4
        yT = [cx.sb("yT%d" % i, [128, EC, 512], BF16) for i in range(2)]
        xt = [cx.sb("xt%d" % i, [128, D], F32) for i in range(NX)]
        tmp = [cx.sb("tmp%d" % i, [128, D], F32) for i in range(2)]
        sq = cx.sb("sq", [128, D], F32)
        b_sq = Buf()
        xo = [cx.sb("xo%d" % i, [128, D], F32) for i in range(NX)]
        stat = [cx.sb("stat%d" % i, [128, 4], F32) for i in range(2)]
        mhalf = cx.sb("mhalf", [128, 1], F32)
        b_w, b_GP, b_tg = Buf(), Buf(), Buf()
        b_tmp = [Buf(), Buf()]
        b_yT, b_stat = [Buf(), Buf()], [Buf(), Buf()]
        b_xt, b_xo = [Buf() for _ in range(NX)], [Buf() for _ in range(NX)]
        cx.load_const(GP, io["adaG%d" % layer][:, :], b_GP)
        wsem = S.new_dma_sem()
        wv = w_d.rearrange("(ec p) n -> p ec n", p=128)
        for ec0 in range(0, EC, 4):
            S.dma("sp", lambda e, ec0=ec0: e.dma_start(out=wbf[:, ec0:ec0 + 4, :], in_=wv[:, ec0:ec0 + 4, :]), wsem, writes=[b_w])
        S.op("pool", lambda e: e.memset(mhalf[:], -0.5), writes=[cx.B("mhalf")])

        ysem = [S.new_dma_sem(), S.new_dma_sem()]
        xsem = [S.new_dma_sem() for _ in range(NX)]
        osem = [S.new_dma_sem() for _ in range(NX)]
        out_toks = {}
        nblk = ntok // 128

        def emit_loads(tb):
            if tb >= nblk:
                return
            if tb % 4 == 0:
                g4 = tb // 4
                yi = g4 % 2
                y_ap, y_buf = ysrc(g4)
                S.dma("sp", lambda e: e.dma_start(out=yT[yi][:], in_=y_ap), ysem[yi], reads=[y_buf], writes=[b_yT[yi]])
            xi = tb % NX
            S.dma("sp", lambda e: e.dma_start(out=xt[xi][:], in_=x_d[tb * 128:(tb + 1) * 128, :]), xsem[xi], writes=[b_xt[xi]])

        for tb in range(3):
            emit_loads(tb)
        for tb in range(nblk):
            emit_loads(tb + 3)
            g4, t = tb // 4, tb % 4
            yi = g4 % 2
            xi = tb % NX
            pi = tb % 2
            bk = (4 * pi, 4 * pi + 1)
            for half in range(2):
                for ec in range(EC):
                    S.op("pe", lambda e, half=half, ec=ec, yi=yi, t=t, bk=bk: e.matmul(
                        banks[bk[half]][:], lhsT=yT[yi][:, ec, t * 128:(t + 1) * 128], rhs=wbf[:, ec, half * 512:(half + 1) * 512],
                        start=(ec == 0), stop=(ec == EC - 1)), reads=[b_yT[yi], b_w], writes=[bank_b[bk[half]]])
            stt = stat[pi]
            for half in range(2):
                S.op("act", lambda e, half=half, bk=bk, stt=stt: e.activation(
                    out=sq[:, half * 512:(half + 1) * 512], in_=banks[bk[half]][:], func=AF.Square,
                    accum_out=stt[:, half:half + 1]), reads=[bank_b[bk[half]]], writes=[b_sq, b_stat[pi]])
            S.op("dve", lambda e, stt=stt: e.tensor_tensor(out=stt[:, 2:3], in0=stt[:, 0:1], in1=stt[:, 1:2], op=ALU.add),
                 reads=[b_stat[pi]], writes=[b_stat[pi]])
            S.op("dve", lambda e, stt=stt: e.tensor_scalar(out=stt[:, 2:3], in0=stt[:, 2:3], scalar1=1.0 / D, scalar2=1e-6, op0=ALU.mult, op1=ALU.add),
                 reads=[b_stat[pi]], writes=[b_stat[pi]])
            S.op("pool", lambda e, stt=stt: e.tensor_tensor(out=stt[:, 3:4], in0=stt[:, 2:3], in1=mhalf[:], op=ALU.pow),
                 reads=[b_stat[pi], cx.B("mhalf")], writes=[b_stat[pi]])
            for half in range(2):
                S.op("dve", lambda e, half=half, bk=bk, stt=stt, pi=pi: e.scalar_tensor_tensor(
                    out=tmp[pi][:, half * 512:(half + 1) * 512], in0=banks[bk[half]][:], scalar=stt[:, 3:4], in1=GP[:, half * 512:(half + 1) * 512],
                    op0=ALU.mult, op1=ALU.mult), reads=[bank_b[bk[half]], b_stat[pi], b_GP], writes=[b_tmp[pi]])
            S.op("pool", lambda e, xi=xi, pi=pi: e.tensor_tensor(out=xo[xi][:], in0=tmp[pi][:], in1=xt[xi][:], op=ALU.add),
                 reads=[b_tmp[pi], b_xt[xi]], writes=[b_xo[xi]])
            out_toks[xi] = S.dma("sp", lambda e, tb=tb, xi=xi: e.dma_start(out=xo_d[tb * 128:(tb + 1) * 128, :], in_=xo[xi][:]), osem[xi],
                                 reads=[b_xo[xi]])
        return list(out_toks.values())


RET_HEADS = 4


def ret_gamma(h):
    return float(np.float32(1.0) - np.float32(2.0) ** np.float32(-5.0 - h))


def phase_l1(cx, io, nsb=NSB):
    S = cx.S
    x_d, ccol_d = io["x1"], io["ccol"]
    adaw_d, adab_d, pg_d = io["adaw1"][:, 0:2048], io["adab1"][:, 0:2048], io["pre1"]
    banks, bank_b = cx.banks, cx.bank_b
    wq = cx.sb("wq", [128, 8, 512], BF16)
    wk = cx.sb("wk", [128, 8, 512], BF16)
    wv = cx.sb("wv", [128, 8, 1024], BF16)
    wg = cx.sb("wg", [128, 8, 1024], BF16)
    A_t = cx.sb("A_t", [128, D], F32)
    B_t = cx.sb("B_t", [128, D], F32)
    dmt = cx.sb("dmt", [128, 2, 128], F32)
    qd = cx.sb("qd", [128, 2, 512], F32)
    kdg = cx.sb("kdg", [128, 4], F32)
    ident = cx.sb("ident", [128, 128], BF16)
    mhalf = cx.sb("mhalf", [128, 4], F32)
    b_wq, b_wk, b_wv, b_wg, b_A, b_B = Buf(), Buf(), Buf(), Buf(), Buf(), Buf()
    cx.load_const(dmt.rearrange("p a b -> p (a b)"), io["dmt"], cx.B("dmt"))
    cx.load_const(qd.rearrange("p a b -> p (a b)"), io["qd"], cx.B("qd"))
    cx.load_const(kdg, io["kdg"], cx.B("kdg"))
    cx.load_const(ident, io["ident"], cx.B("ident"))
    for (wt, wd, bw) in ((wq, io["wq1_bf"], b_wq), (wk, io["wk1_bf"], b_wk), (wv, io["wv1_bf"], b_wv), (wg, io["wg1_bf"], b_wg)):
        sem = S.new_dma_sem()
        wvv = wd[:, :].rearrange("(dc p) n -> p dc n", p=128)
        for dc0 in range(0, 8, 4):
            S.dma("sp", lambda e, wt=wt, wvv=wvv, dc0=dc0: e.dma_start(out=wt[:, dc0:dc0 + 4, :], in_=wvv[:, dc0:dc0 + 4, :]), sem, writes=[bw])
    S.op("pool", lambda e: e.memset(mhalf, -0.5), writes=[cx.B("mhalf")])

    cx.load_const(A_t, io["adaA1"][:, :], b_A)
    cx.load_const(B_t, io["adaB1"][:, :], b_B)

    NXB = 4
    xbuf = cx.sb("xbuf", [128, NXB, 1024], F32)
    hT = [cx.sb("hT%d" % p, [128, 8, 512], BF16) for p in range(2)]
    hb = [cx.sb("hb%d" % i, [128, 1024], BF16) for i in range(NXB)]
    tmp = cx.sb("tmp", [128, 1024], F32)
    tg = cx.sb("tg", [128, 512], F32)
    qT = [cx.sb("qT%d" % p, [128, 4, 512], BF16) for p in range(2)]
    qdT = [cx.sb("qdT%d" % p, [128, 4, 512], BF16) for p in range(2)]
    kT = [cx.sb("kT%d" % p, [128, 4, 512], BF16) for p in range(2)]
    kd = [cx.sb("kd%d" % i, [128, 512], BF16) for i in range(2)]
    vt = [cx.sb("vt%d" % i, [128, 1024], BF16) for i in range(2)]
    gs = [cx.sb("gs%d" % i, [128, 1024], BF16) for i in range(3)]
    S32 = cx.sb("S32", [128, 2, 2, 512], F32)
    Sbf = cx.sb("Sbf", [128, 2, 2, 512], BF16)
    MT = [cx.sb("MT%d" % i, [128, 128], BF16) for i in range(2)]
    stat = [cx.sb("stat%d" % i, [128, 4], F32) for i in range(NXB)]
    ep = [[cx.sb("ep%d%d" % (i, hl), [128, 4], F32) for hl in range(2)] for i in range(2)]
    sq = cx.sb("sq", [128, 512], F32)
    yb = [[cx.sb("yb%d%d" % (i, hl), [128, 512], BF16) for hl in range(2)] for i in range(2)]
    ystage = [cx.sb("ystage%d" % i, [128, 8, 512], BF16) for i in range(2)]
    b_hT, b_qT, b_qdT, b_kT = [Buf(), Buf()], [Buf(), Buf()], [Buf(), Buf()], [Buf(), Buf()]
    b_tmp, b_tg, b_sq = Buf(), Buf(), Buf()
    b_x, b_hb, b_stat = [Buf() for _ in range(NXB)], [Buf() for _ in range(NXB)], [Buf() for _ in range(NXB)]
    b_kd, b_vt = [Buf(), Buf()], [Buf(), Buf()]
    b_gs = [Buf(), Buf(), Buf()]
    b_S32 = [[Buf(), Buf()], [Buf(), Buf()]]
    b_Sbf = [[Buf(), Buf()], [Buf(), Buf()]]
    b_MT = [Buf(), Buf()]
    b_ep = [[Buf(), Buf()], [Buf(), Buf()]]
    b_yb = [[Buf(), Buf()], [Buf(), Buf()]]
    b_ys, b_ywr = [Buf(), Buf()], [Buf(), Buf()]
    xsem = [S.new_dma_sem() for _ in range(NXB)]
    ysem = [S.new_dma_sem(), S.new_dma_sem()]
    for hl in range(2):
        for dk in range(2):
            S.op("pool", lambda e, hl=hl, dk=dk: e.memset(S32[:, hl, dk, :], 0.0), writes=[b_S32[hl][dk]])
            S.op("pool", lambda e, hl=hl, dk=dk: e.memset(Sbf[:, hl, dk, :], 0.0), writes=[b_Sbf[hl][dk]])

    pb_i = [0]

    PROT = (0, 1, 2, 6, 7)

    def next_pbank():
        b = PROT[pb_i[0] % 5]
        pb_i[0] += 1
        return b

    ob_i = [0]
    ntb = 4 * nsb

    def load_x(tb):
        xi = tb % NXB
        S.dma("sp", lambda e: e.dma_start(out=xbuf[:, xi, :], in_=x_d[tb * 128:(tb + 1) * 128, :]), xsem[xi], writes=[b_x[xi]])

    def chunk_norm(tb):
        def f():
            xi = tb % NXB
            xt = xbuf[:, xi, :]
            stt = stat[xi]
            S.op("dve", lambda e: e.scalar_tensor_tensor(out=tmp, in0=xt, scalar=1.0, in1=xt, op0=ALU.mult, op1=ALU.mult, accum_out=stt[:, 0:1]),
                 reads=[b_x[xi]], writes=[b_tmp, b_stat[xi]])
            S.op("dve", lambda e: e.tensor_scalar(out=stt[:, 1:2], in0=stt[:, 0:1], scalar1=1.0 / D, scalar2=1e-6, op0=ALU.mult, op1=ALU.add),
                 reads=[b_stat[xi]], writes=[b_stat[xi]])
            S.op("pool", lambda e: e.tensor_tensor(out=stt[:, 2:3], in0=stt[:, 1:2], in1=mhalf[:, 0:1], op=ALU.pow),
                 reads=[b_stat[xi], cx.B("mhalf")], writes=[b_stat[xi]])
            S.op("dve", lambda e: e.scalar_tensor_tensor(out=tmp, in0=xt, scalar=stt[:, 2:3], in1=A_t, op0=ALU.mult, op1=ALU.mult),
                 reads=[b_x[xi], b_stat[xi], b_A], writes=[b_tmp])
            S.op("pool", lambda e: e.tensor_tensor(out=hb[xi], in0=tmp, in1=B_t, op=ALU.add), reads=[b_tmp, b_B], writes=[b_hb[xi]])
            if tb + NXB < ntb:
                load_x(tb + NXB)
        return f

    def chunk_tr(tb):
        def f():
            xi, t, p = tb % NXB, tb % 4, (tb // 4) % 2
            pb = next_pbank()
            pbv = banks[pb][:].bitcast(BF16)
            for dc in range(8):
                S.op("pe", lambda e, dc=dc: e.transpose(out=pbv[:, dc * 128:(dc + 1) * 128], in_=hb[xi][:, dc * 128:(dc + 1) * 128], identity=ident),
                     reads=[b_hb[xi], cx.B("ident")], writes=[bank_b[pb]])
            S.op("act", lambda e: e.copy(out=hT[p][:, :, t * 128:(t + 1) * 128], in_=pbv.rearrange("p (dc n) -> p dc n", n=128)),
                 reads=[bank_b[pb]], writes=[b_hT[p]])
        return f

    def chunk_q(sb, ch):
        def f():
            p = sb % 2
            pb = next_pbank()
            for dc in range(8):
                S.op("pe", lambda e, dc=dc: e.matmul(banks[pb][:], lhsT=wq[:, dc, ch * 128:(ch + 1) * 128], rhs=hT[p][:, dc, :], start=(dc == 0), stop=(dc == 7)),
                     reads=[b_wq, b_hT[p]], writes=[bank_b[pb]])
            S.op("act", lambda e: e.copy(out=qT[p][:, ch, :], in_=banks[pb][:]), reads=[bank_b[pb]], writes=[b_qT[p]])
            S.op("dve", lambda e: e.tensor_tensor(out=qdT[p][:, ch, :], in0=banks[pb][:], in1=qd[:, ch // 2, :], op=ALU.mult),
                 reads=[bank_b[pb], cx.B("qd")], writes=[b_qdT[p]])
        return f

    def chunk_k(sb, ch):
        def f():
            p = sb % 2
            pb = next_pbank()
            for dc in range(8):
                S.op("pe", lambda e, dc=dc: e.matmul(banks[pb][:], lhsT=wk[:, dc, ch * 128:(ch + 1) * 128], rhs=hT[p][:, dc, :], start=(dc == 0), stop=(dc == 7)),
                     reads=[b_wk, b_hT[p]], writes=[bank_b[pb]])
            S.op("act", lambda e: e.copy(out=kT[p][:, ch, :], in_=banks[pb][:]), reads=[bank_b[pb]], writes=[b_kT[p]])
        return f

    def sb_chunks(sb):
        tb0 = 4 * sb
        ch = [chunk_norm(tb0 + t) for t in range(4)] + [chunk_tr(tb0 + t) for t in range(4)]
        ch += [chunk_q(sb, c) for c in range(4)] + [chunk_k(sb, c) for c in range(4)]
        return ch

    def proj_tok(tb):
        p, t, bi = (tb // 4) % 2, tb % 4, tb % 2
        gi = tb % 3
        tsl = slice(t * 128, (t + 1) * 128)
        pb = next_pbank()
        pbk = banks[pb][:].bitcast(BF16)
        for ch in range(4):
            S.op("pe", lambda e, ch=ch, pbk=pbk: e.transpose(out=pbk[:, ch * 128:(ch + 1) * 128], in_=kT[p][:, ch, tsl], identity=ident),
                 reads=[b_kT[p], cx.B("ident")], writes=[bank_b[pb]])
        for hl in range(2):
            S.op("dve", lambda e, hl=hl, pbk=pbk: e.tensor_scalar(out=kd[bi][:, hl * 256:(hl + 1) * 256], in0=pbk[:, hl * 256:(hl + 1) * 256],
                                                                  scalar1=kdg[:, hl:hl + 1], scalar2=None, op0=ALU.mult),
                 reads=[bank_b[pb], cx.B("kdg")], writes=[b_kd[bi]])
        for half in range(2):
            pb = next_pbank()
            for dc in range(8):
                S.op("pe", lambda e, dc=dc, pb=pb, half=half: e.matmul(banks[pb][:], lhsT=hT[p][:, dc, tsl], rhs=wv[:, dc, half * 512:(half + 1) * 512],
                                                                         start=(dc == 0), stop=(dc == 7)),
                     reads=[b_wv, b_hT[p]], writes=[bank_b[pb]])
            S.op("act", lambda e, pb=pb, half=half: e.copy(out=vt[bi][:, half * 512:(half + 1) * 512], in_=banks[pb][:]),
                 reads=[bank_b[pb]], writes=[b_vt[bi]])
        for half in range(2):
            pb = next_pbank()
            for dc in range(8):
                S.op("pe", lambda e, dc=dc, pb=pb, half=half: e.matmul(banks[pb][:], lhsT=hT[p][:, dc, tsl], rhs=wg[:, dc, half * 512:(half + 1) * 512],
                                                                         start=(dc == 0), stop=(dc == 7)),
                     reads=[b_wg, b_hT[p]], writes=[bank_b[pb]])
            S.op("act", lambda e, pb=pb: e.activation(out=tg, in_=banks[pb][:], func=AF.Tanh, scale=0.5), reads=[bank_b[pb]], writes=[b_tg])
            S.op("dve", lambda e, pb=pb, half=half: e.scalar_tensor_tensor(out=gs[gi][:, half * 512:(half + 1) * 512], in0=tg, scalar=1.0, in1=banks[pb][:],
                                                                            op0=ALU.add, op1=ALU.mult),
                 reads=[b_tg, bank_b[pb]], writes=[b_gs[gi]])

    def core(tb):
        p, t, bi = (tb // 4) % 2, tb % 4, tb % 2
        tsl = slice(t * 128, (t + 1) * 128)
        obs = []
        for hl in range(2):
            ob = 4 + ob_i[0] % 2
            ob_i[0] += 1
            obs.append(ob)
        vsl = [slice(hl * 512, (hl + 1) * 512) for hl in range(2)]

        def sc(hl):
            sreg = banks[3][:, hl * 128:(hl + 1) * 128]
            for dk in range(2):
                ch = 2 * hl + dk
                S.op("pe", lambda e, ch=ch, dk=dk: e.matmul(sreg, lhsT=kT[p][:, ch, tsl], rhs=qT[p][:, ch, tsl], start=(dk == 0), stop=(dk == 1)),
                     reads=[b_kT[p], b_qT[p]], writes=[bank_b[3]])
            S.op("dve", lambda e: e.tensor_tensor(out=MT[hl], in0=sreg, in1=dmt[:, hl, :], op=ALU.mult), reads=[bank_b[3], cx.B("dmt")], writes=[b_MT[hl]])

        dsb = {}

        def dS(hl):
            for dk in range(2):
                bk = next_pbank()
                dsb[(hl, dk)] = bk
                S.op("pe", lambda e, dk=dk, bk=bk: e.matmul(banks[bk][:], lhsT=kd[bi][:, hl * 256 + dk * 128:hl * 256 + (dk + 1) * 128], rhs=vt[bi][:, vsl[hl]],
                                                             start=True, stop=True),
                     reads=[b_kd[bi], b_vt[bi]], writes=[bank_b[bk]])

        def upd(hl):
            for dk in range(2):
                bk = dsb[(hl, dk)]
                S.op("dve", lambda e, dk=dk, bk=bk: e.scalar_tensor_tensor(out=S32[:, hl, dk, :], in0=S32[:, hl, dk, :], scalar=kdg[:, 2 + hl:3 + hl],
                                                                            in1=banks[bk][:], op0=ALU.mult, op1=ALU.add),
                     reads=[b_S32[hl][dk], cx.B("kdg"), bank_b[bk]], writes=[b_S32[hl][dk]])
                S.op("act", lambda e, dk=dk: e.copy(out=Sbf[:, hl, dk, :], in_=S32[:, hl, dk, :]), reads=[b_S32[hl][dk]], writes=[b_Sbf[hl][dk]])

        def out(hl):
            ob = obs[hl]
            S.op("pe", lambda e: e.matmul(banks[ob][:], lhsT=MT[hl], rhs=vt[bi][:, vsl[hl]], start=True, stop=False),
                 reads=[b_MT[hl], b_vt[bi]], writes=[bank_b[ob]])
            for dk in range(2):
                ch = 2 * hl + dk
                S.op("pe", lambda e, dk=dk, ch=ch: e.matmul(banks[ob][:], lhsT=qdT[p][:, ch, tsl], rhs=Sbf[:, hl, dk, :], start=False, stop=(dk == 1)),
                     reads=[b_qdT[p], b_Sbf[hl][dk]], writes=[bank_b[ob]])

        def epi(hl):
            ob = obs[hl]
            epi_t = ep[bi][hl]
            S.op("act", lambda e: e.activation(out=sq, in_=banks[ob][:], func=AF.Square, accum_out=epi_t[:, 0:1]),
                 reads=[bank_b[ob]], writes=[b_sq, b_ep[bi][hl]])
            S.op("dve", lambda e: e.tensor_scalar(out=epi_t[:, 1:2], in0=epi_t[:, 0:1], scalar1=1.0 / 128, scalar2=4e-5, op0=ALU.mult, op1=ALU.add),
                 reads=[b_ep[bi][hl]], writes=[b_ep[bi][hl]])
            S.op("pool", lambda e: e.tensor_tensor(out=epi_t[:, 2:3], in0=epi_t[:, 1:2], in1=mhalf[:, 0:1], op=ALU.pow),
                 reads=[b_ep[bi][hl], cx.B("mhalf")], writes=[b_ep[bi][hl]])
            S.op("dve", lambda e: e.scalar_tensor_tensor(out=yb[bi][hl], in0=banks[ob][:], scalar=epi_t[:, 2:3], in1=gs[tb % 3][:, vsl[hl]],
                                                          op0=ALU.mult, op1=ALU.mult),
                 reads=[bank_b[ob], b_ep[bi][hl], b_gs[tb % 3]], writes=[b_yb[bi][hl]])

        sc(0)
        dS(0)
        sc(1)
        out(0)
        upd(0)
        epi(0)
        dS(1)
        out(1)
        upd(1)
        epi(1)

    def transposes(tb):
        bi, t, yi = tb % 2, tb % 4, (tb // 4) % 2
        tsl = slice(t * 128, (t + 1) * 128)
        for hl in range(2):
            pb = next_pbank()
            pbv = banks[pb][:].bitcast(BF16)
            for ec in range(4):
                S.op("pe", lambda e, ec=ec, hl=hl, pbv=pbv: e.transpose(out=pbv[:, ec * 128:(ec + 1) * 128], in_=yb[bi][hl][:, ec * 128:(ec + 1) * 128],
                                                                         identity=ident),
                     reads=[b_yb[bi][hl], cx.B("ident")], writes=[bank_b[pb]])
            S.op("act", lambda e, hl=hl, pbv=pbv: e.copy(out=ystage[yi][:, hl * 4:(hl + 1) * 4, tsl], in_=pbv[:, 0:512].rearrange("p (c n) -> p c n", n=128)),
                 reads=[bank_b[pb]], writes=[b_ys[yi]])
        if t == 3:
            sb = tb // 4
            S.dma("sp", lambda e: e.dma_start(
                out=io["yg1_src"][sb // 2][:, :].rearrange("(c p) t -> p c t", p=128)[:, :, (sb % 2) * 512:(sb % 2 + 1) * 512],
                in_=ystage[yi]), ysem[yi], reads=[b_ys[yi]], writes=[b_ywr[yi]])
            if sb % 2 == 1:
                g = sb // 2
                S.dma("pool", lambda e: e.collective_compute("AllGather", ALU.bypass, replica_groups=PAIRS,
                                                             ins=[io["yg1_src"][g].ap().opt()], outs=[io["yg1_all"][g].ap().opt()]),
                      io["ccsem"], reads=[b_ywr[0], b_ywr[1]], writes=[io["b_cc1"][g]], inc=1)

    for tb in range(min(NXB, ntb)):
        load_x(tb)
    for f in sb_chunks(0):
        f()
    if nsb > 1:
        for t in range(4):
            chunk_norm(4 + t)()
    proj_tok(0)

    def sched(sb, t):
        out = []
        n1, n2 = sb + 1, sb + 2
        if n1 < nsb:
            if t == 0:
                out += [chunk_tr(4 * n1 + i) for i in range(4)]
            elif t == 1:
                out += [chunk_k(n1, 0), chunk_k(n1, 1), chunk_q(n1, 0), chunk_q(n1, 1)]
            elif t == 2:
                out += [chunk_k(n1, 2), chunk_k(n1, 3), chunk_q(n1, 2), chunk_q(n1, 3)]
        if n2 < nsb:
            if t == 1:
                out += [chunk_norm(4 * n2), chunk_norm(4 * n2 + 1)]
            elif t == 2:
                out += [chunk_norm(4 * n2 + 2), chunk_norm(4 * n2 + 3)]
        return out

    for tb in range(ntb):
        sb, t = tb // 4, tb % 4
        for f in sched(sb, t):
            f()
        if tb + 1 < ntb:
            proj_tok(tb + 1)
        if tb >= 1:
            transposes(tb - 1)
        core(tb)
    transposes(ntb - 1)


def l1_consts(hh):
    dmt = np.zeros((128, 2, 128), np.float32)
    qd = np.zeros((128, 2, 512), np.float32)
    kdg = np.zeros((128, 4), np.float32)
    j = np.arange(128, dtype=np.float64)[:, None]
    i = np.arange(128, dtype=np.float64)[None, :]
    for hl in range(2):
        lg = math.log(ret_gamma(2 * hh + hl))
        same = (j // 64) == (i // 64)
        later = (i // 64) > (j // 64)
        m = np.where(same, np.exp(lg * np.abs(i - j)), np.where(later, np.exp(lg * (i - j)), 0.0))
        dmt[:, hl, :] = m / 16.0
        qd[:, hl, :] = np.tile(np.exp(lg * np.arange(128, dtype=np.float64)), 4)[None, :]
        kdg[:, hl] = np.exp(lg * (128.0 - np.arange(128, dtype=np.float64))) / 16.0
        kdg[:, 2 + hl] = math.exp(lg * 128.0)
    return dmt.reshape(128, 256), qd.reshape(128, 1024), kdg


def build_fused(nsb=NSB):
    nc = bass.Bass("TRN2", target_bir_lowering=False)
    with ExitStack() as st:
        cx = Ctx(nc, st)
        S = cx.S
        io = {}
        shapes = {
            "x": [S_LEN, D], "ccol": [128, 8],
            "adaw0": [D, 3072], "adab0": [128, 3072], "adaw1": [D, 3072], "adab1": [128, 3072],
            "pre0": [128, D], "post0": [128, D], "pre1": [128, D], "post1": [128, D],
            "wq0": [D, 512], "wk0": [D, 512], "wv0": [D, 512], "wg0": [D, 512],
            "lamv": [128, 256], "subg": [128, 512], "kbtab": [128, 4 * 68], "dg0": [128, 4 * 128], "cbt": [128, 16],
            "wout0": [1024, D],
            "wq1": [D, 512], "wk1": [D, 512], "wv1": [D, 1024], "wg1": [D, 1024],
            "dmt": [128, 256], "qd": [128, 1024], "kdg": [128, 4], "wout1": [2048, D],
        }
        for n, shp in shapes.items():
            io[n] = cx.din(n, shp)
        io["ident"] = cx.din("ident", [128, 128], BF16)
        xo_d = cx.dout("xo", [S_LEN, D])
        io["yg0_src"] = [cx.dscratch("yg0_src%d" % g, [512, 2048], BF16) for g in range(4)]
        io["yg0_all"] = [cx.dscratch("yg0_all%d" % g, [1024, 2048], BF16) for g in range(4)]
        io["yg1_src"] = [cx.dscratch("yg1_src%d" % g, [1024, 1024], BF16) for g in range(8)]
        io["yg1_all"] = [cx.dscratch("yg1_all%d" % g, [2048, 1024], BF16) for g in range(8)]
        io["x1"] = cx.dscratch("x1s", [S_LEN, D], F32)
        for nme in ("wout0", "wq1", "wk1", "wv1", "wg1", "wout1"):
            io[nme + "_bf"] = cx.dscratch(nme + "_bf", shapes[nme], BF16)
        for nme in ("adaA1", "adaB1", "adaG0", "adaG1"):
            io[nme] = cx.dscratch(nme, [128, D], F32)
        io["b_cc0"] = [Buf() for _ in range(4)]
        io["b_cc1"] = [Buf() for _ in range(8)]
        io["ccsem"] = S.new_dma_sem()
        S.barrier_skip.add(io["ccsem"])
        cx.alloc_banks()

        phase_l0(cx, io, nsb)
        cx.new_phase()
        io["op_x0"], io["op_xo0"] = io["x"], io["x1"]
        io["op_ysrc0"] = lambda g4: (io["yg0_all"][g4 // 4][:, :].rearrange("(ec p) t -> p ec t", p=128)[:, :, (g4 % 4) * 512:(g4 % 4 + 1) * 512],
                                     io["b_cc0"][g4 // 4])
        phase_outproj(cx, io, 0, 1024, ntok=512 * nsb)
        cx.new_phase()
        phase_l1(cx, io, nsb)
        cx.new_phase()
        io["op_x1"], io["op_xo1"] = io["x1"], xo_d
        io["op_ysrc1"] = lambda g4: (io["yg1_all"][g4 // 2][:, :].rearrange("(ec p) t -> p ec t", p=128)[:, :, (g4 % 2) * 512:(g4 % 2 + 1) * 512],
                                     io["b_cc1"][g4 // 2])
        toks = phase_outproj(cx, io, 1, 2048, ntok=512 * nsb)
        S.wait_all("sp", toks)
        counts = S.emit(st)
    return nc, counts


_CACHE = {}


def make_in_maps(inp):
    ident = np.eye(128, dtype=np.float32).astype(ml_dtypes.bfloat16)
    w_in0, w_in1 = inp["da_w_in"][0], inp["ret_w_in"][0]
    lamv = np.concatenate([rep128(inp["da_lambda_q1"][0]), rep128(inp["da_lambda_k1"][0]),
                           rep128(inp["da_lambda_q2"][0]), rep128(inp["da_lambda_k2"][0])], axis=1)
    subg = np.ascontiguousarray(np.tile(rep128(inp["da_subln_gain"][0]), (1, 4)))
    wout0_rows = np.concatenate([np.arange(h * 128, (h + 1) * 128) for h in HL[0] + HL[1]])
    shared = {
        "adaw0": np.ascontiguousarray(inp["ada_w"][0]), "adab0": rep128(inp["ada_b"][0]),
        "adaw1": np.ascontiguousarray(inp["ada_w"][1]), "adab1": rep128(inp["ada_b"][1]),
        "pre0": rep128(inp["pre_gain"][0]), "post0": rep128(inp["post_gain"][0]),
        "pre1": rep128(inp["pre_gain"][1]), "post1": rep128(inp["post_gain"][1]),
        "lamv": lamv, "subg": subg, "ident": ident,
        "wout0": np.ascontiguousarray(inp["da_w_out"][0][wout0_rows]),
        "wout1": np.ascontiguousarray(inp["ret_w_out"][0]),
    }
    in_maps = []
    for core in range(8):
        b, hh = core // 2, core % 2
        hc = head_cols(hh, 128)
        kb, dg, cb = l0_consts(hh)
        dmt, qd, kdg = l1_consts(hh)
        qc = np.arange(hh * 512, (hh + 1) * 512)
        vc = np.arange(hh * 1024, (hh + 1) * 1024)
        m = dict(shared)
        m.update({
            "x": np.ascontiguousarray(inp["x"][b], dtype=np.float32),
            "ccol": np.ascontiguousarray(inp["c"][b].reshape(8, 128).T),
            "wq0": np.ascontiguousarray(w_in0[:, 0 + hc]), "wk0": np.ascontiguousarray(w_in0[:, 1024 + hc]),
            "wv0": np.ascontiguousarray(w_in0[:, 2048 + hc]), "wg0": np.ascontiguousarray(w_in0[:, 3072 + hc]),
            "kbtab": kb, "dg0": dg, "cbt": cb,
            "wq1": np.ascontiguousarray(w_in1[:, qc]), "wk1": np.ascontiguousarray(w_in1[:, 1024 + qc]),
            "wv1": np.ascontiguousarray(w_in1[:, 2048 + vc]), "wg1": np.ascontiguousarray(w_in1[:, 4096 + vc]),
            "dmt": dmt, "qd": qd, "kdg": kdg,
        })
        in_maps.append(m)
    return in_maps


def kernel(**inp):
    inp = {k: np.asarray(v) for k, v in inp.items()}
    if "fused" not in _CACHE:
        _CACHE["fused"] = build_fused()
    nc, _ = _CACHE["fused"]
    res = run_bass_kernel_spmd(nc, make_in_maps(inp), core_ids=list(range(8)))
    out = np.empty((4, S_LEN, D), np.float32)
    for b in range(4):
        out[b, :4096] = res.results[2 * b]["xo"][:4096]
        out[b, 4096:] = res.results[2 * b + 1]["xo"][4096:]
    return out
```
